# Optimizing a Trainium2 kernel written in Bass

```python
import jax, jax.numpy as jnp
from jax import lax
import numpy as np

D_MODEL = 2048
BATCH = 4
SEQ = 2048
DEPTH = 1

D_MIX = D_MODEL
D_GMLP = D_MIX // 2
D_LRU = D_MIX - D_GMLP
CHUNK = 128
GMLP_HEADS = 8
GMLP_HEAD_DIM = D_GMLP // GMLP_HEADS
LRU_HEADS = 16
LRU_HEAD_DIM = D_LRU // LRU_HEADS
LRU_CONV = 4
LRU_C = 8.0
FFN_DIM = 3 * D_MODEL
FFN_CONV = 3
N_MOD = 6
EPS = 1e-6
D_IN = 2 * D_GMLP + 2 * D_LRU

kernel_name = "hybrid_gmlp_rglru_convffn_block"


def rms_norm(x, g):
    xf = x.astype(jnp.float32)
    y = xf * lax.rsqrt(jnp.mean(xf * xf, axis=-1, keepdims=True) + EPS)
    return (y * g.astype(jnp.float32)).astype(x.dtype)


def causal_dwconv(x, w, b):
    K = w.shape[0]
    S = x.shape[1]
    xp = jnp.pad(x, ((0, 0), (K - 1, 0), (0, 0)))
    y = b
    for k in range(K):
        y = y + w[k] * xp[:, k:k + S]
    return y


def gmlp_spatial_gating(u, v, ln_g, ln_b, w_s, b_s):
    B, S, _ = u.shape
    n = S // CHUNK
    vh = v.reshape(B, n, CHUNK, GMLP_HEADS, GMLP_HEAD_DIM)
    vf = vh.astype(jnp.float32)
    mu = jnp.mean(vf, axis=-1, keepdims=True)
    var = jnp.mean(jnp.square(vf - mu), axis=-1, keepdims=True)
    g = ln_g.reshape(GMLP_HEADS, GMLP_HEAD_DIM).astype(jnp.float32)
    bb = ln_b.reshape(GMLP_HEADS, GMLP_HEAD_DIM).astype(jnp.float32)
    vn = ((vf - mu) * lax.rsqrt(var + EPS) * g + bb).astype(v.dtype)
    causal = jnp.tril(jnp.ones((CHUNK, CHUNK), dtype=bool))
    ws = jnp.where(causal[None], w_s, jnp.zeros_like(w_s))
    mixed = jnp.einsum('hts,bnshd->bnthd', ws, vn) + b_s.T[None, None, :, :, None]
    uh = u.reshape(B, n, CHUNK, GMLP_HEADS, GMLP_HEAD_DIM)
    return (uh * mixed).reshape(B, S, D_GMLP)


def rg_lru(x, w_r, b_r, w_i, b_i, lam):
    B, S, _ = x.shape
    xh = x.reshape(B, S, LRU_HEADS, LRU_HEAD_DIM)
    r = jax.nn.sigmoid(jnp.einsum('bshi,hio->bsho', xh, w_r).reshape(B, S, D_LRU) + b_r)
    i = jax.nn.sigmoid(jnp.einsum('bshi,hio->bsho', xh, w_i).reshape(B, S, D_LRU) + b_i)
    log_a = (-LRU_C * r.astype(jnp.float32)) * jax.nn.softplus(-lam.astype(jnp.float32))
    a = jnp.exp(log_a)
    mult = jnp.sqrt(-jnp.expm1(2.0 * log_a))
    bterm = mult * (i.astype(jnp.float32) * x.astype(jnp.float32))

    def combine(left, right):
        a1, b1 = left
        a2, b2 = right
        return a1 * a2, a2 * b1 + b2

    _, h = lax.associative_scan(combine, (a, bterm), axis=1)
    return h.astype(x.dtype)


def setup_inputs(seed: int = 0) -> dict:
    key = jax.random.key(seed)
    ks = jax.random.split(key, 26)
    L = DEPTH

    def nrm(k, shape, s):
        return jax.random.normal(k, shape, jnp.float32) * s

    u = jax.random.uniform(ks[16], (L, D_LRU), jnp.float32, minval=0.9, maxval=0.999)
    s = u ** (1.0 / LRU_C)
    lru_lambda = jnp.log(s) - jnp.log1p(-s)
    return {
        "x": nrm(ks[0], (BATCH, SEQ, D_MODEL), 1.0),
        "c": nrm(ks[1], (BATCH, D_MODEL), 1.0),
        "w_ada": nrm(ks[2], (L, D_MODEL, N_MOD * D_MODEL), 0.5 * D_MODEL ** -0.5),
        "b_ada": nrm(ks[3], (L, N_MOD * D_MODEL), 0.02),
        "norm1": 1.0 + nrm(ks[4], (L, D_MODEL), 0.02),
        "w_in": nrm(ks[5], (L, D_MODEL, D_IN), D_MODEL ** -0.5),
        "gmlp_ln_g": 1.0 + nrm(ks[6], (L, D_GMLP), 0.02),
        "gmlp_ln_b": nrm(ks[7], (L, D_GMLP), 0.02),
        "gmlp_w_s": nrm(ks[8], (L, GMLP_HEADS, CHUNK, CHUNK), 0.5 * CHUNK ** -0.5),
        "gmlp_b_s": 1.0 + nrm(ks[9], (L, GMLP_HEADS, CHUNK), 0.1),
        "lru_conv_w": nrm(ks[10], (L, LRU_CONV, D_LRU), LRU_CONV ** -0.5),
        "lru_conv_b": nrm(ks[11], (L, D_LRU), 0.02),
        "lru_w_r": nrm(ks[12], (L, LRU_HEADS, LRU_HEAD_DIM, LRU_HEAD_DIM), LRU_HEAD_DIM ** -0.5),
        "lru_b_r": nrm(ks[13], (L, D_LRU), 0.02),
        "lru_w_i": nrm(ks[14], (L, LRU_HEADS, LRU_HEAD_DIM, LRU_HEAD_DIM), LRU_HEAD_DIM ** -0.5),
        "lru_b_i": nrm(ks[15], (L, D_LRU), 0.02),
        "lru_lambda": lru_lambda,
        "out_norm_gmlp": 1.0 + nrm(ks[17], (L, D_GMLP), 0.02),
        "out_norm_lru": 1.0 + nrm(ks[18], (L, D_LRU), 0.02),
        "w_out": nrm(ks[19], (L, D_MIX, D_MODEL), D_MIX ** -0.5),
        "norm2": 1.0 + nrm(ks[20], (L, D_MODEL), 0.02),
        "w_up": nrm(ks[21], (L, D_MODEL, 2 * FFN_DIM), D_MODEL ** -0.5),
        "ffn_conv_w": nrm(ks[22], (L, FFN_CONV, FFN_DIM), FFN_CONV ** -0.5),
        "ffn_conv_b": nrm(ks[23], (L, FFN_DIM), 0.02),
        "w_down": nrm(ks[24], (L, FFN_DIM, D_MODEL), FFN_DIM ** -0.5),
        "norm_final": 1.0 + nrm(ks[25], (D_MODEL,), 0.02),
    }


def reference(x, c, w_ada, b_ada, norm1, w_in, gmlp_ln_g, gmlp_ln_b, gmlp_w_s, gmlp_b_s,
              lru_conv_w, lru_conv_b, lru_w_r, lru_b_r, lru_w_i, lru_b_i, lru_lambda,
              out_norm_gmlp, out_norm_lru, w_out, norm2, w_up, ffn_conv_w, ffn_conv_b,
              w_down, norm_final):
    c_act = jax.nn.silu(c)
    for l in range(DEPTH):
        mod = jnp.einsum('bd,de->be', c_act, w_ada[l]) + b_ada[l]
        sh1, sc1, g1, sh2, sc2, g2 = [m[:, None, :] for m in jnp.split(mod, N_MOD, axis=-1)]

        h = rms_norm(x, norm1[l]) * (1 + sc1) + sh1
        z = jnp.einsum('bsd,de->bse', h, w_in[l])
        u, v, xb, gb = jnp.split(z, [D_GMLP, 2 * D_GMLP, 2 * D_GMLP + D_LRU], axis=-1)
        ya = gmlp_spatial_gating(jax.nn.gelu(u), jax.nn.gelu(v), gmlp_ln_g[l], gmlp_ln_b[l],
                                 gmlp_w_s[l], gmlp_b_s[l])
        xb = causal_dwconv(xb, lru_conv_w[l], lru_conv_b[l])
        yb = rg_lru(xb, lru_w_r[l], lru_b_r[l], lru_w_i[l], lru_b_i[l], lru_lambda[l]) * jax.nn.gelu(gb)
        mix = jnp.concatenate([rms_norm(ya, out_norm_gmlp[l]), rms_norm(yb, out_norm_lru[l])], axis=-1)
        x = x + g1 * jnp.einsum('bse,ed->bsd', mix, w_out[l])

        h = rms_norm(x, norm2[l]) * (1 + sc2) + sh2
        gate, val = jnp.split(jnp.einsum('bsd,df->bsf', h, w_up[l]), 2, axis=-1)
        gate = causal_dwconv(gate, ffn_conv_w[l], ffn_conv_b[l])
        x = x + g2 * jnp.einsum('bsf,fd->bsd', jax.nn.gelu(gate) * val, w_down[l])
    return rms_norm(x, norm_final)
```

```python
import contextlib
import numpy as np
import concourse.bass as bass
import concourse.mybir as mybir
from concourse.bass_utils import run_bass_kernel_spmd

F32 = mybir.dt.float32
BF16 = mybir.dt.bfloat16
AF = mybir.ActivationFunctionType
ALU = mybir.AluOpType

D = 2048
T = 1024
HAL = 4
TW = T + HAL
EPS = 1e-6
NPRM = 448
O_BADA, O_N1, O_N2, O_NF, O_LNG, O_LNB, O_CW, O_CB, O_BR, O_BI, O_LAM, O_GL, O_GA, O_FCW, O_FCB, O_SEL, O_FLAG, O_SELB = (
    0, 96, 112, 128, 144, 152, 160, 192, 200, 208, 216, 224, 232, 240, 384, 432, 440, 441)


class DSem:
    def __init__(self, name, step=16):
        self.name = name
        self.sem = None
        self.cnt = 0
        self.step = step


class Eng:
    def __init__(self, name, self_sync=True):
        self.name = name
        self.sem = None
        self.cnt = 0
        self.prog = []
        self.waited = {}
        self.self_sync = self_sync

    def _wait(self, tok):
        src, val = tok
        if src is self and not self.self_sync:
            return
        if self.waited.get(id(src), 0) >= val:
            return
        self.waited[id(src)] = val
        self.prog.append(("w", src, val))


class Sched:
    def __init__(self):
        self.engs = {}
        self.dsems = []
        self.last_w = {}
        self.readers = {}

    def eng(self, name, self_sync=True):
        e = Eng(name, self_sync)
        self.engs[name] = e
        return e

    def dsem(self, name, step=16):
        d = DSem(name, step)
        self.dsems.append(d)
        return d

    def _deps(self, e, reads, writes, extra):
        for k in reads:
            t = self.last_w.get(k)
            if t is not None:
                e._wait(t)
        for k in writes:
            t = self.last_w.get(k)
            if t is not None:
                e._wait(t)
            for t in self.readers.get(k, ()):
                e._wait(t)
        for t in extra:
            if t is not None:
                e._wait(t)

    def _commit(self, tok, reads, writes):
        for k in reads:
            self.readers.setdefault(k, []).append(tok)
        for k in writes:
            self.last_w[k] = tok
            self.readers[k] = []

    def op(self, e, fn, reads=(), writes=(), extra=(), inc=True):
        self._deps(e, reads, writes, extra)
        if inc:
            e.cnt += 1
            tok = (e, e.cnt)
            e.prog.append(("o", fn, True))
        else:
            tok = (e, e.cnt + 1)
            e.prog.append(("o", fn, False))
        self._commit(tok, reads, writes)
        return tok

    def dma(self, e, fn, ds, reads=(), writes=(), extra=()):
        self._deps(e, reads, writes, extra)
        ds.cnt += ds.step
        tok = (ds, ds.cnt)
        e.prog.append(("d", fn, ds))
        self._commit(tok, reads, writes)
        return tok

    def wait(self, e, tok):
        e._wait(tok)

    def fix_pending(self):
        pass

    def alloc(self, nc):
        stack = contextlib.ExitStack()
        for e in self.engs.values():
            e.sem = stack.enter_context(nc.semaphore("s_" + e.name))
        for d in self.dsems:
            d.sem = stack.enter_context(nc.semaphore("d_" + d.name))
        return stack

    def emit(self, block):
        def runner(e):
            def run(engine):
                for item in e.prog:
                    if item[0] == "w":
                        engine.wait_ge(item[1].sem, item[2])
                    elif item[0] == "o":
                        ins = item[1](engine)
                        if item[2]:
                            ins.then_inc(e.sem, 1)
                    else:
                        ins = item[1](engine)
                        ins.then_inc(item[2].sem, item[2].step)
            return run

        for name, e in self.engs.items():
            getattr(block, name)(runner(e))


def build_program():
    nc = bass.Bass("TRN2", target_bir_lowering=False)

    def din(name, shape):
        return nc.dram_tensor(name, list(shape), F32, kind="ExternalInput").ap()

    xT = din("xT", [D, TW])
    cT = din("cT", [128, 64])
    w_ada = din("w_ada_r", [D, 1536])
    prm = din("prm", [128, NPRM])
    wsT_d = din("wsT", [128, 1024])
    tri_d = din("tri", [128, 128])
    bs_d = din("bs_bc", [128, 1024])
    wr_d = din("wr_bd", [128, 1024])
    wi_d = din("wi_bd", [128, 1024])
    w_in = din("w_in", [D, 4096])
    w_out = din("w_out", [D, D])
    w_up = din("w_up", [D, 12288])
    w_down = din("w_down", [6144, D])
    yT = nc.dram_tensor("yT", [D, T], F32, kind="ExternalOutput").ap()
    ag0_in = nc.dram_tensor("ag0_in", [128, 48], F32).ap()
    ag0_out = nc.dram_tensor("ag0_out", [1024, 48], F32).ap()
    ag1_in = nc.dram_tensor("ag1_in", [128, 8], F32).ap()
    ag1_out = nc.dram_tensor("ag1_out", [1024, 8], F32).ap()
    ag2_in = nc.dram_tensor("ag2_in", [128, 32], F32).ap()
    ag2_out = nc.dram_tensor("ag2_out", [1024, 32], F32).ap()

    S = Sched()
    PE = S.eng("tensor", self_sync=False)
    ACT = S.eng("scalar")
    DVE = S.eng("vector")
    POOL = S.eng("gpsimd")
    SP = S.eng("sync")
    dx = S.dsem("x")
    dp = S.dsem("prm")
    dwg = S.dsem("wg")
    dws = [S.dsem("w%d" % i) for i in range(3)]
    dbo = S.dsem("bo")
    dbi = S.dsem("bi")
    dxr = S.dsem("xr")
    dout = S.dsem("out")
    dcc = [S.dsem("cc%d" % i, step=1) for i in range(3)]

    es = contextlib.ExitStack()

    def sb(name, shape, dt=F32):
        return es.enter_context(nc.sbuf_tensor(name, list(shape), dt))

    def pst(name):
        return es.enter_context(nc.psum_tensor(name, [128, 1024], F32))

    XA = sb("XA", [128, 16, TW])
    HT = sb("HT", [128, 16, TW], BF16)
    MIX = sb("MIX", [128, 16, T], BF16)
    W = sb("W", [128, 3, 16, 512], BF16)
    SQ = [sb("SQ%d" % i, [128, T], BF16) for i in range(2)]
    TA = [sb("TA%d" % i, [128, TW]) for i in range(2)]
    TT01 = [sb("T%d" % i, [128, T]) for i in range(2)]
    TT = TT01 + [MIX[:, 8:10, :].rearrange("p a b -> p (a b)").bitcast(F32),
                 MIX[:, 10:12, :].rearrange("p a b -> p (a b)").bitcast(F32)]
    TK = [["T0"], ["T1"], ["MX8", "MX9"], ["MX10", "MX11"]]
    R = MIX[:, 12:14, :].rearrange("p a b -> p (a b)").bitcast(F32)
    RK = ["MX12", "MX13"]
    WRB = MIX[:, 14, :].rearrange("p (j q) -> p j q", q=128)
    WIB = MIX[:, 15, :].rearrange("p (j q) -> p j q", q=128)
    RSTDM = TT01[1]
    WSB = sb("WSB", [128, 8, 128], BF16)
    PRM = sb("PRM", [128, NPRM])
    CT = sb("CT", [128, 64])
    CB = sb("CB", [128, 64], BF16)
    ONES = sb("ONES", [128, 128], BF16)
    MODP = sb("MODP", [128, 48])
    MOD = sb("MOD", [128, 96])
    A1 = sb("A1", [128, 16])
    A2 = sb("A2", [128, 16])
    LP = sb("LP", [128, 64])
    CST = sb("CST", [128, 4])
    SQH = sb("SQH", [128, 16, 4], BF16)
    STATS = sb("STATS", [128, 16, 6])
    MV = sb("MV", [128, 16, 2])
    RS4 = sb("RS4", [128, 16])
    VH = [sb("VH0", [128, 512], BF16)] * 2
    HEND = sb("HEND", [128, 8])
    AGT1 = sb("AGT1", [128, 8, 8])
    INIT = sb("INIT", [128, 8])
    X1H = sb("X1H", [128, 32])
    XH = sb("XH", [128, 32])
    SQH2 = sb("SQH2", [128, 32], BF16)
    RSTDH = sb("RSTDH", [128, 4])
    TMPH = sb("TMPH", [128, 32])
    PSA = pst("PSA")
    PSB = pst("PSB")
    PSC = pst("PSC")
    PSS = pst("PSS")
    TRI = TT01[0][:, 0:128]
    MODG = TT01[1][:, 0:384].rearrange("p (a b) -> p a b", b=4)
    AGT2 = TT01[0][:, 0:256].rearrange("p (a b) -> p a b", b=32)
    pairs = [(PSA, ["PA0", "PA1"]), (PSB, ["PB0", "PB1"]), (PSC, ["PC0", "PC1"])]
    pair_i = [0]

    def next_pair():
        p = pairs[pair_i[0] % 3]
        pair_i[0] += 1
        return p

    def P(o, n=1):
        return PRM[:, o:o + n]

    xa_keys = ["XA%d" % k for k in range(16)]
    ht_keys = ["HT%d" % k for k in range(16)]
    mx_keys = ["MX%d" % k for k in range(16)]

    wq = []
    for g in range(3):
        wq.append(w_ada[:, g * 512:(g + 1) * 512])
    IN_ORDER = [4, 5, 2, 0, 3, 1, 6, 7]
    for g in IN_ORDER:
        wq.append(w_in[:, g * 512:(g + 1) * 512])
    for g in range(4):
        wq.append(w_out[:, g * 512:(g + 1) * 512])
    for H in range(3):
        for q in range(4):
            j0 = H * 16 + q * 4
            wq.append(w_up[:, j0 * 128:j0 * 128 + 512])
            wq.append(w_up[:, 6144 + j0 * 128:6144 + j0 * 128 + 512])
        for gq in range(4):
            wq.append(w_down[H * 2048:(H + 1) * 2048, gq * 512:(gq + 1) * 512])
    w_issued = [0]
    w_used = [0]

    def w_issue_upto(n):
        while w_issued[0] < min(n, len(wq)):
            i = w_issued[0]
            s = i % 3
            src = wq[i].rearrange("(k p) e -> p k e", p=128)
            S.dma(POOL, lambda e, s=s, src=src: e.dma_start(out=W[:, s], in_=src), dws[s], writes=["W%d" % s])
            w_issued[0] += 1

    def w_next(ahead=3):
        i = w_used[0]
        w_used[0] += 1
        w_issue_upto(i + ahead)
        return i % 3

    for q in range(4):
        S.dma(SP, lambda e, q=q: e.dma_start(out=XA[:, 4 * q:4 * q + 4, :],
                                             in_=xT[512 * q:512 * (q + 1), :].rearrange("(k p) t -> p k t", p=128)),
              dx, writes=xa_keys[4 * q:4 * q + 4])
    S.dma(SP, lambda e: e.dma_start(out=PRM[:], in_=prm), dp, writes=["PRM"])
    S.dma(SP, lambda e: e.dma_start(out=CT[:], in_=cT), dp, writes=["CT"])
    S.dma(SP, lambda e: e.dma_start(out=TRI[:], in_=tri_d), dp, writes=["T0"])
    S.dma(SP, lambda e: e.dma_start(out=TA[1][:, 0:1024], in_=wsT_d), dp, writes=["TA1"])
    S.dma(SP, lambda e: e.dma_start(out=TA[0][:, 0:1024], in_=bs_d), dp, writes=["TA0"])
    tok_p = (dp, dp.cnt)
    for k in ["PRM", "CT", "T0", "TA1", "TA0"]:
        S.last_w[k] = tok_p
    tok_x = (dx, dx.cnt)
    for k in xa_keys:
        S.last_w[k] = tok_x
    w_issue_upto(3)
    S.dma(POOL, lambda e: e.dma_start(out=MIX[:, 14, :], in_=wr_d), dwg, writes=["MX14"])
    S.dma(POOL, lambda e: e.dma_start(out=MIX[:, 15, :], in_=wi_d), dwg, writes=["MX15"])
    tok_wg = (dwg, dwg.cnt)
    S.last_w["MX14"] = tok_wg
    S.last_w["MX15"] = tok_wg

    S.op(DVE, lambda e: e.memset(ONES[:], 1.0), writes=["ONES"])
    S.op(DVE, lambda e: e.memset(CST[:, 0:1], EPS), writes=["CST"])
    S.op(DVE, lambda e: e.memset(CST[:, 1:2], 1.0), writes=["CST"])
    S.op(DVE, lambda e: e.memset(CST[:, 2:3], 0.25), writes=["CST"])
    S.op(DVE, lambda e: e.memset(CST[:, 3:4], 0.0), writes=["CST"])
    EPSC = CST[:, 0:1]
    ONEC = CST[:, 1:2]
    QRTC = CST[:, 2:3]
    ZEROC = CST[:, 3:4]

    S.op(ACT, lambda e: e.activation(out=CB[:], in_=CT[:], func=AF.Silu, bias=ZEROC, scale=1.0), reads=["CT", "CST"], writes=["CB"])
    for g in range(3):
        s = w_next()
        for cc in range(4):
            c0 = (g * 4 + cc) * 4
            for k in range(16):
                S.op(PE, lambda e, s=s, cc=cc, k=k, c0=c0: e.matmul(
                    PSA[:, c0:c0 + 4], lhsT=W[:, s, k, cc * 128:(cc + 1) * 128], rhs=CB[:, k * 4:(k + 1) * 4],
                    start=(k == 0), stop=(k == 15)),
                    reads=["W%d" % s, "CB"], writes=["PA0"], inc=(k == 15))
    S.op(DVE, lambda e: e.tensor_copy(out=MODP[:], in_=PSA[:, 0:48]), reads=["PA0"], writes=["MODP"])
    S.dma(SP, lambda e: e.dma_start(out=ag0_in, in_=MODP[:]), dbo, reads=["MODP"], writes=["ag0_in"])
    S.dma(POOL, lambda e: e.collective_compute("AllGather", ALU.bypass, replica_groups=[list(range(8))],
                                               ins=[ag0_in], outs=[ag0_out]), dcc[0], reads=["ag0_in"], writes=["ag0_out"])
    S.dma(SP, lambda e: e.dma_start(out=MODG[:].rearrange("p (r c) b -> p r (c b)", r=8),
                                    in_=ag0_out.rearrange("(r p) c -> p r c", p=128)),
          dbi, reads=["ag0_out"], writes=["T1"])
    S.op(DVE, lambda e: e.tensor_scalar(out=MOD[:], in0=MODG[:, :, 0], scalar1=P(O_SELB), scalar2=None, op0=ALU.mult),
         reads=["T1", "PRM"], writes=["MOD"])
    for b in range(1, 4):
        S.op(DVE, lambda e, b=b: e.scalar_tensor_tensor(out=MOD[:], in0=MODG[:, :, b], scalar=P(O_SELB + b), in1=MOD[:],
                                                        op0=ALU.mult, op1=ALU.add), reads=["T1", "MOD", "PRM"], writes=["MOD"])
    S.op(DVE, lambda e: e.tensor_tensor(out=MOD[:], in0=MOD[:], in1=P(O_BADA, 96), op=ALU.add), reads=["MOD", "PRM"], writes=["MOD"])
    S.op(DVE, lambda e: e.scalar_tensor_tensor(out=A1[:], in0=MOD[:, 16:32], scalar=1.0, in1=P(O_N1, 16), op0=ALU.add, op1=ALU.mult),
         reads=["MOD", "PRM"], writes=["A1"])
    S.op(DVE, lambda e: e.scalar_tensor_tensor(out=A2[:], in0=MOD[:, 64:80], scalar=1.0, in1=P(O_N2, 16), op0=ALU.add, op1=ALU.mult),
         reads=["MOD", "PRM"], writes=["A2"])
    S.op(ACT, lambda e: e.activation(out=LP[:, 32:40], in_=P(O_LAM, 8), func=AF.Abs, bias=ZEROC, scale=1.0), reads=["PRM", "CST"], writes=["LP"])
    S.op(ACT, lambda e: e.activation(out=LP[:, 32:40], in_=LP[:, 32:40], func=AF.Exp, bias=ZEROC, scale=-1.0), reads=["LP", "CST"], writes=["LP"])
    S.op(ACT, lambda e: e.activation(out=LP[:, 32:40], in_=LP[:, 32:40], func=AF.Ln, bias=ONEC, scale=1.0), reads=["LP", "CST"], writes=["LP"])
    S.op(DVE, lambda e: e.tensor_scalar(out=LP[:, 40:48], in0=P(O_LAM, 8), scalar1=-1.0, scalar2=0.0, op0=ALU.mult, op1=ALU.max),
         reads=["PRM", "LP"], writes=["LP"])
    S.op(DVE, lambda e: e.tensor_tensor(out=LP[:, 40:48], in0=LP[:, 40:48], in1=LP[:, 32:40], op=ALU.add), reads=["LP"], writes=["LP"])
    S.op(DVE, lambda e: e.tensor_scalar(out=LP[:, 0:8], in0=LP[:, 40:48], scalar1=-4.0, scalar2=None, op0=ALU.mult), reads=["LP"], writes=["LP"])
    S.op(DVE, lambda e: e.tensor_scalar(out=LP[:, 8:16], in0=LP[:, 40:48], scalar1=-8.0, scalar2=None, op0=ALU.mult), reads=["LP"], writes=["LP"])
    S.op(DVE, lambda e: e.tensor_scalar(out=LP[:, 16:24], in0=P(O_BR, 8), scalar1=0.5, scalar2=None, op0=ALU.mult), reads=["PRM", "LP"], writes=["LP"])
    S.op(DVE, lambda e: e.tensor_scalar(out=LP[:, 24:32], in0=P(O_BI, 8), scalar1=0.5, scalar2=None, op0=ALU.mult), reads=["PRM", "LP"], writes=["LP"])
    for h in range(8):
        S.op(DVE, lambda e, h=h: e.tensor_tensor(out=WSB[:, h, :], in0=TA[1][:, h * 128:(h + 1) * 128], in1=TRI, op=ALU.mult),
             reads=["TA1", "T0"], writes=["WSB"])
    for h in range(8):
        S.op(PE, lambda e, h=h: e.matmul(PSB[:, h * 128:(h + 1) * 128], lhsT=ONES[:], rhs=WSB[:, h, :], start=True, stop=True),
             reads=["ONES", "WSB"], writes=["PB0", "PB1"], inc=(h == 7))
    for h in range(8):
        S.op(DVE, lambda e, h=h: e.scalar_tensor_tensor(out=R[:, h * 128:(h + 1) * 128], in0=PSB[:, h * 128:(h + 1) * 128],
                                                        scalar=P(O_LNB + h), in1=TA[0][:, h * 128:(h + 1) * 128],
                                                        op0=ALU.mult, op1=ALU.add),
             reads=["PB0", "PB1", "PRM", "TA0"], writes=RK)

    def rms_main(sum_div):
        for k in range(16):
            sq = SQ[k % 2]
            sk = "SQ%d" % (k % 2)
            S.op(ACT, lambda e, k=k, sq=sq: e.activation(out=sq[:, 0:T], in_=XA[:, k, HAL:TW], func=AF.Square, bias=ZEROC, scale=1.0),
                 reads=[xa_keys[k], "CST"], writes=[sk])
            for tb in range(2):
                S.op(PE, lambda e, k=k, sq=sq, tb=tb: e.matmul(PSS[:, tb * 512:(tb + 1) * 512], lhsT=ONES[:], rhs=sq[:, tb * 512:(tb + 1) * 512],
                                                               start=(k == 0), stop=(k == 15)),
                     reads=["ONES", sk], writes=["S%d" % tb], inc=(tb == 1))
        S.op(ACT, lambda e: e.activation(out=RSTDM[:], in_=PSS[:, 0:T], func=AF.Sqrt, bias=EPSC, scale=1.0 / sum_div),
             reads=["S0", "S1", "CST"], writes=["T1"])
        S.op(DVE, lambda e: e.reciprocal(out=RSTDM[:], in_=RSTDM[:]), reads=["T1"], writes=["T1"])

    S.op(ACT, lambda e: e.activation(out=SQH[:], in_=XA[:, :, 0:HAL], func=AF.Square, bias=ZEROC, scale=1.0), reads=xa_keys + ["CST"], writes=["SQH"])
    for k in range(16):
        S.op(PE, lambda e, k=k: e.matmul(PSC[:, 0:HAL], lhsT=ONES[:], rhs=SQH[:, k, :], start=(k == 0), stop=(k == 15)),
             reads=["ONES", "SQH"], writes=["PC0"], inc=(k == 15))
    S.op(ACT, lambda e: e.activation(out=RSTDH[:, 0:HAL], in_=PSC[:, 0:HAL], func=AF.Sqrt, bias=EPSC, scale=1.0 / D), reads=["PC0", "CST"], writes=["RSTDH"])
    S.op(DVE, lambda e: e.reciprocal(out=RSTDH[:, 0:HAL], in_=RSTDH[:, 0:HAL]), reads=["RSTDH"], writes=["RSTDH"])
    rms_main(D)
    for k in range(16):
        ta = TA[k % 2]
        tk = "TA%d" % (k % 2)
        S.op(DVE, lambda e, k=k, ta=ta: e.scalar_tensor_tensor(out=ta[:, HAL:TW], in0=XA[:, k, HAL:TW], scalar=A1[:, k:k + 1], in1=RSTDM[:],
                                                               op0=ALU.mult, op1=ALU.mult),
             reads=[xa_keys[k], "A1", "T1"], writes=[tk])
        S.op(DVE, lambda e, k=k, ta=ta: e.scalar_tensor_tensor(out=ta[:, 0:HAL], in0=XA[:, k, 0:HAL], scalar=A1[:, k:k + 1], in1=RSTDH[:, 0:HAL],
                                                               op0=ALU.mult, op1=ALU.mult),
             reads=[xa_keys[k], "A1", "RSTDH"], writes=[tk])
        S.op(ACT, lambda e, k=k, ta=ta: e.activation(out=HT[:, k, :], in_=ta[:], func=AF.Identity, bias=MOD[:, k:k + 1], scale=1.0),
             reads=[tk, "MOD"], writes=[ht_keys[k], "HTh"])
    S.op(DVE, lambda e: e.tensor_scalar(out=HT[:, :, 0:HAL], in0=HT[:, :, 0:HAL], scalar1=P(O_FLAG), scalar2=None, op0=ALU.mult),
         reads=["HTh", "PRM"], writes=["HTh"])

    def inproj_fm(s, cc, halo_col=None):
        ps, pk = next_pair()
        for k in range(16):
            for tb in range(2):
                S.op(PE, lambda e, s=s, cc=cc, k=k, tb=tb, ps=ps: e.matmul(
                    ps[:, tb * 512:(tb + 1) * 512], lhsT=W[:, s, k, cc * 128:(cc + 1) * 128],
                    rhs=HT[:, k, HAL + tb * 512:HAL + (tb + 1) * 512], start=(k == 0), stop=(k == 15)),
                    reads=["W%d" % s, ht_keys[k]], writes=[pk[tb]], inc=(k == 15 and tb == 1 and halo_col is None))
            if halo_col is not None:
                S.op(PE, lambda e, s=s, cc=cc, k=k: e.matmul(
                    PSS[:, halo_col:halo_col + HAL], lhsT=W[:, s, k, cc * 128:(cc + 1) * 128], rhs=HT[:, k, 0:HAL],
                    start=(k == 0), stop=(k == 15)),
                    reads=["W%d" % s, "HTh"], writes=["S0"], inc=(k == 15))
        return ps, pk

    def lru_chunk(s, cc, j):
        ps, pk = inproj_fm(s, cc, halo_col=j * HAL)
        xbs = TA[0]
        S.op(ACT, lambda e: e.activation(out=xbs[:, HAL:TW], in_=ps[:, 0:T], func=AF.Identity, bias=ZEROC, scale=1.0), reads=pk + ["CST"], writes=["TA0"])
        S.op(ACT, lambda e: e.activation(out=xbs[:, 0:HAL], in_=PSS[:, j * HAL:(j + 1) * HAL], func=AF.Identity, bias=ZEROC, scale=1.0),
             reads=["S0", "CST"], writes=["TA0"])
        acc = TT[0]
        S.op(ACT, lambda e: e.activation(out=acc[:], in_=xbs[:, HAL:TW], func=AF.Identity, bias=P(O_CB + j), scale=P(O_CW + j * 4 + 3)),
             reads=["TA0", "PRM"], writes=["T0"])
        for sft in (1, 2, 3):
            S.op(DVE, lambda e, sft=sft: e.scalar_tensor_tensor(out=acc[:], in0=xbs[:, HAL - sft:TW - sft], scalar=P(O_CW + j * 4 + 3 - sft),
                                                                in1=acc[:], op0=ALU.mult, op1=ALU.add),
                 reads=["TA0", "T0", "PRM"], writes=["T0"])
        xcb = SQ[0]
        S.op(ACT, lambda e: e.activation(out=xcb[:, 0:T], in_=acc[:], func=AF.Identity, bias=ZEROC, scale=1.0), reads=["T0", "CST"], writes=["SQ0"])
        pr, pkr = next_pair()
        pi, pki = next_pair()
        for (pp, kk, wb, wk) in ((pr, pkr, WRB, "MX14"), (pi, pki, WIB, "MX15")):
            for tb in range(2):
                S.op(PE, lambda e, pp=pp, wb=wb, tb=tb: e.matmul(pp[:, tb * 512:(tb + 1) * 512], lhsT=wb[:, j, :],
                                                                  rhs=xcb[:, tb * 512:(tb + 1) * 512], start=True, stop=True),
                     reads=[wk, "SQ0"], writes=[kk[tb]], inc=(tb == 1))
        thr = TT[1]
        thi = TT[2]
        S.op(ACT, lambda e: e.activation(out=thr[:], in_=pr[:, 0:T], func=AF.Tanh, bias=LP[:, 16 + j:17 + j], scale=0.5), reads=pkr + ["LP"], writes=["T1"])
        S.op(ACT, lambda e: e.activation(out=thi[:], in_=pi[:, 0:T], func=AF.Tanh, bias=LP[:, 24 + j:25 + j], scale=0.5), reads=pki + ["LP"], writes=TK[2])
        a_ap = XA[:, j, HAL:TW]
        b_ap = XA[:, 8 + j, HAL:TW]
        S.op(ACT, lambda e: e.activation(out=a_ap, in_=thr[:], func=AF.Exp, bias=LP[:, j:j + 1], scale=LP[:, j:j + 1]), reads=["T1", "LP"], writes=[xa_keys[j]])
        a2 = TT[3]
        S.op(ACT, lambda e: e.activation(out=a2[:], in_=thr[:], func=AF.Exp, bias=LP[:, 8 + j:9 + j], scale=LP[:, 8 + j:9 + j]), reads=["T1", "LP"], writes=TK[3])
        S.op(ACT, lambda e: e.activation(out=a2[:], in_=a2[:], func=AF.Sqrt, bias=QRTC, scale=-0.25), reads=TK[3] + ["CST"], writes=TK[3])
        S.op(DVE, lambda e: e.scalar_tensor_tensor(out=thi[:], in0=thi[:], scalar=1.0, in1=acc[:], op0=ALU.add, op1=ALU.mult),
             reads=TK[2] + ["T0"], writes=TK[2])
        S.op(DVE, lambda e: e.tensor_tensor(out=b_ap, in0=thi[:], in1=a2[:], op=ALU.mult), reads=TK[2] + TK[3], writes=[xa_keys[8 + j]])
        S.op(DVE, lambda e: e.tensor_tensor_scan(out=thr[:], data0=a_ap, data1=b_ap, initial=0.0, op0=ALU.mult, op1=ALU.add),
             reads=[xa_keys[j], xa_keys[8 + j], "T1"], writes=["T1"])
        S.op(DVE, lambda e: e.tensor_copy(out=HEND[:, j:j + 1], in_=thr[:, T - 1:T]), reads=["T1"], writes=["HEND"])

    def v_group(s, hb):
        for batch in range(2):
            banks = [(PSA, 0, "PA0"), (PSA, 1, "PA1"), (PSB, 0, "PB0"), (PSB, 1, "PB1")]
            for ci in range(4):
                c = batch * 4 + ci
                pp, half, key = banks[ci]
                for k in range(16):
                    S.op(PE, lambda e, pp=pp, half=half, k=k, c=c: e.matmul(
                        pp[:, half * 512:(half + 1) * 512], lhsT=HT[:, k, HAL + c * 128:HAL + (c + 1) * 128], rhs=W[:, s, k, :],
                        start=(k == 0), stop=(k == 15)),
                        reads=["W%d" % s, ht_keys[k]], writes=[key], inc=(k == 15))
            vgs = []
            for ci in range(4):
                pp, half, key = banks[ci]
                ta = TA[ci // 2]
                tk = "TA%d" % (ci // 2)
                vg = ta[:, (ci % 2) * 512:(ci % 2 + 1) * 512]
                S.op(ACT, lambda e, pp=pp, half=half, vg=vg: e.activation(out=vg, in_=pp[:, half * 512:(half + 1) * 512], func=AF.Gelu_apprx_tanh,
                                                                            bias=ZEROC, scale=1.0), reads=[key, "CST"], writes=[tk])
                vgs.append((vg, tk))
                for hh in range(4):
                    S.op(DVE, lambda e, vg=vg, ci=ci, hh=hh: e.bn_stats(out=STATS[:, ci * 4 + hh, :], in_=vg[:, hh * 128:(hh + 1) * 128]),
                         reads=[tk], writes=["STATS"])
                    S.op(DVE, lambda e, ci=ci, hh=hh: e.bn_aggr(out=MV[:, ci * 4 + hh, :], in_=STATS[:, ci * 4 + hh, :]),
                         reads=["STATS"], writes=["MV"])
            S.op(ACT, lambda e: e.activation(out=RS4[:], in_=MV[:, :, 1], func=AF.Sqrt, bias=EPSC, scale=1.0), reads=["MV", "CST"], writes=["RS4"])
            S.op(DVE, lambda e: e.reciprocal(out=RS4[:], in_=RS4[:]), reads=["RS4"], writes=["RS4"])
            for ci in range(4):
                c = batch * 4 + ci
                vg, tk = vgs[ci]
                vh = VH[ci % 2]
                vk = "VH%d" % (ci % 2)
                for hh in range(4):
                    S.op(DVE, lambda e, vg=vg, vh=vh, ci=ci, hh=hh: e.tensor_scalar(
                        out=vh[:, hh * 128:(hh + 1) * 128], in0=vg[:, hh * 128:(hh + 1) * 128],
                        scalar1=MV[:, ci * 4 + hh, 0:1], scalar2=RS4[:, ci * 4 + hh:ci * 4 + hh + 1], op0=ALU.subtract, op1=ALU.mult),
                        reads=[tk, "MV", "RS4"], writes=[vk])
                for hh in range(4):
                    S.op(PE, lambda e, vh=vh, hh=hh: e.matmul(PSC[:, hh * 128:(hh + 1) * 128], lhsT=vh[:, hh * 128:(hh + 1) * 128],
                                                               rhs=WSB[:, hb + hh, :], start=True, stop=True),
                         reads=[vk, "WSB"], writes=["PC0"], inc=(hh == 3))
                for hh in range(4):
                    S.op(DVE, lambda e, hh=hh, c=c: e.scalar_tensor_tensor(
                        out=TT[hh][:, c * 128:(c + 1) * 128], in0=PSC[:, hh * 128:(hh + 1) * 128], scalar=P(O_LNG + hb + hh),
                        in1=R[:, (hb + hh) * 128:(hb + hh + 1) * 128], op0=ALU.mult, op1=ALU.add),
                        reads=["PC0", "PRM"] + RK, writes=TK[hh])

    sumsq_state = {"n": 0}

    def branch_tail(y_ap, ykey, idx, scale_col, mixk):
        sq = SQ[idx % 2]
        sk = "SQ%d" % (idx % 2)
        S.op(ACT, lambda e: e.activation(out=sq[:, 0:T], in_=y_ap, func=AF.Square, bias=ZEROC, scale=1.0), reads=[ykey, "CST"], writes=[sk])
        n = sumsq_state["n"]
        for tb in range(2):
            S.op(PE, lambda e, tb=tb, n=n: e.matmul(PSS[:, tb * 512:(tb + 1) * 512], lhsT=ONES[:], rhs=sq[:, tb * 512:(tb + 1) * 512],
                                                    start=(n == 0), stop=(n == 7)),
                 reads=["ONES", sk], writes=["S%d" % tb], inc=(tb == 1))
        sumsq_state["n"] = (n + 1) % 8
        S.op(ACT, lambda e: e.activation(out=MIX[:, mixk, :], in_=y_ap, func=AF.Identity, bias=ZEROC, scale=P(scale_col)),
             reads=[ykey, "PRM", "CST"], writes=[mx_keys[mixk]])

    def branch_finish(k0):
        S.op(ACT, lambda e: e.activation(out=RSTDM[:], in_=PSS[:, 0:T], func=AF.Sqrt, bias=EPSC, scale=1.0 / 1024.0),
             reads=["S0", "S1", "CST"], writes=["T1"])
        S.op(DVE, lambda e: e.reciprocal(out=RSTDM[:], in_=RSTDM[:]), reads=["T1"], writes=["T1"])
        for k in range(k0, k0 + 8):
            S.op(DVE, lambda e, k=k: e.tensor_tensor(out=MIX[:, k, :], in0=MIX[:, k, :], in1=RSTDM[:], op=ALU.mult),
                 reads=[mx_keys[k], "T1"], writes=[mx_keys[k]])

    def u_group(s, hb):
        for cc in range(4):
            h = hb + cc
            ps, pk = inproj_fm(s, cc)
            gu = TA[cc % 2]
            gk = "TA%d" % (cc % 2)
            S.op(ACT, lambda e, ps=ps, gu=gu: e.activation(out=gu[:, 0:T], in_=ps[:, 0:T], func=AF.Gelu_apprx_tanh, bias=ZEROC, scale=1.0),
                 reads=pk + ["CST"], writes=[gk])
            S.op(DVE, lambda e, gu=gu, cc=cc: e.tensor_tensor(out=gu[:, 0:T], in0=gu[:, 0:T], in1=TT[cc][:], op=ALU.mult),
                 reads=[gk] + TK[cc], writes=[gk])
            branch_tail(gu[:, 0:T], gk, cc, O_GA + h, h)

    def gb_group(s, jb):
        for cc in range(4):
            j = jb + cc
            ps, pk = inproj_fm(s, cc)
            hbuf = TT[cc % 2]
            hk = "T%d" % (cc % 2)
            S.op(DVE, lambda e, hbuf=hbuf, j=j: e.tensor_tensor_scan(out=hbuf[:], data0=XA[:, j, HAL:TW], data1=XA[:, 8 + j, HAL:TW],
                                                                      initial=INIT[:, j:j + 1], op0=ALU.mult, op1=ALU.add),
                 reads=[xa_keys[j], xa_keys[8 + j], "INIT"], writes=[hk])
            gg = TA[cc % 2]
            gk = "TA%d" % (cc % 2)
            S.op(ACT, lambda e, ps=ps, gg=gg: e.activation(out=gg[:, 0:T], in_=ps[:, 0:T], func=AF.Gelu_apprx_tanh, bias=ZEROC, scale=1.0),
                 reads=pk + ["CST"], writes=[gk])
            S.op(DVE, lambda e, gg=gg, hbuf=hbuf: e.tensor_tensor(out=gg[:, 0:T], in0=gg[:, 0:T], in1=hbuf[:], op=ALU.mult),
                 reads=[gk, hk], writes=[gk])
            branch_tail(gg[:, 0:T], gk, cc, O_GL + j, 8 + j)

    for gi in range(2):
        s = w_next()
        for cc in range(4):
            lru_chunk(s, cc, gi * 4 + cc)
    s = w_next(); v_group(s, 0)
    s = w_next(); u_group(s, 0)
    S.dma(SP, lambda e: e.dma_start(out=ag1_in, in_=HEND[:]), dbo, reads=["HEND"], writes=["ag1_in"])
    S.dma(POOL, lambda e: e.collective_compute("AllGather", ALU.bypass, replica_groups=[list(range(8))],
                                               ins=[ag1_in], outs=[ag1_out]), dcc[1], reads=["ag1_in"], writes=["ag1_out"])
    S.dma(SP, lambda e: e.dma_start(out=AGT1[:], in_=ag1_out.rearrange("(r p) c -> p r c", p=128)), dbi, reads=["ag1_out"], writes=["AGT1"])
    s = w_next(); v_group(s, 4)
    s = w_next(); u_group(s, 4)
    branch_finish(0)
    S.op(DVE, lambda e: e.tensor_scalar(out=INIT[:], in0=AGT1[:, 0, :], scalar1=P(O_SEL), scalar2=None, op0=ALU.mult),
         reads=["AGT1", "PRM"], writes=["INIT"])
    for r in range(1, 8):
        S.op(DVE, lambda e, r=r: e.scalar_tensor_tensor(out=INIT[:], in0=AGT1[:, r, :], scalar=P(O_SEL + r), in1=INIT[:],
                                                        op0=ALU.mult, op1=ALU.add), reads=["AGT1", "INIT", "PRM"], writes=["INIT"])
    s = w_next(); gb_group(s, 0)
    s = w_next(); gb_group(s, 4)
    branch_finish(8)
    for q in range(4):
        S.dma(SP, lambda e, q=q: e.dma_start(out=XA[:, 4 * q:4 * q + 4, :],
                                             in_=xT[512 * q:512 * (q + 1), :].rearrange("(k p) t -> p k t", p=128)),
              dxr, writes=xa_keys[4 * q:4 * q + 4])
    tok_xr = (dxr, dxr.cnt)
    for k in xa_keys:
        S.last_w[k] = tok_xr

    def proj_down(s, mm, m, gate_col, nk):
        ps, pk = next_pair()
        for k in range(nk):
            for tb in range(2):
                S.op(PE, lambda e, k=k, tb=tb, ps=ps: e.matmul(ps[:, tb * 512:(tb + 1) * 512], lhsT=W[:, s, k, mm * 128:(mm + 1) * 128],
                                                               rhs=MIX[:, k, tb * 512:(tb + 1) * 512], start=(k == 0), stop=(k == nk - 1)),
                     reads=["W%d" % s, mx_keys[k]], writes=[pk[tb]], inc=(k == nk - 1 and tb == 1))
        S.op(DVE, lambda e, ps=ps: e.scalar_tensor_tensor(out=XA[:, m, HAL:TW], in0=ps[:, 0:T], scalar=MOD[:, gate_col + m:gate_col + m + 1],
                                                          in1=XA[:, m, HAL:TW], op0=ALU.mult, op1=ALU.add),
             reads=pk + ["MOD", xa_keys[m]], writes=[xa_keys[m]])

    for g in range(4):
        s = w_next()
        for mm in range(4):
            proj_down(s, mm, g * 4 + mm, 32, 16)

    S.op(DVE, lambda e: e.tensor_copy(out=X1H[:].rearrange("p (k t) -> p k t", t=2), in_=XA[:, :, TW - 2:TW]), reads=xa_keys, writes=["X1H"])
    S.dma(SP, lambda e: e.dma_start(out=ag2_in, in_=X1H[:]), dbo, reads=["X1H"], writes=["ag2_in"])
    S.dma(POOL, lambda e: e.collective_compute("AllGather", ALU.bypass, replica_groups=[list(range(8))],
                                               ins=[ag2_in], outs=[ag2_out]), dcc[2], reads=["ag2_in"], writes=["ag2_out"])
    S.dma(SP, lambda e: e.dma_start(out=AGT2[:], in_=ag2_out.rearrange("(r p) c -> p r c", p=128)), dbi, reads=["ag2_out"], writes=["T0"])

    rms_main(D)
    for k in range(16):
        ta = TA[k % 2]
        tk = "TA%d" % (k % 2)
        S.op(DVE, lambda e, k=k, ta=ta: e.scalar_tensor_tensor(out=ta[:, HAL:TW], in0=XA[:, k, HAL:TW], scalar=A2[:, k:k + 1], in1=RSTDM[:],
                                                               op0=ALU.mult, op1=ALU.mult),
             reads=[xa_keys[k], "A2", "T1"], writes=[tk])
        S.op(ACT, lambda e, k=k, ta=ta: e.activation(out=HT[:, k, HAL:TW], in_=ta[:, HAL:TW], func=AF.Identity, bias=MOD[:, 48 + k:49 + k], scale=1.0),
             reads=[tk, "MOD"], writes=[ht_keys[k]])
    S.op(DVE, lambda e: e.tensor_scalar(out=XH[:], in0=AGT2[:, 0, :], scalar1=P(O_SEL), scalar2=None, op0=ALU.mult), reads=["T0", "PRM"], writes=["XH"])
    for r in range(1, 8):
        S.op(DVE, lambda e, r=r: e.scalar_tensor_tensor(out=XH[:], in0=AGT2[:, r, :], scalar=P(O_SEL + r), in1=XH[:], op0=ALU.mult, op1=ALU.add),
             reads=["T0", "XH", "PRM"], writes=["XH"])
    S.op(ACT, lambda e: e.activation(out=SQH2[:], in_=XH[:], func=AF.Square, bias=ZEROC, scale=1.0), reads=["XH", "CST"], writes=["SQH2"])
    for k in range(16):
        S.op(PE, lambda e, k=k: e.matmul(PSC[:, 0:2], lhsT=ONES[:], rhs=SQH2[:, k * 2:(k + 1) * 2], start=(k == 0), stop=(k == 15)),
             reads=["ONES", "SQH2"], writes=["PC0"], inc=(k == 15))
    S.op(ACT, lambda e: e.activation(out=RSTDH[:, 0:2], in_=PSC[:, 0:2], func=AF.Sqrt, bias=EPSC, scale=1.0 / D), reads=["PC0", "CST"], writes=["RSTDH"])
    S.op(DVE, lambda e: e.reciprocal(out=RSTDH[:, 0:2], in_=RSTDH[:, 0:2]), reads=["RSTDH"], writes=["RSTDH"])
    for k in range(16):
        S.op(DVE, lambda e, k=k: e.scalar_tensor_tensor(out=TMPH[:, k * 2:(k + 1) * 2], in0=XH[:, k * 2:(k + 1) * 2], scalar=A2[:, k:k + 1],
                                                        in1=RSTDH[:, 0:2], op0=ALU.mult, op1=ALU.mult),
             reads=["XH", "A2", "RSTDH"], writes=["TMPH"])
        S.op(DVE, lambda e, k=k: e.tensor_scalar(out=HT[:, k, 2:HAL], in0=TMPH[:, k * 2:(k + 1) * 2], scalar1=MOD[:, 48 + k:49 + k],
                                                 scalar2=P(O_FLAG), op0=ALU.add, op1=ALU.mult),
             reads=["TMPH", "MOD", "PRM"], writes=["HTh"])

    ffn_i = [0]
    for H in range(3):
        for q in range(4):
            sg = w_next()
            sv = w_next(2)
            for cc in range(4):
                j = H * 16 + q * 4 + cc
                jj = q * 4 + cc
                i = ffn_i[0]
                ffn_i[0] += 1
                pg, pkg = next_pair()
                for k in range(16):
                    for tb in range(2):
                        S.op(PE, lambda e, k=k, tb=tb, pg=pg, cc=cc, sg=sg: e.matmul(
                            pg[:, tb * 512:(tb + 1) * 512], lhsT=W[:, sg, k, cc * 128:(cc + 1) * 128],
                            rhs=HT[:, k, HAL + tb * 512:HAL + (tb + 1) * 512], start=(k == 0), stop=(k == 15)),
                            reads=["W%d" % sg, ht_keys[k]], writes=[pkg[tb]], inc=False)
                    S.op(PE, lambda e, k=k, cc=cc, sg=sg, j=j: e.matmul(
                        PSS[:, j * 2:j * 2 + 2], lhsT=W[:, sg, k, cc * 128:(cc + 1) * 128], rhs=HT[:, k, 2:HAL],
                        start=(k == 0), stop=(k == 15)),
                        reads=["W%d" % sg, "HTh"], writes=["S0"], inc=(k == 15))
                pv, pkv = next_pair()
                for k in range(16):
                    for tb in range(2):
                        S.op(PE, lambda e, k=k, tb=tb, pv=pv, cc=cc, sv=sv: e.matmul(
                            pv[:, tb * 512:(tb + 1) * 512], lhsT=W[:, sv, k, cc * 128:(cc + 1) * 128],
                            rhs=HT[:, k, HAL + tb * 512:HAL + (tb + 1) * 512], start=(k == 0), stop=(k == 15)),
                            reads=["W%d" % sv, ht_keys[k]], writes=[pkv[tb]], inc=(k == 15 and tb == 1))
                gs = TA[i % 2]
                gk = "TA%d" % (i % 2)
                acc = TT[i % 2]
                ak = "T%d" % (i % 2)
                S.op(ACT, lambda e, gs=gs, pg=pg: e.activation(out=gs[:, 2:2 + T], in_=pg[:, 0:T], func=AF.Identity, bias=ZEROC, scale=1.0),
                     reads=pkg + ["CST"], writes=[gk])
                S.op(ACT, lambda e, gs=gs, j=j: e.activation(out=gs[:, 0:2], in_=PSS[:, j * 2:j * 2 + 2], func=AF.Identity, bias=ZEROC, scale=1.0),
                     reads=["S0", "CST"], writes=[gk])
                S.op(ACT, lambda e, gs=gs, acc=acc, j=j: e.activation(out=acc[:], in_=gs[:, 2:2 + T], func=AF.Identity, bias=P(O_FCB + j),
                                                                      scale=P(O_FCW + j * 3 + 2)), reads=[gk, "PRM"], writes=[ak])
                for sft in (1, 2):
                    S.op(DVE, lambda e, gs=gs, acc=acc, j=j, sft=sft: e.scalar_tensor_tensor(
                        out=acc[:], in0=gs[:, 2 - sft:2 - sft + T], scalar=P(O_FCW + j * 3 + 2 - sft), in1=acc[:], op0=ALU.mult, op1=ALU.add),
                        reads=[gk, ak, "PRM"], writes=[ak])
                S.op(ACT, lambda e, acc=acc: e.activation(out=acc[:], in_=acc[:], func=AF.Gelu_apprx_tanh, bias=ZEROC, scale=1.0),
                     reads=[ak, "CST"], writes=[ak])
                S.op(DVE, lambda e, acc=acc, pv=pv, jj=jj: e.tensor_tensor(out=MIX[:, jj, :], in0=pv[:, 0:T], in1=acc[:], op=ALU.mult),
                     reads=[ak] + pkv, writes=[mx_keys[jj]])
        for gq in range(4):
            s = w_next()
            for mm in range(4):
                proj_down(s, mm, gq * 4 + mm, 80, 16)

    rms_main(D)
    for k in range(16):
        S.op(DVE, lambda e, k=k: e.scalar_tensor_tensor(out=XA[:, k, HAL:TW], in0=XA[:, k, HAL:TW], scalar=P(O_NF + k), in1=RSTDM[:],
                                                        op0=ALU.mult, op1=ALU.mult),
             reads=[xa_keys[k], "PRM", "T1"], writes=[xa_keys[k]])
        S.dma(SP, lambda e, k=k: e.dma_start(out=yT[k * 128:(k + 1) * 128, :], in_=XA[:, k, HAL:TW]), dout, reads=[xa_keys[k]])
    S.wait(SP, (dout, dout.cnt))
    SP.prog.append(("o", lambda e: e.nop(), False))

    with S.alloc(nc):
        with nc.Block() as block:
            S.emit(block)
    es.close()
    return nc


def _pk(v, n):
    return np.ascontiguousarray(np.asarray(v, np.float32).reshape(n, 128).T)


def prep_inputs(inp):
    f = lambda a: np.asarray(a, np.float32)
    x = f(inp["x"]); c = f(inp["c"])
    w_ada = f(inp["w_ada"])[0]
    shared = {
        "w_in": np.ascontiguousarray(f(inp["w_in"])[0]),
        "w_out": np.ascontiguousarray(f(inp["w_out"])[0]),
        "w_up": np.ascontiguousarray(f(inp["w_up"])[0]),
        "w_down": np.ascontiguousarray(f(inp["w_down"])[0]),
    }
    cT = np.ascontiguousarray(c.T.reshape(16, 128, 4).transpose(1, 0, 2).reshape(128, 64))
    w_s = f(inp["gmlp_w_s"])[0]
    wsT = np.ascontiguousarray(w_s.transpose(2, 0, 1).reshape(128, 1024))
    tri = np.triu(np.ones((128, 128), np.float32))
    bs_bc = np.ascontiguousarray(np.broadcast_to(f(inp["gmlp_b_s"])[0].reshape(1, 1024), (128, 1024)))

    def bd(w):
        w = f(w)[0]
        o = np.zeros((128, 8, 128), np.float32)
        for j in range(8):
            o[0:64, j, 0:64] = w[2 * j]
            o[64:128, j, 64:128] = w[2 * j + 1]
        return o.reshape(128, 1024)

    wr_bd = bd(inp["lru_w_r"]); wi_bd = bd(inp["lru_w_i"])
    base = np.zeros((128, NPRM), np.float32)
    base[:, O_BADA:O_BADA + 96] = _pk(f(inp["b_ada"])[0], 96)
    base[:, O_N1:O_N1 + 16] = _pk(f(inp["norm1"])[0], 16)
    base[:, O_N2:O_N2 + 16] = _pk(f(inp["norm2"])[0], 16)
    base[:, O_NF:O_NF + 16] = _pk(f(inp["norm_final"]), 16)
    base[:, O_LNG:O_LNG + 8] = _pk(f(inp["gmlp_ln_g"])[0], 8)
    base[:, O_LNB:O_LNB + 8] = _pk(f(inp["gmlp_ln_b"])[0], 8)
    base[:, O_CW:O_CW + 32] = f(inp["lru_conv_w"])[0].reshape(4, 8, 128).transpose(2, 1, 0).reshape(128, 32)
    base[:, O_CB:O_CB + 8] = _pk(f(inp["lru_conv_b"])[0], 8)
    base[:, O_BR:O_BR + 8] = _pk(f(inp["lru_b_r"])[0], 8)
    base[:, O_BI:O_BI + 8] = _pk(f(inp["lru_b_i"])[0], 8)
    base[:, O_LAM:O_LAM + 8] = _pk(f(inp["lru_lambda"])[0], 8)
    base[:, O_GL:O_GL + 8] = _pk(f(inp["out_norm_lru"])[0], 8)
    base[:, O_GA:O_GA + 8] = _pk(f(inp["out_norm_gmlp"])[0], 8)
    base[:, O_FCW:O_FCW + 144] = f(inp["ffn_conv_w"])[0].reshape(3, 48, 128).transpose(2, 1, 0).reshape(128, 144)
    base[:, O_FCB:O_FCB + 48] = _pk(f(inp["ffn_conv_b"])[0], 48)
    maps = []
    for core in range(8):
        b, half = core // 2, core % 2
        t0 = half * T
        xt = np.zeros((D, TW), np.float32)
        xt[:, HAL:] = x[b, t0:t0 + T, :].T
        if half == 1:
            xt[:, 0:HAL] = x[b, t0 - HAL:t0, :].T
        p = base.copy()
        if half == 1:
            p[:, O_SEL + core - 1] = 1.0
            p[:, O_FLAG] = 1.0
        p[:, O_SELB + b] = 1.0
        m = dict(shared)
        m.update({
            "xT": xt, "cT": cT, "w_ada_r": np.ascontiguousarray(w_ada[:, core * 1536:(core + 1) * 1536]),
            "prm": p, "wsT": wsT, "tri": tri, "bs_bc": bs_bc, "wr_bd": wr_bd, "wi_bd": wi_bd,
        })
        maps.append(m)
    return maps


_NC = None


def kernel(**inputs):
    global _NC
    maps = prep_inputs(inputs)
    if _NC is None:
        _NC = build_program()
    res = run_bass_kernel_spmd(_NC, maps, core_ids=list(range(8)))
    out = np.empty((4, 2048, D), np.float32)
    for core in range(8):
        b, half = core // 2, core % 2
        out[b, half * T:(half + 1) * T, :] = np.asarray(res.results[core]["yT"], np.float32).T
    return out
```

```python
import contextlib
import numpy as np
import concourse.bass as bass
import concourse.mybir as mybir
from concourse.bass_utils import run_bass_kernel_spmd

F32 = mybir.dt.float32
BF16 = mybir.dt.bfloat16
AF = mybir.ActivationFunctionType
ALU = mybir.AluOpType

D = 2048
T = 1024
HAL = 4
TW = T + HAL
EPS = 1e-6
NPRM = 448
O_BADA, O_N1, O_N2, O_NF, O_LNG, O_LNB, O_CW, O_CB, O_BR, O_BI, O_LAM, O_GL, O_GA, O_FCW, O_FCB, O_SEL, O_FLAG, O_SELB = (
    0, 96, 112, 128, 144, 152, 160, 192, 200, 208, 216, 224, 232, 240, 384, 432, 440, 441)


class DSem:
    def __init__(self, name, step=16):
        self.name = name
        self.sem = None
        self.cnt = 0
        self.step = step


class Eng:
    def __init__(self, name, self_sync=True):
        self.name = name
        self.sem = None
        self.cnt = 0
        self.prog = []
        self.waited = {}
        self.self_sync = self_sync

    def _wait(self, tok):
        src, val = tok
        if src is self and not self.self_sync:
            return
        if self.waited.get(id(src), 0) >= val:
            return
        self.waited[id(src)] = val
        self.prog.append(("w", src, val))


class Sched:
    def __init__(self):
        self.engs = {}
        self.dsems = []
        self.last_w = {}
        self.readers = {}

    def eng(self, name, self_sync=True):
        e = Eng(name, self_sync)
        self.engs[name] = e
        return e

    def dsem(self, name, step=16):
        d = DSem(name, step)
        self.dsems.append(d)
        return d

    def _deps(self, e, reads, writes, extra):
        for k in reads:
            t = self.last_w.get(k)
            if t is not None:
                e._wait(t)
        for k in writes:
            t = self.last_w.get(k)
            if t is not None:
                e._wait(t)
            for t in self.readers.get(k, ()):
                e._wait(t)
        for t in extra:
            if t is not None:
                e._wait(t)

    def _commit(self, tok, reads, writes):
        for k in reads:
            self.readers.setdefault(k, []).append(tok)
        for k in writes:
            self.last_w[k] = tok
            self.readers[k] = []

    def op(self, e, fn, reads=(), writes=(), extra=(), inc=True):
        self._deps(e, reads, writes, extra)
        if inc:
            e.cnt += 1
            tok = (e, e.cnt)
            e.prog.append(("o", fn, True))
        else:
            tok = (e, e.cnt + 1)
            e.prog.append(("o", fn, False))
        self._commit(tok, reads, writes)
        return tok

    def dma(self, e, fn, ds, reads=(), writes=(), extra=()):
        self._deps(e, reads, writes, extra)
        ds.cnt += ds.step
        tok = (ds, ds.cnt)
        e.prog.append(("d", fn, ds))
        self._commit(tok, reads, writes)
        return tok

    def wait(self, e, tok):
        e._wait(tok)

    def fix_pending(self):
        pass

    def alloc(self, nc):
        stack = contextlib.ExitStack()
        for e in self.engs.values():
            e.sem = stack.enter_context(nc.semaphore("s_" + e.name))
        for d in self.dsems:
            d.sem = stack.enter_context(nc.semaphore("d_" + d.name))
        return stack

    def emit(self, block):
        def runner(e):
            def run(engine):
                for item in e.prog:
                    if item[0] == "w":
                        engine.wait_ge(item[1].sem, item[2])
                    elif item[0] == "o":
                        ins = item[1](engine)
                        if item[2]:
                            ins.then_inc(e.sem, 1)
                    else:
                        ins = item[1](engine)
                        ins.then_inc(item[2].sem, item[2].step)
            return run

        for name, e in self.engs.items():
            getattr(block, name)(runner(e))


def build_program():
    nc = bass.Bass("TRN2", target_bir_lowering=False)

    def din(name, shape):
        return nc.dram_tensor(name, list(shape), F32, kind="ExternalInput").ap()

    xT = din("xT", [D, TW])
    cT = din("cT", [128, 64])
    w_ada = din("w_ada_r", [D, 1536])
    prm = din("prm", [128, NPRM])
    wsT_d = din("wsT", [128, 1024])
    tri_d = din("tri", [128, 128])
    bs_d = din("bs_bc", [128, 1024])
    wr_d = din("wr_bd", [128, 1024])
    wi_d = din("wi_bd", [128, 1024])
    w_in = din("w_in", [D, 4096])
    w_out = din("w_out", [D, D])
    w_up = din("w_up", [D, 12288])
    w_down = din("w_down", [6144, D])
    yT = nc.dram_tensor("yT", [D, T], F32, kind="ExternalOutput").ap()
    ag0_in = nc.dram_tensor("ag0_in", [128, 48], F32).ap()
    ag0_out = nc.dram_tensor("ag0_out", [1024, 48], F32).ap()
    ag1_in = nc.dram_tensor("ag1_in", [128, 8], F32).ap()
    ag1_out = nc.dram_tensor("ag1_out", [1024, 8], F32).ap()
    ag2_in = nc.dram_tensor("ag2_in", [128, 32], F32).ap()
    ag2_out = nc.dram_tensor("ag2_out", [1024, 32], F32).ap()

    S = Sched()
    PE = S.eng("tensor", self_sync=False)
    ACT = S.eng("scalar")
    DVE = S.eng("vector")
    POOL = S.eng("gpsimd")
    SP = S.eng("sync")
    dx = S.dsem("x")
    dp = S.dsem("prm")
    dwg = S.dsem("wg")
    dws = [S.dsem("w%d" % i) for i in range(3)]
    dbo = S.dsem("bo")
    dbi = S.dsem("bi")
    dxr = S.dsem("xr")
    dout = S.dsem("out")
    dcc = [S.dsem("cc%d" % i, step=1) for i in range(3)]

    es = contextlib.ExitStack()

    def sb(name, shape, dt=F32):
        return es.enter_context(nc.sbuf_tensor(name, list(shape), dt))

    def pst(name):
        return es.enter_context(nc.psum_tensor(name, [128, 1024], F32))

    XA = sb("XA", [128, 16, TW])
    HT = sb("HT", [128, 16, TW], BF16)
    MIX = sb("MIX", [128, 16, T], BF16)
    W = sb("W", [128, 3, 16, 512], BF16)
    SQ = [sb("SQ%d" % i, [128, T], BF16) for i in range(2)]
    TA = [sb("TA%d" % i, [128, TW]) for i in range(2)]
    TT01 = [sb("T%d" % i, [128, T]) for i in range(2)]
    TT = TT01 + [MIX[:, 8:10, :].rearrange("p a b -> p (a b)").bitcast(F32),
                 MIX[:, 10:12, :].rearrange("p a b -> p (a b)").bitcast(F32)]
    TK = [["T0"], ["T1"], ["MX8", "MX9"], ["MX10", "MX11"]]
    R = MIX[:, 12:14, :].rearrange("p a b -> p (a b)").bitcast(F32)
    RK = ["MX12", "MX13"]
    WRB = MIX[:, 14, :].rearrange("p (j q) -> p j q", q=128)
    WIB = MIX[:, 15, :].rearrange("p (j q) -> p j q", q=128)
    RSTDM = TT01[1]
    WSB = sb("WSB", [128, 8, 128], BF16)
    PRM = sb("PRM", [128, NPRM])
    CT = sb("CT", [128, 64])
    CB = sb("CB", [128, 64], BF16)
    ONES = sb("ONES", [128, 128], BF16)
    MODP = sb("MODP", [128, 48])
    MOD = sb("MOD", [128, 96])
    A1 = sb("A1", [128, 16])
    A2 = sb("A2", [128, 16])
    LP = sb("LP", [128, 64])
    CST = sb("CST", [128, 4])
    SQH = sb("SQH", [128, 16, 4], BF16)
    STATS = sb("STATS", [128, 16, 6])
    MV = sb("MV", [128, 16, 2])
    RS4 = sb("RS4", [128, 16])
    VH = [sb("VH%d" % i, [128, 512], BF16) for i in range(2)]
    HEND = sb("HEND", [128, 8])
    AGT1 = sb("AGT1", [128, 8, 8])
    INIT = sb("INIT", [128, 8])
    X1H = sb("X1H", [128, 32])
    XH = sb("XH", [128, 32])
    SQH2 = sb("SQH2", [128, 32], BF16)
    RSTDH = sb("RSTDH", [128, 4])
    TMPH = sb("TMPH", [128, 32])
    PSA = pst("PSA")
    PSB = pst("PSB")
    PSC = pst("PSC")
    PSS = pst("PSS")
    TRI = TT01[0][:, 0:128]
    MODG = sb("MODG", [128, 96, 4])
    HTMP = sb("HTMP", [128, 16, 4])
    AGT2 = TT01[0][:, 0:256].rearrange("p (a b) -> p a b", b=32)
    pairs = [(PSA, ["PA0", "PA1"]), (PSB, ["PB0", "PB1"]), (PSC, ["PC0", "PC1"])]
    pair_i = [0]

    def next_pair():
        p = pairs[pair_i[0] % 3]
        pair_i[0] += 1
        return p

    def P(o, n=1):
        return PRM[:, o:o + n]

    xa_keys = ["XA%d" % k for k in range(16)]
    ht_keys = ["HT%d" % k for k in range(16)]
    mx_keys = ["MX%d" % k for k in range(16)]

    wq = []
    for g in range(3):
        wq.append(w_ada[:, g * 512:(g + 1) * 512])
    IN_ORDER = [4, 5, 2, 0, 3, 1, 6, 7]
    for g in IN_ORDER:
        wq.append(w_in[:, g * 512:(g + 1) * 512])
    for g in range(4):
        wq.append(w_out[:, g * 512:(g + 1) * 512])
    for H in range(3):
        for q in range(4):
            j0 = H * 16 + q * 4
            wq.append(w_up[:, j0 * 128:j0 * 128 + 512])
            wq.append(w_up[:, 6144 + j0 * 128:6144 + j0 * 128 + 512])
        for gq in range(4):
            wq.append(w_down[H * 2048:(H + 1) * 2048, gq * 512:(gq + 1) * 512])
    w_issued = [0]
    w_used = [0]

    def w_issue_upto(n):
        while w_issued[0] < min(n, len(wq)):
            i = w_issued[0]
            s = i % 3
            src = wq[i].rearrange("(k p) e -> p k e", p=128)
            S.dma(POOL, lambda e, s=s, src=src: e.dma_start(out=W[:, s], in_=src), dws[s], writes=["W%d" % s])
            w_issued[0] += 1

    def w_next(ahead=3):
        i = w_used[0]
        w_used[0] += 1
        w_issue_upto(i + ahead)
        return i % 3

    for q in range(4):
        S.dma(SP, lambda e, q=q: e.dma_start(out=XA[:, 4 * q:4 * q + 4, :],
                                             in_=xT[512 * q:512 * (q + 1), :].rearrange("(k p) t -> p k t", p=128)),
              dx, writes=xa_keys[4 * q:4 * q + 4])
    S.dma(SP, lambda e: e.dma_start(out=PRM[:], in_=prm), dp, writes=["PRM"])
    S.dma(SP, lambda e: e.dma_start(out=CT[:], in_=cT), dp, writes=["CT"])
    S.dma(SP, lambda e: e.dma_start(out=TRI[:], in_=tri_d), dp, writes=["T0"])
    S.dma(SP, lambda e: e.dma_start(out=TA[1][:, 0:1024], in_=wsT_d), dp, writes=["TA1"])
    S.dma(SP, lambda e: e.dma_start(out=TA[0][:, 0:1024], in_=bs_d), dp, writes=["TA0"])
    tok_p = (dp, dp.cnt)
    for k in ["PRM", "CT", "T0", "TA1", "TA0"]:
        S.last_w[k] = tok_p
    tok_x = (dx, dx.cnt)
    for k in xa_keys:
        S.last_w[k] = tok_x
    w_issue_upto(3)
    S.dma(POOL, lambda e: e.dma_start(out=MIX[:, 14, :], in_=wr_d), dwg, writes=["MX14"])
    S.dma(POOL, lambda e: e.dma_start(out=MIX[:, 15, :], in_=wi_d), dwg, writes=["MX15"])
    tok_wg = (dwg, dwg.cnt)
    S.last_w["MX14"] = tok_wg
    S.last_w["MX15"] = tok_wg

    S.op(DVE, lambda e: e.memset(ONES[:], 1.0), writes=["ONES"])
    S.op(DVE, lambda e: e.memset(CST[:, 0:1], EPS), writes=["CST"])
    S.op(DVE, lambda e: e.memset(CST[:, 1:2], 1.0), writes=["CST"])
    S.op(DVE, lambda e: e.memset(CST[:, 2:3], 0.25), writes=["CST"])
    S.op(DVE, lambda e: e.memset(CST[:, 3:4], 0.0), writes=["CST"])
    EPSC = CST[:, 0:1]
    ONEC = CST[:, 1:2]
    QRTC = CST[:, 2:3]
    ZEROC = CST[:, 3:4]

    S.op(ACT, lambda e: e.activation(out=LP[:, 32:40], in_=P(O_LAM, 8), func=AF.Abs, bias=ZEROC, scale=1.0), reads=["PRM", "CST"], writes=["LP"])
    S.op(ACT, lambda e: e.activation(out=LP[:, 32:40], in_=LP[:, 32:40], func=AF.Exp, bias=ZEROC, scale=-1.0), reads=["LP", "CST"], writes=["LP"])
    S.op(ACT, lambda e: e.activation(out=LP[:, 32:40], in_=LP[:, 32:40], func=AF.Ln, bias=ONEC, scale=1.0), reads=["LP", "CST"], writes=["LP"])
    S.op(DVE, lambda e: e.tensor_scalar(out=LP[:, 40:48], in0=P(O_LAM, 8), scalar1=-1.0, scalar2=0.0, op0=ALU.mult, op1=ALU.max),
         reads=["PRM", "LP"], writes=["LP"])
    S.op(DVE, lambda e: e.tensor_tensor(out=LP[:, 40:48], in0=LP[:, 40:48], in1=LP[:, 32:40], op=ALU.add), reads=["LP"], writes=["LP"])
    S.op(DVE, lambda e: e.tensor_scalar(out=LP[:, 0:8], in0=LP[:, 40:48], scalar1=-4.0, scalar2=None, op0=ALU.mult), reads=["LP"], writes=["LP"])
    S.op(DVE, lambda e: e.tensor_scalar(out=LP[:, 8:16], in0=LP[:, 40:48], scalar1=-8.0, scalar2=None, op0=ALU.mult), reads=["LP"], writes=["LP"])
    S.op(DVE, lambda e: e.tensor_scalar(out=LP[:, 16:24], in0=P(O_BR, 8), scalar1=0.5, scalar2=None, op0=ALU.mult), reads=["PRM", "LP"], writes=["LP"])
    S.op(DVE, lambda e: e.tensor_scalar(out=LP[:, 24:32], in0=P(O_BI, 8), scalar1=0.5, scalar2=None, op0=ALU.mult), reads=["PRM", "LP"], writes=["LP"])
    for h in range(8):
        S.op(DVE, lambda e, h=h: e.tensor_tensor(out=WSB[:, h, :], in0=TA[1][:, h * 128:(h + 1) * 128], in1=TRI, op=ALU.mult),
             reads=["TA1", "T0"], writes=["WSB"])
    for h in range(8):
        S.op(PE, lambda e, h=h: e.matmul(PSB[:, h * 128:(h + 1) * 128], lhsT=ONES[:], rhs=WSB[:, h, :], start=True, stop=True),
             reads=["ONES", "WSB"], writes=["PB0", "PB1"], inc=(h == 7))
    for h in range(8):
        S.op(DVE, lambda e, h=h: e.scalar_tensor_tensor(out=R[:, h * 128:(h + 1) * 128], in0=PSB[:, h * 128:(h + 1) * 128],
                                                        scalar=P(O_LNB + h), in1=TA[0][:, h * 128:(h + 1) * 128],
                                                        op0=ALU.mult, op1=ALU.add),
             reads=["PB0", "PB1", "PRM", "TA0"], writes=RK)

    S.op(ACT, lambda e: e.activation(out=CB[:], in_=CT[:], func=AF.Silu, bias=ZEROC, scale=1.0), reads=["CT", "CST"], writes=["CB"])
    for g in range(3):
        s = w_next()
        for cc in range(4):
            c0 = (g * 4 + cc) * 4
            for k in range(16):
                S.op(PE, lambda e, s=s, cc=cc, k=k, c0=c0: e.matmul(
                    PSA[:, c0:c0 + 4], lhsT=W[:, s, k, cc * 128:(cc + 1) * 128], rhs=CB[:, k * 4:(k + 1) * 4],
                    start=(k == 0), stop=(k == 15)),
                    reads=["W%d" % s, "CB"], writes=["PA0"], inc=(k == 15))
    S.op(DVE, lambda e: e.tensor_copy(out=MODP[:], in_=PSA[:, 0:48]), reads=["PA0"], writes=["MODP"])
    S.dma(SP, lambda e: e.dma_start(out=ag0_in, in_=MODP[:]), dbo, reads=["MODP"], writes=["ag0_in"])
    S.dma(POOL, lambda e: e.collective_compute("AllGather", ALU.bypass, replica_groups=[list(range(8))],
                                               ins=[ag0_in], outs=[ag0_out]), dcc[0], reads=["ag0_in"], writes=["ag0_out"])
    S.dma(SP, lambda e: e.dma_start(out=MODG[:].rearrange("p (r c) b -> p r (c b)", r=8),
                                    in_=ag0_out.rearrange("(r p) c -> p r c", p=128)),
          dbi, reads=["ag0_out"], writes=["MODG"])
    def rms_square(k, buf, bkey):
        S.op(ACT, lambda e, k=k, buf=buf: e.activation(out=buf, in_=XA[:, k, HAL:TW], func=AF.Square, bias=ZEROC, scale=1.0),
             reads=[xa_keys[k], "CST"], writes=[bkey])

    def rms_mm(k, buf, bkey):
        for tb in range(2):
            S.op(PE, lambda e, k=k, buf=buf, tb=tb: e.matmul(PSS[:, tb * 512:(tb + 1) * 512], lhsT=ONES[:], rhs=buf[:, tb * 512:(tb + 1) * 512],
                                                           start=(k == 0), stop=(k == 15)),
                 reads=["ONES", bkey], writes=["S%d" % tb], inc=(tb == 1))

    def rms_finish(sum_div):
        S.op(ACT, lambda e: e.activation(out=RSTDM[:], in_=PSS[:, 0:T], func=AF.Sqrt, bias=EPSC, scale=1.0 / sum_div),
             reads=["S0", "S1", "CST"], writes=["T1"])
        S.op(DVE, lambda e: e.reciprocal(out=RSTDM[:], in_=RSTDM[:]), reads=["T1"], writes=["T1"])

    def sqbuf2(k):
        return SQ[k % 2][:, 0:T], "SQ%d" % (k % 2)

    S.op(ACT, lambda e: e.activation(out=SQH[:], in_=XA[:, :, 0:HAL], func=AF.Square, bias=ZEROC, scale=1.0), reads=xa_keys + ["CST"], writes=["SQH"])
    for k in range(16):
        S.op(PE, lambda e, k=k: e.matmul(PSC[:, 0:HAL], lhsT=ONES[:], rhs=SQH[:, k, :], start=(k == 0), stop=(k == 15)),
             reads=["ONES", "SQH"], writes=["PC0"], inc=(k == 15))
    S.op(ACT, lambda e: e.activation(out=RSTDH[:, 0:HAL], in_=PSC[:, 0:HAL], func=AF.Sqrt, bias=EPSC, scale=1.0 / D), reads=["PC0", "CST"], writes=["RSTDH"])
    S.op(DVE, lambda e: e.reciprocal(out=RSTDH[:, 0:HAL], in_=RSTDH[:, 0:HAL]), reads=["RSTDH"], writes=["RSTDH"])
    for k in range(16):
        rms_square(k, MIX[:, k % 8, :], mx_keys[k % 8])
        rms_mm(k, MIX[:, k % 8, :], mx_keys[k % 8])
    rms_finish(D)
    S.op(DVE, lambda e: e.tensor_scalar(out=MOD[:], in0=MODG[:, :, 0], scalar1=P(O_SELB), scalar2=None, op0=ALU.mult),
         reads=["MODG", "PRM"], writes=["MOD"])
    for b in range(1, 4):
        S.op(DVE, lambda e, b=b: e.scalar_tensor_tensor(out=MOD[:], in0=MODG[:, :, b], scalar=P(O_SELB + b), in1=MOD[:],
                                                        op0=ALU.mult, op1=ALU.add), reads=["MODG", "MOD", "PRM"], writes=["MOD"])
    S.op(DVE, lambda e: e.tensor_tensor(out=MOD[:], in0=MOD[:], in1=P(O_BADA, 96), op=ALU.add), reads=["MOD", "PRM"], writes=["MOD"])
    S.op(DVE, lambda e: e.scalar_tensor_tensor(out=A1[:], in0=MOD[:, 16:32], scalar=1.0, in1=P(O_N1, 16), op0=ALU.add, op1=ALU.mult),
         reads=["MOD", "PRM"], writes=["A1"])
    S.op(DVE, lambda e: e.scalar_tensor_tensor(out=A2[:], in0=MOD[:, 64:80], scalar=1.0, in1=P(O_N2, 16), op0=ALU.add, op1=ALU.mult),
         reads=["MOD", "PRM"], writes=["A2"])
    n1bufs = [(TA[0][:, HAL:TW], ["TA0"]), (TA[1][:, HAL:TW], ["TA1"]), (TT[0][:], TK[0]), (TT[2][:], TK[2]), (TT[3][:], TK[3])]
    for k in range(16):
        tb_, tk_ = n1bufs[k % 5]
        S.op(DVE, lambda e, k=k, tb_=tb_: e.scalar_tensor_tensor(out=tb_, in0=XA[:, k, HAL:TW], scalar=A1[:, k:k + 1], in1=RSTDM[:],
                                                                 op0=ALU.mult, op1=ALU.mult),
             reads=[xa_keys[k], "A1", "T1"], writes=tk_)
        S.op(ACT, lambda e, k=k, tb_=tb_: e.activation(out=HT[:, k, HAL:TW], in_=tb_, func=AF.Identity, bias=MOD[:, k:k + 1], scale=1.0),
             reads=tk_ + ["MOD"], writes=[ht_keys[k]])
    for k in range(16):
        S.op(DVE, lambda e, k=k: e.scalar_tensor_tensor(out=HTMP[:, k, :], in0=XA[:, k, 0:HAL], scalar=A1[:, k:k + 1], in1=RSTDH[:, 0:HAL],
                                                        op0=ALU.mult, op1=ALU.mult),
             reads=[xa_keys[k], "A1", "RSTDH"], writes=["HTMP"])
    for k in range(16):
        S.op(DVE, lambda e, k=k: e.tensor_scalar(out=HT[:, k, 0:HAL], in0=HTMP[:, k, :], scalar1=MOD[:, k:k + 1], scalar2=P(O_FLAG),
                                                 op0=ALU.add, op1=ALU.mult),
             reads=["HTMP", "MOD", "PRM"], writes=["HTh"])
    def inproj_fm(s, cc, halo_col=None):
        ps, pk = next_pair()
        for k in range(16):
            for tb in range(2):
                S.op(PE, lambda e, s=s, cc=cc, k=k, tb=tb, ps=ps: e.matmul(
                    ps[:, tb * 512:(tb + 1) * 512], lhsT=W[:, s, k, cc * 128:(cc + 1) * 128],
                    rhs=HT[:, k, HAL + tb * 512:HAL + (tb + 1) * 512], start=(k == 0), stop=(k == 15)),
                    reads=["W%d" % s, ht_keys[k]], writes=[pk[tb]], inc=(k == 15 and tb == 1 and halo_col is None))
            if halo_col is not None:
                S.op(PE, lambda e, s=s, cc=cc, k=k: e.matmul(
                    PSS[:, halo_col:halo_col + HAL], lhsT=W[:, s, k, cc * 128:(cc + 1) * 128], rhs=HT[:, k, 0:HAL],
                    start=(k == 0), stop=(k == 15)),
                    reads=["W%d" % s, "HTh"], writes=["S0"], inc=(k == 15))
        return ps, pk

    def lru_chunk(s, cc, j):
        ps, pk = inproj_fm(s, cc, halo_col=j * HAL)
        xbs = TA[0]
        S.op(ACT, lambda e: e.activation(out=xbs[:, HAL:TW], in_=ps[:, 0:T], func=AF.Identity, bias=ZEROC, scale=1.0), reads=pk + ["CST"], writes=["TA0"])
        S.op(ACT, lambda e: e.activation(out=xbs[:, 0:HAL], in_=PSS[:, j * HAL:(j + 1) * HAL], func=AF.Identity, bias=ZEROC, scale=1.0),
             reads=["S0", "CST"], writes=["TA0"])
        acc = TT[0]
        S.op(ACT, lambda e: e.activation(out=acc[:], in_=xbs[:, HAL:TW], func=AF.Identity, bias=P(O_CB + j), scale=P(O_CW + j * 4 + 3)),
             reads=["TA0", "PRM"], writes=["T0"])
        for sft in (1, 2, 3):
            S.op(DVE, lambda e, sft=sft: e.scalar_tensor_tensor(out=acc[:], in0=xbs[:, HAL - sft:TW - sft], scalar=P(O_CW + j * 4 + 3 - sft),
                                                                in1=acc[:], op0=ALU.mult, op1=ALU.add),
                 reads=["TA0", "T0", "PRM"], writes=["T0"])
        xcb = SQ[0]
        S.op(ACT, lambda e: e.activation(out=xcb[:, 0:T], in_=acc[:], func=AF.Identity, bias=ZEROC, scale=1.0), reads=["T0", "CST"], writes=["SQ0"])
        pr, pkr = next_pair()
        pi, pki = next_pair()
        for (pp, kk, wb, wk) in ((pr, pkr, WRB, "MX14"), (pi, pki, WIB, "MX15")):
            for tb in range(2):
                S.op(PE, lambda e, pp=pp, wb=wb, tb=tb: e.matmul(pp[:, tb * 512:(tb + 1) * 512], lhsT=wb[:, j, :],
                                                                  rhs=xcb[:, tb * 512:(tb + 1) * 512], start=True, stop=True),
                     reads=[wk, "SQ0"], writes=[kk[tb]], inc=(tb == 1))
        thr = TT[1]
        thi = TT[2]
        S.op(ACT, lambda e: e.activation(out=thr[:], in_=pr[:, 0:T], func=AF.Tanh, bias=LP[:, 16 + j:17 + j], scale=0.5), reads=pkr + ["LP"], writes=["T1"])
        S.op(ACT, lambda e: e.activation(out=thi[:], in_=pi[:, 0:T], func=AF.Tanh, bias=LP[:, 24 + j:25 + j], scale=0.5), reads=pki + ["LP"], writes=TK[2])
        a_ap = XA[:, j, HAL:TW]
        b_ap = XA[:, 8 + j, HAL:TW]
        S.op(ACT, lambda e: e.activation(out=a_ap, in_=thr[:], func=AF.Exp, bias=LP[:, j:j + 1], scale=LP[:, j:j + 1]), reads=["T1", "LP"], writes=[xa_keys[j]])
        a2 = TT[3]
        S.op(ACT, lambda e: e.activation(out=a2[:], in_=thr[:], func=AF.Exp, bias=LP[:, 8 + j:9 + j], scale=LP[:, 8 + j:9 + j]), reads=["T1", "LP"], writes=TK[3])
        S.op(ACT, lambda e: e.activation(out=a2[:], in_=a2[:], func=AF.Sqrt, bias=QRTC, scale=-0.25), reads=TK[3] + ["CST"], writes=TK[3])
        S.op(DVE, lambda e: e.scalar_tensor_tensor(out=thi[:], in0=thi[:], scalar=1.0, in1=acc[:], op0=ALU.add, op1=ALU.mult),
             reads=TK[2] + ["T0"], writes=TK[2])
        S.op(DVE, lambda e: e.tensor_tensor(out=b_ap, in0=thi[:], in1=a2[:], op=ALU.mult), reads=TK[2] + TK[3], writes=[xa_keys[8 + j]])
        S.op(DVE, lambda e: e.tensor_tensor_scan(out=thr[:], data0=a_ap, data1=b_ap, initial=0.0, op0=ALU.mult, op1=ALU.add),
             reads=[xa_keys[j], xa_keys[8 + j], "T1"], writes=["T1"])
        S.op(DVE, lambda e: e.tensor_copy(out=HEND[:, j:j + 1], in_=thr[:, T - 1:T]), reads=["T1"], writes=["HEND"])

    def v_group(s, hb):
        for batch in range(2):
            banks = [(PSA, 0, "PA0"), (PSA, 1, "PA1"), (PSB, 0, "PB0"), (PSB, 1, "PB1")]
            for ci in range(4):
                c = batch * 4 + ci
                pp, half, key = banks[ci]
                for k in range(16):
                    S.op(PE, lambda e, pp=pp, half=half, k=k, c=c: e.matmul(
                        pp[:, half * 512:(half + 1) * 512], lhsT=HT[:, k, HAL + c * 128:HAL + (c + 1) * 128], rhs=W[:, s, k, :],
                        start=(k == 0), stop=(k == 15)),
                        reads=["W%d" % s, ht_keys[k]], writes=[key], inc=(k == 15))
            vgs = []
            for ci in range(4):
                pp, half, key = banks[ci]
                ta = TA[ci // 2]
                tk = "TA%d" % (ci // 2)
                vg = ta[:, (ci % 2) * 512:(ci % 2 + 1) * 512]
                S.op(ACT, lambda e, pp=pp, half=half, vg=vg: e.activation(out=vg, in_=pp[:, half * 512:(half + 1) * 512], func=AF.Gelu_apprx_tanh,
                                                                            bias=ZEROC, scale=1.0), reads=[key, "CST"], writes=[tk])
                vgs.append((vg, tk))
                for hh in range(4):
                    S.op(DVE, lambda e, vg=vg, ci=ci, hh=hh: e.bn_stats(out=STATS[:, ci * 4 + hh, :], in_=vg[:, hh * 128:(hh + 1) * 128]),
                         reads=[tk], writes=["STATS"])
                    S.op(DVE, lambda e, ci=ci, hh=hh: e.bn_aggr(out=MV[:, ci * 4 + hh, :], in_=STATS[:, ci * 4 + hh, :]),
                         reads=["STATS"], writes=["MV"])
            S.op(ACT, lambda e: e.activation(out=RS4[:], in_=MV[:, :, 1], func=AF.Sqrt, bias=EPSC, scale=1.0), reads=["MV", "CST"], writes=["RS4"])
            S.op(DVE, lambda e: e.reciprocal(out=RS4[:], in_=RS4[:]), reads=["RS4"], writes=["RS4"])
            for ci in range(4):
                c = batch * 4 + ci
                vg, tk = vgs[ci]
                vh = VH[ci % 2]
                vk = "VH%d" % (ci % 2)
                pc_o = (ci % 2) * 512
                pck = "PC%d" % (ci % 2)
                for hh in range(4):
                    S.op(DVE, lambda e, vg=vg, vh=vh, ci=ci, hh=hh: e.tensor_scalar(
                        out=vh[:, hh * 128:(hh + 1) * 128], in0=vg[:, hh * 128:(hh + 1) * 128],
                        scalar1=MV[:, ci * 4 + hh, 0:1], scalar2=RS4[:, ci * 4 + hh:ci * 4 + hh + 1], op0=ALU.subtract, op1=ALU.mult),
                        reads=[tk, "MV", "RS4"], writes=[vk])
                for hh in range(4):
                    S.op(PE, lambda e, vh=vh, hh=hh, pc_o=pc_o: e.matmul(PSC[:, pc_o + hh * 128:pc_o + (hh + 1) * 128], lhsT=vh[:, hh * 128:(hh + 1) * 128],
                                                               rhs=WSB[:, hb + hh, :], start=True, stop=True),
                         reads=[vk, "WSB"], writes=[pck], inc=(hh == 3))
                for hh in range(4):
                    S.op(DVE, lambda e, hh=hh, c=c, pc_o=pc_o: e.scalar_tensor_tensor(
                        out=TT[hh][:, c * 128:(c + 1) * 128], in0=PSC[:, pc_o + hh * 128:pc_o + (hh + 1) * 128], scalar=P(O_LNG + hb + hh),
                        in1=R[:, (hb + hh) * 128:(hb + hh + 1) * 128], op0=ALU.mult, op1=ALU.add),
                        reads=[pck, "PRM"] + RK, writes=TK[hh])

    sumsq_state = {"n": 0}

    def branch_tail(y_ap, ykey, idx, scale_col, mixk):
        sq = SQ[idx % 2]
        sk = "SQ%d" % (idx % 2)
        S.op(ACT, lambda e: e.activation(out=sq[:, 0:T], in_=y_ap, func=AF.Square, bias=ZEROC, scale=1.0), reads=[ykey, "CST"], writes=[sk])
        n = sumsq_state["n"]
        for tb in range(2):
            S.op(PE, lambda e, tb=tb, n=n: e.matmul(PSS[:, tb * 512:(tb + 1) * 512], lhsT=ONES[:], rhs=sq[:, tb * 512:(tb + 1) * 512],
                                                    start=(n == 0), stop=(n == 7)),
                 reads=["ONES", sk], writes=["S%d" % tb], inc=(tb == 1))
        sumsq_state["n"] = (n + 1) % 8
        S.op(ACT, lambda e: e.activation(out=MIX[:, mixk, :], in_=y_ap, func=AF.Identity, bias=ZEROC, scale=P(scale_col)),
             reads=[ykey, "PRM", "CST"], writes=[mx_keys[mixk]])

    def branch_finish(k0):
        S.op(ACT, lambda e: e.activation(out=RSTDM[:], in_=PSS[:, 0:T], func=AF.Sqrt, bias=EPSC, scale=1.0 / 1024.0),
             reads=["S0", "S1", "CST"], writes=["T1"])
        S.op(DVE, lambda e: e.reciprocal(out=RSTDM[:], in_=RSTDM[:]), reads=["T1"], writes=["T1"])
        for k in range(k0, k0 + 8):
            S.op(DVE, lambda e, k=k: e.tensor_tensor(out=MIX[:, k, :], in0=MIX[:, k, :], in1=RSTDM[:], op=ALU.mult),
                 reads=[mx_keys[k], "T1"], writes=[mx_keys[k]])

    def u_group(s, hb):
        for cc in range(4):
            h = hb + cc
            ps, pk = inproj_fm(s, cc)
            gu = TA[cc % 2]
            gk = "TA%d" % (cc % 2)
            S.op(ACT, lambda e, ps=ps, gu=gu: e.activation(out=gu[:, 0:T], in_=ps[:, 0:T], func=AF.Gelu_apprx_tanh, bias=ZEROC, scale=1.0),
                 reads=pk + ["CST"], writes=[gk])
            S.op(DVE, lambda e, gu=gu, cc=cc: e.tensor_tensor(out=gu[:, 0:T], in0=gu[:, 0:T], in1=TT[cc][:], op=ALU.mult),
                 reads=[gk] + TK[cc], writes=[gk])
            branch_tail(gu[:, 0:T], gk, cc, O_GA + h, h)

    def gb_group(s, jb):
        for cc in range(4):
            j = jb + cc
            ps, pk = inproj_fm(s, cc)
            hbuf = TT[cc % 2]
            hk = "T%d" % (cc % 2)
            S.op(DVE, lambda e, hbuf=hbuf, j=j: e.tensor_tensor_scan(out=hbuf[:], data0=XA[:, j, HAL:TW], data1=XA[:, 8 + j, HAL:TW],
                                                                      initial=INIT[:, j:j + 1], op0=ALU.mult, op1=ALU.add),
                 reads=[xa_keys[j], xa_keys[8 + j], "INIT"], writes=[hk])
            gg = TA[cc % 2]
            gk = "TA%d" % (cc % 2)
            S.op(ACT, lambda e, ps=ps, gg=gg: e.activation(out=gg[:, 0:T], in_=ps[:, 0:T], func=AF.Gelu_apprx_tanh, bias=ZEROC, scale=1.0),
                 reads=pk + ["CST"], writes=[gk])
            S.op(DVE, lambda e, gg=gg, hbuf=hbuf: e.tensor_tensor(out=gg[:, 0:T], in0=gg[:, 0:T], in1=hbuf[:], op=ALU.mult),
                 reads=[gk, hk], writes=[gk])
            branch_tail(gg[:, 0:T], gk, cc, O_GL + j, 8 + j)

    for gi in range(2):
        s = w_next()
        for cc in range(4):
            lru_chunk(s, cc, gi * 4 + cc)
    s = w_next(); v_group(s, 0)
    s = w_next(); u_group(s, 0)
    S.dma(SP, lambda e: e.dma_start(out=ag1_in, in_=HEND[:]), dbo, reads=["HEND"], writes=["ag1_in"])
    S.dma(POOL, lambda e: e.collective_compute("AllGather", ALU.bypass, replica_groups=[list(range(8))],
                                               ins=[ag1_in], outs=[ag1_out]), dcc[1], reads=["ag1_in"], writes=["ag1_out"])
    S.dma(SP, lambda e: e.dma_start(out=AGT1[:], in_=ag1_out.rearrange("(r p) c -> p r c", p=128)), dbi, reads=["ag1_out"], writes=["AGT1"])
    s = w_next(); v_group(s, 4)
    s = w_next(); u_group(s, 4)
    branch_finish(0)
    S.op(DVE, lambda e: e.tensor_scalar(out=INIT[:], in0=AGT1[:, 0, :], scalar1=P(O_SEL), scalar2=None, op0=ALU.mult),
         reads=["AGT1", "PRM"], writes=["INIT"])
    for r in range(1, 8):
        S.op(DVE, lambda e, r=r: e.scalar_tensor_tensor(out=INIT[:], in0=AGT1[:, r, :], scalar=P(O_SEL + r), in1=INIT[:],
                                                        op0=ALU.mult, op1=ALU.add), reads=["AGT1", "INIT", "PRM"], writes=["INIT"])
    s = w_next(); gb_group(s, 0)
    s = w_next(); gb_group(s, 4)
    branch_finish(8)
    for q in range(4):
        S.dma(SP, lambda e, q=q: e.dma_start(out=XA[:, 4 * q:4 * q + 4, :],
                                             in_=xT[512 * q:512 * (q + 1), :].rearrange("(k p) t -> p k t", p=128)),
              dxr, writes=xa_keys[4 * q:4 * q + 4])
    tok_xr = (dxr, dxr.cnt)
    for k in xa_keys:
        S.last_w[k] = tok_xr

    def proj_down(s, mm, m, gate_col, nk):
        ps, pk = next_pair()
        for k in range(nk):
            for tb in range(2):
                S.op(PE, lambda e, k=k, tb=tb, ps=ps: e.matmul(ps[:, tb * 512:(tb + 1) * 512], lhsT=W[:, s, k, mm * 128:(mm + 1) * 128],
                                                               rhs=MIX[:, k, tb * 512:(tb + 1) * 512], start=(k == 0), stop=(k == nk - 1)),
                     reads=["W%d" % s, mx_keys[k]], writes=[pk[tb]], inc=(k == nk - 1 and tb == 1))
        S.op(DVE, lambda e, ps=ps: e.scalar_tensor_tensor(out=XA[:, m, HAL:TW], in0=ps[:, 0:T], scalar=MOD[:, gate_col + m:gate_col + m + 1],
                                                          in1=XA[:, m, HAL:TW], op0=ALU.mult, op1=ALU.add),
             reads=pk + ["MOD", xa_keys[m]], writes=[xa_keys[m]])

    for g in range(4):
        s = w_next()
        for mm in range(4):
            m = g * 4 + mm
            proj_down(s, mm, m, 32, 16)
            if m >= 2:
                rms_mm(m - 2, *sqbuf2(m - 2))
            rms_square(m, *sqbuf2(m))
    rms_mm(14, *sqbuf2(14))
    rms_mm(15, *sqbuf2(15))

    S.op(DVE, lambda e: e.tensor_copy(out=X1H[:].rearrange("p (k t) -> p k t", t=2), in_=XA[:, :, TW - 2:TW]), reads=xa_keys, writes=["X1H"])
    S.dma(SP, lambda e: e.dma_start(out=ag2_in, in_=X1H[:]), dbo, reads=["X1H"], writes=["ag2_in"])
    S.dma(POOL, lambda e: e.collective_compute("AllGather", ALU.bypass, replica_groups=[list(range(8))],
                                               ins=[ag2_in], outs=[ag2_out]), dcc[2], reads=["ag2_in"], writes=["ag2_out"])
    S.dma(SP, lambda e: e.dma_start(out=AGT2[:], in_=ag2_out.rearrange("(r p) c -> p r c", p=128)), dbi, reads=["ag2_out"], writes=["T0"])

    rms_finish(D)
    T5 = MIX[:, 14:16, :].rearrange("p a b -> p (a b)").bitcast(F32)
    n2bufs = [(TA[0][:, HAL:TW], ["TA0"]), (TA[1][:, HAL:TW], ["TA1"]), (TT[2][:], TK[2]), (TT[3][:], TK[3]), (R[:], RK), (T5[:], ["MX14", "MX15"])]
    for k in range(16):
        tb_, tk_ = n2bufs[k % 6]
        S.op(DVE, lambda e, k=k, tb_=tb_: e.scalar_tensor_tensor(out=tb_, in0=XA[:, k, HAL:TW], scalar=A2[:, k:k + 1], in1=RSTDM[:],
                                                                 op0=ALU.mult, op1=ALU.mult),
             reads=[xa_keys[k], "A2", "T1"], writes=tk_)
        S.op(ACT, lambda e, k=k, tb_=tb_: e.activation(out=HT[:, k, HAL:TW], in_=tb_, func=AF.Identity, bias=MOD[:, 48 + k:49 + k], scale=1.0),
             reads=tk_ + ["MOD"], writes=[ht_keys[k]])
    S.op(DVE, lambda e: e.tensor_scalar(out=XH[:], in0=AGT2[:, 0, :], scalar1=P(O_SEL), scalar2=None, op0=ALU.mult), reads=["T0", "PRM"], writes=["XH"])
    for r in range(1, 8):
        S.op(DVE, lambda e, r=r: e.scalar_tensor_tensor(out=XH[:], in0=AGT2[:, r, :], scalar=P(O_SEL + r), in1=XH[:], op0=ALU.mult, op1=ALU.add),
             reads=["T0", "XH", "PRM"], writes=["XH"])
    S.op(ACT, lambda e: e.activation(out=SQH2[:], in_=XH[:], func=AF.Square, bias=ZEROC, scale=1.0), reads=["XH", "CST"], writes=["SQH2"])
    for k in range(16):
        S.op(PE, lambda e, k=k: e.matmul(PSC[:, 0:2], lhsT=ONES[:], rhs=SQH2[:, k * 2:(k + 1) * 2], start=(k == 0), stop=(k == 15)),
             reads=["ONES", "SQH2"], writes=["PC0"], inc=(k == 15))
    S.op(ACT, lambda e: e.activation(out=RSTDH[:, 0:2], in_=PSC[:, 0:2], func=AF.Sqrt, bias=EPSC, scale=1.0 / D), reads=["PC0", "CST"], writes=["RSTDH"])
    S.op(DVE, lambda e: e.reciprocal(out=RSTDH[:, 0:2], in_=RSTDH[:, 0:2]), reads=["RSTDH"], writes=["RSTDH"])
    for k in range(16):
        S.op(DVE, lambda e, k=k: e.scalar_tensor_tensor(out=TMPH[:, k * 2:(k + 1) * 2], in0=XH[:, k * 2:(k + 1) * 2], scalar=A2[:, k:k + 1],
                                                        in1=RSTDH[:, 0:2], op0=ALU.mult, op1=ALU.mult),
             reads=["XH", "A2", "RSTDH"], writes=["TMPH"])
        S.op(DVE, lambda e, k=k: e.tensor_scalar(out=HT[:, k, 2:HAL], in0=TMPH[:, k * 2:(k + 1) * 2], scalar1=MOD[:, 48 + k:49 + k],
                                                 scalar2=P(O_FLAG), op0=ALU.add, op1=ALU.mult),
             reads=["TMPH", "MOD", "PRM"], writes=["HTh"])

    ffn_i = [0]
    for H in range(3):
        for q in range(4):
            sg = w_next()
            sv = w_next(2)
            for cc in range(4):
                j = H * 16 + q * 4 + cc
                jj = q * 4 + cc
                i = ffn_i[0]
                ffn_i[0] += 1
                pg, pkg = next_pair()
                for k in range(16):
                    for tb in range(2):
                        S.op(PE, lambda e, k=k, tb=tb, pg=pg, cc=cc, sg=sg: e.matmul(
                            pg[:, tb * 512:(tb + 1) * 512], lhsT=W[:, sg, k, cc * 128:(cc + 1) * 128],
                            rhs=HT[:, k, HAL + tb * 512:HAL + (tb + 1) * 512], start=(k == 0), stop=(k == 15)),
                            reads=["W%d" % sg, ht_keys[k]], writes=[pkg[tb]], inc=False)
                    S.op(PE, lambda e, k=k, cc=cc, sg=sg, j=j: e.matmul(
                        PSS[:, j * 2:j * 2 + 2], lhsT=W[:, sg, k, cc * 128:(cc + 1) * 128], rhs=HT[:, k, 2:HAL],
                        start=(k == 0), stop=(k == 15)),
                        reads=["W%d" % sg, "HTh"], writes=["S0"], inc=(k == 15))
                pv, pkv = next_pair()
                for k in range(16):
                    for tb in range(2):
                        S.op(PE, lambda e, k=k, tb=tb, pv=pv, cc=cc, sv=sv: e.matmul(
                            pv[:, tb * 512:(tb + 1) * 512], lhsT=W[:, sv, k, cc * 128:(cc + 1) * 128],
                            rhs=HT[:, k, HAL + tb * 512:HAL + (tb + 1) * 512], start=(k == 0), stop=(k == 15)),
                            reads=["W%d" % sv, ht_keys[k]], writes=[pkv[tb]], inc=(k == 15 and tb == 1))
                gs = TA[i % 2]
                gk = "TA%d" % (i % 2)
                acc = TT[i % 2]
                ak = "T%d" % (i % 2)
                S.op(ACT, lambda e, gs=gs, pg=pg: e.activation(out=gs[:, 2:2 + T], in_=pg[:, 0:T], func=AF.Identity, bias=ZEROC, scale=1.0),
                     reads=pkg + ["CST"], writes=[gk])
                S.op(ACT, lambda e, gs=gs, j=j: e.activation(out=gs[:, 0:2], in_=PSS[:, j * 2:j * 2 + 2], func=AF.Identity, bias=ZEROC, scale=1.0),
                     reads=["S0", "CST"], writes=[gk])
                S.op(ACT, lambda e, gs=gs, acc=acc, j=j: e.activation(out=acc[:], in_=gs[:, 2:2 + T], func=AF.Identity, bias=P(O_FCB + j),
                                                                      scale=P(O_FCW + j * 3 + 2)), reads=[gk, "PRM"], writes=[ak])
                for sft in (1, 2):
                    S.op(DVE, lambda e, gs=gs, acc=acc, j=j, sft=sft: e.scalar_tensor_tensor(
                        out=acc[:], in0=gs[:, 2 - sft:2 - sft + T], scalar=P(O_FCW + j * 3 + 2 - sft), in1=acc[:], op0=ALU.mult, op1=ALU.add),
                        reads=[gk, ak, "PRM"], writes=[ak])
                S.op(ACT, lambda e, acc=acc: e.activation(out=acc[:], in_=acc[:], func=AF.Gelu_apprx_tanh, bias=ZEROC, scale=1.0),
                     reads=[ak, "CST"], writes=[ak])
                S.op(DVE, lambda e, acc=acc, pv=pv, jj=jj: e.tensor_tensor(out=MIX[:, jj, :], in0=pv[:, 0:T], in1=acc[:], op=ALU.mult),
                     reads=[ak] + pkv, writes=[mx_keys[jj]])
        for gq in range(4):
            s = w_next()
            for mm in range(4):
                m = gq * 4 + mm
                proj_down(s, mm, m, 80, 16)
                if H == 2:
                    if m >= 2:
                        rms_mm(m - 2, *sqbuf2(m - 2))
                    rms_square(m, *sqbuf2(m))

    rms_mm(14, *sqbuf2(14))
    rms_mm(15, *sqbuf2(15))
    rms_finish(D)
    for k in range(16):
        S.op(DVE, lambda e, k=k: e.scalar_tensor_tensor(out=XA[:, k, HAL:TW], in0=XA[:, k, HAL:TW], scalar=P(O_NF + k), in1=RSTDM[:],
                                                        op0=ALU.mult, op1=ALU.mult),
             reads=[xa_keys[k], "PRM", "T1"], writes=[xa_keys[k]])
        S.dma(SP, lambda e, k=k: e.dma_start(out=yT[k * 128:(k + 1) * 128, :], in_=XA[:, k, HAL:TW]), dout, reads=[xa_keys[k]])
    S.wait(SP, (dout, dout.cnt))
    SP.prog.append(("o", lambda e: e.nop(), False))

    with S.alloc(nc):
        with nc.Block() as block:
            S.emit(block)
    es.close()
    return nc


def _pk(v, n):
    return np.ascontiguousarray(np.asarray(v, np.float32).reshape(n, 128).T)


def prep_inputs(inp):
    f = lambda a: np.asarray(a, np.float32)
    x = f(inp["x"]); c = f(inp["c"])
    w_ada = f(inp["w_ada"])[0]
    shared = {
        "w_in": np.ascontiguousarray(f(inp["w_in"])[0]),
        "w_out": np.ascontiguousarray(f(inp["w_out"])[0]),
        "w_up": np.ascontiguousarray(f(inp["w_up"])[0]),
        "w_down": np.ascontiguousarray(f(inp["w_down"])[0]),
    }
    cT = np.ascontiguousarray(c.T.reshape(16, 128, 4).transpose(1, 0, 2).reshape(128, 64))
    w_s = f(inp["gmlp_w_s"])[0]
    wsT = np.ascontiguousarray(w_s.transpose(2, 0, 1).reshape(128, 1024))
    tri = np.triu(np.ones((128, 128), np.float32))
    bs_bc = np.ascontiguousarray(np.broadcast_to(f(inp["gmlp_b_s"])[0].reshape(1, 1024), (128, 1024)))

    def bd(w):
        w = f(w)[0]
        o = np.zeros((128, 8, 128), np.float32)
        for j in range(8):
            o[0:64, j, 0:64] = w[2 * j]
            o[64:128, j, 64:128] = w[2 * j + 1]
        return o.reshape(128, 1024)

    wr_bd = bd(inp["lru_w_r"]); wi_bd = bd(inp["lru_w_i"])
    base = np.zeros((128, NPRM), np.float32)
    base[:, O_BADA:O_BADA + 96] = _pk(f(inp["b_ada"])[0], 96)
    base[:, O_N1:O_N1 + 16] = _pk(f(inp["norm1"])[0], 16)
    base[:, O_N2:O_N2 + 16] = _pk(f(inp["norm2"])[0], 16)
    base[:, O_NF:O_NF + 16] = _pk(f(inp["norm_final"]), 16)
    base[:, O_LNG:O_LNG + 8] = _pk(f(inp["gmlp_ln_g"])[0], 8)
    base[:, O_LNB:O_LNB + 8] = _pk(f(inp["gmlp_ln_b"])[0], 8)
    base[:, O_CW:O_CW + 32] = f(inp["lru_conv_w"])[0].reshape(4, 8, 128).transpose(2, 1, 0).reshape(128, 32)
    base[:, O_CB:O_CB + 8] = _pk(f(inp["lru_conv_b"])[0], 8)
    base[:, O_BR:O_BR + 8] = _pk(f(inp["lru_b_r"])[0], 8)
    base[:, O_BI:O_BI + 8] = _pk(f(inp["lru_b_i"])[0], 8)
    base[:, O_LAM:O_LAM + 8] = _pk(f(inp["lru_lambda"])[0], 8)
    base[:, O_GL:O_GL + 8] = _pk(f(inp["out_norm_lru"])[0], 8)
    base[:, O_GA:O_GA + 8] = _pk(f(inp["out_norm_gmlp"])[0], 8)
    base[:, O_FCW:O_FCW + 144] = f(inp["ffn_conv_w"])[0].reshape(3, 48, 128).transpose(2, 1, 0).reshape(128, 144)
    base[:, O_FCB:O_FCB + 48] = _pk(f(inp["ffn_conv_b"])[0], 48)
    maps = []
    for core in range(8):
        b, half = core // 2, core % 2
        t0 = half * T
        xt = np.zeros((D, TW), np.float32)
        xt[:, HAL:] = x[b, t0:t0 + T, :].T
        if half == 1:
            xt[:, 0:HAL] = x[b, t0 - HAL:t0, :].T
        p = base.copy()
        if half == 1:
            p[:, O_SEL + core - 1] = 1.0
            p[:, O_FLAG] = 1.0
        p[:, O_SELB + b] = 1.0
        m = dict(shared)
        m.update({
            "xT": xt, "cT": cT, "w_ada_r": np.ascontiguousarray(w_ada[:, core * 1536:(core + 1) * 1536]),
            "prm": p, "wsT": wsT, "tri": tri, "bs_bc": bs_bc, "wr_bd": wr_bd, "wi_bd": wi_bd,
        })
        maps.append(m)
    return maps


_NC = None


def kernel(**inputs):
    global _NC
    maps = prep_inputs(inputs)
    if _NC is None:
        _NC = build_program()
    res = run_bass_kernel_spmd(_NC, maps, core_ids=list(range(8)))
    out = np.empty((4, 2048, D), np.float32)
    for core in range(8):
        b, half = core // 2, core % 2
        out[b, half * T:(half + 1) * T, :] = np.asarray(res.results[core]["yT"], np.float32).T
    return out
```

```python
import contextlib
import numpy as np
import concourse.bass as bass
import concourse.mybir as mybir
from concourse.bass_utils import run_bass_kernel_spmd

F32 = mybir.dt.float32
BF16 = mybir.dt.bfloat16
AF = mybir.ActivationFunctionType
ALU = mybir.AluOpType

D = 2048
T = 1024
HAL = 4
TW = T + HAL
EPS = 1e-6
NPRM = 448
O_BADA, O_N1, O_N2, O_NF, O_LNG, O_LNB, O_CW, O_CB, O_BR, O_BI, O_LAM, O_GL, O_GA, O_FCW, O_FCB, O_SEL, O_FLAG, O_SELB = (
    0, 96, 112, 128, 144, 152, 160, 192, 200, 208, 216, 224, 232, 240, 384, 432, 440, 441)


class DSem:
    def __init__(self, name, step=16):
        self.name = name
        self.sem = None
        self.cnt = 0
        self.step = step


class Eng:
    def __init__(self, name, self_sync=True):
        self.name = name
        self.sem = None
        self.cnt = 0
        self.prog = []
        self.waited = {}
        self.self_sync = self_sync

    def _wait(self, tok):
        src, val = tok
        if src is self and not self.self_sync:
            return
        if self.waited.get(id(src), 0) >= val:
            return
        self.waited[id(src)] = val
        self.prog.append(("w", src, val))


class Sched:
    def __init__(self):
        self.engs = {}
        self.dsems = []
        self.last_w = {}
        self.readers = {}

    def eng(self, name, self_sync=True):
        e = Eng(name, self_sync)
        self.engs[name] = e
        return e

    def dsem(self, name, step=16):
        d = DSem(name, step)
        self.dsems.append(d)
        return d

    def _deps(self, e, reads, writes, extra):
        for k in reads:
            t = self.last_w.get(k)
            if t is not None:
                e._wait(t)
        for k in writes:
            t = self.last_w.get(k)
            if t is not None:
                e._wait(t)
            for t in self.readers.get(k, ()):
                e._wait(t)
        for t in extra:
            if t is not None:
                e._wait(t)

    def _commit(self, tok, reads, writes):
        for k in reads:
            self.readers.setdefault(k, []).append(tok)
        for k in writes:
            self.last_w[k] = tok
            self.readers[k] = []

    def op(self, e, fn, reads=(), writes=(), extra=(), inc=True):
        self._deps(e, reads, writes, extra)
        if inc:
            e.cnt += 1
            tok = (e, e.cnt)
            e.prog.append(("o", fn, True))
        else:
            tok = (e, e.cnt + 1)
            e.prog.append(("o", fn, False))
        self._commit(tok, reads, writes)
        return tok

    def dma(self, e, fn, ds, reads=(), writes=(), extra=()):
        self._deps(e, reads, writes, extra)
        ds.cnt += ds.step
        tok = (ds, ds.cnt)
        e.prog.append(("d", fn, ds))
        self._commit(tok, reads, writes)
        return tok

    def wait(self, e, tok):
        e._wait(tok)

    def fix_pending(self):
        pass

    def alloc(self, nc):
        stack = contextlib.ExitStack()
        for e in self.engs.values():
            e.sem = stack.enter_context(nc.semaphore("s_" + e.name))
        for d in self.dsems:
            d.sem = stack.enter_context(nc.semaphore("d_" + d.name))
        return stack

    def emit(self, block):
        def runner(e):
            def run(engine):
                for item in e.prog:
                    if item[0] == "w":
                        engine.wait_ge(item[1].sem, item[2])
                    elif item[0] == "o":
                        ins = item[1](engine)
                        if item[2]:
                            ins.then_inc(e.sem, 1)
                    else:
                        ins = item[1](engine)
                        ins.then_inc(item[2].sem, item[2].step)
            return run

        for name, e in self.engs.items():
            getattr(block, name)(runner(e))


def build_program():
    nc = bass.Bass("TRN2", target_bir_lowering=False)

    def din(name, shape):
        return nc.dram_tensor(name, list(shape), F32, kind="ExternalInput").ap()

    xT = din("xT", [D, TW])
    cT = din("cT", [128, 64])
    w_ada = din("w_ada_r", [D, 1536])
    prm = din("prm", [128, NPRM])
    wsT_d = din("wsT", [128, 1024])
    tri_d = din("tri", [128, 128])
    bs_d = din("bs_bc", [128, 1024])
    wr_d = din("wr_bd", [128, 1024])
    wi_d = din("wi_bd", [128, 1024])
    w_in = din("w_in", [D, 4096])
    w_out = din("w_out", [D, D])
    w_up = din("w_up", [D, 12288])
    w_down = din("w_down", [6144, D])
    yT = nc.dram_tensor("yT", [D, T], F32, kind="ExternalOutput").ap()
    ag0_in = nc.dram_tensor("ag0_in", [128, 48], F32).ap()
    ag0_out = nc.dram_tensor("ag0_out", [1024, 48], F32).ap()
    ag1_in = nc.dram_tensor("ag1_in", [128, 8], F32).ap()
    ag1_out = nc.dram_tensor("ag1_out", [1024, 8], F32).ap()
    ag2_in = nc.dram_tensor("ag2_in", [128, 32], F32).ap()
    ag2_out = nc.dram_tensor("ag2_out", [1024, 32], F32).ap()

    S = Sched()
    PE = S.eng("tensor", self_sync=False)
    ACT = S.eng("scalar")
    DVE = S.eng("vector")
    POOL = S.eng("gpsimd")
    SP = S.eng("sync")
    dx = S.dsem("x")
    dp = S.dsem("prm")
    dwg = S.dsem("wg")
    dws = [S.dsem("w%d" % i) for i in range(3)]
    dbo = S.dsem("bo")
    dbi = S.dsem("bi")
    dxr = S.dsem("xr")
    dout = S.dsem("out")
    dcc = [S.dsem("cc%d" % i, step=1) for i in range(3)]

    es = contextlib.ExitStack()

    def sb(name, shape, dt=F32):
        return es.enter_context(nc.sbuf_tensor(name, list(shape), dt))

    def pst(name):
        return es.enter_context(nc.psum_tensor(name, [128, 1024], F32))

    XA = sb("XA", [128, 16, TW])
    HT = sb("HT", [128, 16, TW], BF16)
    MIX = sb("MIX", [128, 16, T], BF16)
    W = sb("W", [128, 3, 16, 512], BF16)
    SQ = [sb("SQ%d" % i, [128, T], BF16) for i in range(2)]
    TA = [sb("TA%d" % i, [128, TW]) for i in range(2)]
    TT01 = [sb("T%d" % i, [128, T]) for i in range(2)]
    TT = TT01 + [MIX[:, 8:10, :].rearrange("p a b -> p (a b)").bitcast(F32),
                 MIX[:, 10:12, :].rearrange("p a b -> p (a b)").bitcast(F32)]
    TK = [["T0"], ["T1"], ["MX8", "MX9"], ["MX10", "MX11"]]
    R = MIX[:, 12:14, :].rearrange("p a b -> p (a b)").bitcast(F32)
    RK = ["MX12", "MX13"]
    WRB = MIX[:, 14, :].rearrange("p (j q) -> p j q", q=128)
    WIB = MIX[:, 15, :].rearrange("p (j q) -> p j q", q=128)
    RSTDM = TT01[1]
    WSB = sb("WSB", [128, 8, 128], BF16)
    PRM = sb("PRM", [128, NPRM])
    CT = sb("CT", [128, 64])
    CB = sb("CB", [128, 64], BF16)
    ONES = sb("ONES", [128, 128], BF16)
    MODP = sb("MODP", [128, 48])
    MOD = sb("MOD", [128, 96])
    A1 = sb("A1", [128, 16])
    A2 = sb("A2", [128, 16])
    LP = sb("LP", [128, 64])
    CST = sb("CST", [128, 4])
    SQH = sb("SQH", [128, 16, 4], BF16)
    STATS2 = [sb("STATS%d" % i, [128, 16, 6]) for i in range(2)]
    MV2 = [sb("MV%d" % i, [128, 16, 2]) for i in range(2)]
    RS42 = [sb("RS4%d" % i, [128, 16]) for i in range(2)]
    VH = [sb("VH%d" % i, [128, 512], BF16) for i in range(2)]
    HEND = sb("HEND", [128, 8])
    AGT1 = sb("AGT1", [128, 8, 8])
    INIT = sb("INIT", [128, 8])
    X1H = sb("X1H", [128, 32])
    XH = sb("XH", [128, 32])
    SQH2 = sb("SQH2", [128, 32], BF16)
    RSTDH = sb("RSTDH", [128, 4])
    TMPH = sb("TMPH", [128, 32])
    PSA = pst("PSA")
    PSB = pst("PSB")
    PSC = pst("PSC")
    PSS = pst("PSS")
    TRI = TT01[0][:, 0:128]
    MODG = sb("MODG", [128, 96, 4])
    HTMP = sb("HTMP", [128, 16, 4])
    AGT2 = TT01[0][:, 0:256].rearrange("p (a b) -> p a b", b=32)
    pairs = [(PSA, ["PA0", "PA1"]), (PSB, ["PB0", "PB1"]), (PSC, ["PC0", "PC1"])]
    pair_i = [0]

    def next_pair():
        p = pairs[pair_i[0] % 3]
        pair_i[0] += 1
        return p

    def P(o, n=1):
        return PRM[:, o:o + n]

    xa_keys = ["XA%d" % k for k in range(16)]
    ht_keys = ["HT%d" % k for k in range(16)]
    mx_keys = ["MX%d" % k for k in range(16)]

    wq = []
    for g in range(3):
        wq.append(w_ada[:, g * 512:(g + 1) * 512])
    IN_ORDER = [4, 5, 2, 0, 3, 1, 6, 7]
    for g in IN_ORDER:
        wq.append(w_in[:, g * 512:(g + 1) * 512])
    for g in range(4):
        wq.append(w_out[:, g * 512:(g + 1) * 512])
    for H in range(3):
        for q in range(4):
            j0 = H * 16 + q * 4
            wq.append(w_up[:, j0 * 128:j0 * 128 + 512])
            wq.append(w_up[:, 6144 + j0 * 128:6144 + j0 * 128 + 512])
        for gq in range(4):
            wq.append(w_down[H * 2048:(H + 1) * 2048, gq * 512:(gq + 1) * 512])
    w_issued = [0]
    w_used = [0]

    def w_issue_upto(n):
        while w_issued[0] < min(n, len(wq)):
            i = w_issued[0]
            s = i % 3
            src = wq[i].rearrange("(k p) e -> p k e", p=128)
            S.dma(POOL, lambda e, s=s, src=src: e.dma_start(out=W[:, s], in_=src), dws[s], writes=["W%d" % s])
            w_issued[0] += 1

    def w_next(ahead=3):
        i = w_used[0]
        w_used[0] += 1
        w_issue_upto(i + ahead)
        return i % 3

    for q in range(4):
        S.dma(SP, lambda e, q=q: e.dma_start(out=XA[:, 4 * q:4 * q + 4, :],
                                             in_=xT[512 * q:512 * (q + 1), :].rearrange("(k p) t -> p k t", p=128)),
              dx, writes=xa_keys[4 * q:4 * q + 4])
    S.dma(SP, lambda e: e.dma_start(out=PRM[:], in_=prm), dp, writes=["PRM"])
    S.dma(SP, lambda e: e.dma_start(out=CT[:], in_=cT), dp, writes=["CT"])
    S.dma(SP, lambda e: e.dma_start(out=TRI[:], in_=tri_d), dp, writes=["T0"])
    S.dma(SP, lambda e: e.dma_start(out=TA[1][:, 0:1024], in_=wsT_d), dp, writes=["TA1"])
    S.dma(SP, lambda e: e.dma_start(out=TA[0][:, 0:1024], in_=bs_d), dp, writes=["TA0"])
    tok_p = (dp, dp.cnt)
    for k in ["PRM", "CT", "T0", "TA1", "TA0"]:
        S.last_w[k] = tok_p
    tok_x = (dx, dx.cnt)
    for k in xa_keys:
        S.last_w[k] = tok_x
    w_issue_upto(3)
    S.dma(POOL, lambda e: e.dma_start(out=MIX[:, 14, :], in_=wr_d), dwg, writes=["MX14"])
    S.dma(POOL, lambda e: e.dma_start(out=MIX[:, 15, :], in_=wi_d), dwg, writes=["MX15"])
    tok_wg = (dwg, dwg.cnt)
    S.last_w["MX14"] = tok_wg
    S.last_w["MX15"] = tok_wg

    S.op(DVE, lambda e: e.memset(ONES[:], 1.0), writes=["ONES"])
    S.op(DVE, lambda e: e.memset(CST[:, 0:1], EPS), writes=["CST"])
    S.op(DVE, lambda e: e.memset(CST[:, 1:2], 1.0), writes=["CST"])
    S.op(DVE, lambda e: e.memset(CST[:, 2:3], 0.25), writes=["CST"])
    S.op(DVE, lambda e: e.memset(CST[:, 3:4], 0.0), writes=["CST"])
    EPSC = CST[:, 0:1]
    ONEC = CST[:, 1:2]
    QRTC = CST[:, 2:3]
    ZEROC = CST[:, 3:4]

    S.op(ACT, lambda e: e.activation(out=LP[:, 32:40], in_=P(O_LAM, 8), func=AF.Abs, bias=ZEROC, scale=1.0), reads=["PRM", "CST"], writes=["LP"])
    S.op(ACT, lambda e: e.activation(out=LP[:, 32:40], in_=LP[:, 32:40], func=AF.Exp, bias=ZEROC, scale=-1.0), reads=["LP", "CST"], writes=["LP"])
    S.op(ACT, lambda e: e.activation(out=LP[:, 32:40], in_=LP[:, 32:40], func=AF.Ln, bias=ONEC, scale=1.0), reads=["LP", "CST"], writes=["LP"])
    S.op(DVE, lambda e: e.tensor_scalar(out=LP[:, 40:48], in0=P(O_LAM, 8), scalar1=-1.0, scalar2=0.0, op0=ALU.mult, op1=ALU.max),
         reads=["PRM", "LP"], writes=["LP"])
    S.op(DVE, lambda e: e.tensor_tensor(out=LP[:, 40:48], in0=LP[:, 40:48], in1=LP[:, 32:40], op=ALU.add), reads=["LP"], writes=["LP"])
    S.op(DVE, lambda e: e.tensor_scalar(out=LP[:, 0:8], in0=LP[:, 40:48], scalar1=-4.0, scalar2=None, op0=ALU.mult), reads=["LP"], writes=["LP"])
    S.op(DVE, lambda e: e.tensor_scalar(out=LP[:, 8:16], in0=LP[:, 40:48], scalar1=-8.0, scalar2=None, op0=ALU.mult), reads=["LP"], writes=["LP"])
    S.op(DVE, lambda e: e.tensor_scalar(out=LP[:, 16:24], in0=P(O_BR, 8), scalar1=0.5, scalar2=None, op0=ALU.mult), reads=["PRM", "LP"], writes=["LP"])
    S.op(DVE, lambda e: e.tensor_scalar(out=LP[:, 24:32], in0=P(O_BI, 8), scalar1=0.5, scalar2=None, op0=ALU.mult), reads=["PRM", "LP"], writes=["LP"])
    for h in range(8):
        S.op(DVE, lambda e, h=h: e.tensor_tensor(out=WSB[:, h, :], in0=TA[1][:, h * 128:(h + 1) * 128], in1=TRI, op=ALU.mult),
             reads=["TA1", "T0"], writes=["WSB"])
    for h in range(8):
        S.op(PE, lambda e, h=h: e.matmul(PSB[:, h * 128:(h + 1) * 128], lhsT=ONES[:], rhs=WSB[:, h, :], start=True, stop=True),
             reads=["ONES", "WSB"], writes=["PB0", "PB1"], inc=(h == 7))
    for h in range(8):
        S.op(DVE, lambda e, h=h: e.scalar_tensor_tensor(out=R[:, h * 128:(h + 1) * 128], in0=PSB[:, h * 128:(h + 1) * 128],
                                                        scalar=P(O_LNB + h), in1=TA[0][:, h * 128:(h + 1) * 128],
                                                        op0=ALU.mult, op1=ALU.add),
             reads=["PB0", "PB1", "PRM", "TA0"], writes=RK)

    S.op(ACT, lambda e: e.activation(out=CB[:], in_=CT[:], func=AF.Silu, bias=ZEROC, scale=1.0), reads=["CT", "CST"], writes=["CB"])
    for g in range(3):
        s = w_next()
        for cc in range(4):
            c0 = (g * 4 + cc) * 4
            for k in range(16):
                S.op(PE, lambda e, s=s, cc=cc, k=k, c0=c0: e.matmul(
                    PSA[:, c0:c0 + 4], lhsT=W[:, s, k, cc * 128:(cc + 1) * 128], rhs=CB[:, k * 4:(k + 1) * 4],
                    start=(k == 0), stop=(k == 15)),
                    reads=["W%d" % s, "CB"], writes=["PA0"], inc=(k == 15))
    S.op(DVE, lambda e: e.tensor_copy(out=MODP[:], in_=PSA[:, 0:48]), reads=["PA0"], writes=["MODP"])
    S.dma(SP, lambda e: e.dma_start(out=ag0_in, in_=MODP[:]), dbo, reads=["MODP"], writes=["ag0_in"])
    S.dma(POOL, lambda e: e.collective_compute("AllGather", ALU.bypass, replica_groups=[list(range(8))],
                                               ins=[ag0_in], outs=[ag0_out]), dcc[0], reads=["ag0_in"], writes=["ag0_out"])
    S.dma(SP, lambda e: e.dma_start(out=MODG[:].rearrange("p (r c) b -> p r (c b)", r=8),
                                    in_=ag0_out.rearrange("(r p) c -> p r c", p=128)),
          dbi, reads=["ag0_out"], writes=["MODG"])
    def rms_square(k, buf, bkey):
        S.op(ACT, lambda e, k=k, buf=buf: e.activation(out=buf, in_=XA[:, k, HAL:TW], func=AF.Square, bias=ZEROC, scale=1.0),
             reads=[xa_keys[k], "CST"], writes=[bkey])

    def rms_mm(k, buf, bkey):
        for tb in range(2):
            S.op(PE, lambda e, k=k, buf=buf, tb=tb: e.matmul(PSS[:, tb * 512:(tb + 1) * 512], lhsT=ONES[:], rhs=buf[:, tb * 512:(tb + 1) * 512],
                                                           start=(k == 0), stop=(k == 15)),
                 reads=["ONES", bkey], writes=["S%d" % tb], inc=(tb == 1))

    def rms_finish(sum_div):
        S.op(ACT, lambda e: e.activation(out=RSTDM[:], in_=PSS[:, 0:T], func=AF.Sqrt, bias=EPSC, scale=1.0 / sum_div),
             reads=["S0", "S1", "CST"], writes=["T1"])
        S.op(DVE, lambda e: e.reciprocal(out=RSTDM[:], in_=RSTDM[:]), reads=["T1"], writes=["T1"])

    def sqbuf2(k):
        return SQ[k % 2][:, 0:T], "SQ%d" % (k % 2)

    S.op(ACT, lambda e: e.activation(out=SQH[:], in_=XA[:, :, 0:HAL], func=AF.Square, bias=ZEROC, scale=1.0), reads=xa_keys + ["CST"], writes=["SQH"])
    for k in range(16):
        S.op(PE, lambda e, k=k: e.matmul(PSC[:, 0:HAL], lhsT=ONES[:], rhs=SQH[:, k, :], start=(k == 0), stop=(k == 15)),
             reads=["ONES", "SQH"], writes=["PC0"], inc=(k == 15))
    S.op(ACT, lambda e: e.activation(out=RSTDH[:, 0:HAL], in_=PSC[:, 0:HAL], func=AF.Sqrt, bias=EPSC, scale=1.0 / D), reads=["PC0", "CST"], writes=["RSTDH"])
    S.op(DVE, lambda e: e.reciprocal(out=RSTDH[:, 0:HAL], in_=RSTDH[:, 0:HAL]), reads=["RSTDH"], writes=["RSTDH"])
    for k in range(16):
        rms_square(k, MIX[:, k % 8, :], mx_keys[k % 8])
        rms_mm(k, MIX[:, k % 8, :], mx_keys[k % 8])
    rms_finish(D)
    S.op(DVE, lambda e: e.tensor_scalar(out=MOD[:], in0=MODG[:, :, 0], scalar1=P(O_SELB), scalar2=None, op0=ALU.mult),
         reads=["MODG", "PRM"], writes=["MOD"])
    for b in range(1, 4):
        S.op(DVE, lambda e, b=b: e.scalar_tensor_tensor(out=MOD[:], in0=MODG[:, :, b], scalar=P(O_SELB + b), in1=MOD[:],
                                                        op0=ALU.mult, op1=ALU.add), reads=["MODG", "MOD", "PRM"], writes=["MOD"])
    S.op(DVE, lambda e: e.tensor_tensor(out=MOD[:], in0=MOD[:], in1=P(O_BADA, 96), op=ALU.add), reads=["MOD", "PRM"], writes=["MOD"])
    S.op(DVE, lambda e: e.scalar_tensor_tensor(out=A1[:], in0=MOD[:, 16:32], scalar=1.0, in1=P(O_N1, 16), op0=ALU.add, op1=ALU.mult),
         reads=["MOD", "PRM"], writes=["A1"])
    S.op(DVE, lambda e: e.scalar_tensor_tensor(out=A2[:], in0=MOD[:, 64:80], scalar=1.0, in1=P(O_N2, 16), op0=ALU.add, op1=ALU.mult),
         reads=["MOD", "PRM"], writes=["A2"])
    n1bufs = [(TA[0][:, HAL:TW], ["TA0"]), (TA[1][:, HAL:TW], ["TA1"]), (TT[0][:], TK[0]), (TT[2][:], TK[2]), (TT[3][:], TK[3])]
    for k in range(16):
        tb_, tk_ = n1bufs[k % 5]
        S.op(DVE, lambda e, k=k, tb_=tb_: e.scalar_tensor_tensor(out=tb_, in0=XA[:, k, HAL:TW], scalar=A1[:, k:k + 1], in1=RSTDM[:],
                                                                 op0=ALU.mult, op1=ALU.mult),
             reads=[xa_keys[k], "A1", "T1"], writes=tk_)
        S.op(ACT, lambda e, k=k, tb_=tb_: e.activation(out=HT[:, k, HAL:TW], in_=tb_, func=AF.Identity, bias=MOD[:, k:k + 1], scale=1.0),
             reads=tk_ + ["MOD"], writes=[ht_keys[k]])
    for k in range(16):
        S.op(DVE, lambda e, k=k: e.scalar_tensor_tensor(out=HTMP[:, k, :], in0=XA[:, k, 0:HAL], scalar=A1[:, k:k + 1], in1=RSTDH[:, 0:HAL],
                                                        op0=ALU.mult, op1=ALU.mult),
             reads=[xa_keys[k], "A1", "RSTDH"], writes=["HTMP"])
    for k in range(16):
        S.op(DVE, lambda e, k=k: e.tensor_scalar(out=HT[:, k, 0:HAL], in0=HTMP[:, k, :], scalar1=MOD[:, k:k + 1], scalar2=P(O_FLAG),
                                                 op0=ALU.add, op1=ALU.mult),
             reads=["HTMP", "MOD", "PRM"], writes=["HTh"])
    def inproj_fm(s, cc, halo_col=None):
        ps, pk = next_pair()
        for k in range(16):
            for tb in range(2):
                S.op(PE, lambda e, s=s, cc=cc, k=k, tb=tb, ps=ps: e.matmul(
                    ps[:, tb * 512:(tb + 1) * 512], lhsT=W[:, s, k, cc * 128:(cc + 1) * 128],
                    rhs=HT[:, k, HAL + tb * 512:HAL + (tb + 1) * 512], start=(k == 0), stop=(k == 15)),
                    reads=["W%d" % s, ht_keys[k]], writes=[pk[tb]], inc=(k == 15 and tb == 1 and halo_col is None))
            if halo_col is not None:
                S.op(PE, lambda e, s=s, cc=cc, k=k: e.matmul(
                    PSS[:, halo_col:halo_col + HAL], lhsT=W[:, s, k, cc * 128:(cc + 1) * 128], rhs=HT[:, k, 0:HAL],
                    start=(k == 0), stop=(k == 15)),
                    reads=["W%d" % s, "HTh"], writes=["S0"], inc=(k == 15))
        return ps, pk

    MAL = [MIX[:, 2 * i:2 * i + 2, :].rearrange("p a b -> p (a b)").bitcast(F32) for i in range(4)]
    MALK = [["MX%d" % (2 * i), "MX%d" % (2 * i + 1)] for i in range(4)]
    LSET = [
        dict(xbs=TA[0], xk=["TA0"], acc=TT[0], ak=TK[0], thr=TT[1], rk=TK[1], thi=TT[2], ik=TK[2], a2=TT[3], a2k=TK[3], xcb=SQ[0], ck=["SQ0"]),
        dict(xbs=TA[1], xk=["TA1"], acc=MAL[0], ak=MALK[0], thr=MAL[1], rk=MALK[1], thi=MAL[2], ik=MALK[2], a2=MAL[3], a2k=MALK[3], xcb=SQ[1], ck=["SQ1"]),
    ]

    def lru_conv(j, ps, pk, L):
        xbs, acc, xcb = L["xbs"], L["acc"], L["xcb"]
        S.op(ACT, lambda e: e.activation(out=xbs[:, HAL:TW], in_=ps[:, 0:T], func=AF.Identity, bias=ZEROC, scale=1.0), reads=pk + ["CST"], writes=L["xk"])
        S.op(ACT, lambda e: e.activation(out=xbs[:, 0:HAL], in_=PSS[:, j * HAL:(j + 1) * HAL], func=AF.Identity, bias=ZEROC, scale=1.0),
             reads=["S0", "CST"], writes=L["xk"])
        S.op(ACT, lambda e: e.activation(out=acc[:], in_=xbs[:, HAL:TW], func=AF.Identity, bias=P(O_CB + j), scale=P(O_CW + j * 4 + 3)),
             reads=L["xk"] + ["PRM"], writes=L["ak"])
        for sft in (1, 2, 3):
            S.op(DVE, lambda e, sft=sft: e.scalar_tensor_tensor(out=acc[:], in0=xbs[:, HAL - sft:TW - sft], scalar=P(O_CW + j * 4 + 3 - sft),
                                                                in1=acc[:], op0=ALU.mult, op1=ALU.add),
                 reads=L["xk"] + L["ak"] + ["PRM"], writes=L["ak"])
        S.op(ACT, lambda e: e.activation(out=xcb[:, 0:T], in_=acc[:], func=AF.Identity, bias=ZEROC, scale=1.0), reads=L["ak"] + ["CST"], writes=L["ck"])

    def lru_gates(j, L):
        xcb = L["xcb"]
        pr, pkr = next_pair()
        pi, pki = next_pair()
        for (pp, kk, wb, wk) in ((pr, pkr, WRB, "MX14"), (pi, pki, WIB, "MX15")):
            for tb in range(2):
                S.op(PE, lambda e, pp=pp, wb=wb, tb=tb: e.matmul(pp[:, tb * 512:(tb + 1) * 512], lhsT=wb[:, j, :],
                                                                  rhs=xcb[:, tb * 512:(tb + 1) * 512], start=True, stop=True),
                     reads=[wk] + L["ck"], writes=[kk[tb]], inc=(tb == 1))
        return pr, pkr, pi, pki

    def lru_rest(j, L, pr, pkr, pi, pki):
        acc, thr, thi, a2 = L["acc"], L["thr"], L["thi"], L["a2"]
        S.op(ACT, lambda e: e.activation(out=thr[:], in_=pr[:, 0:T], func=AF.Tanh, bias=LP[:, 16 + j:17 + j], scale=0.5), reads=pkr + ["LP"], writes=L["rk"])
        S.op(ACT, lambda e: e.activation(out=thi[:], in_=pi[:, 0:T], func=AF.Tanh, bias=LP[:, 24 + j:25 + j], scale=0.5), reads=pki + ["LP"], writes=L["ik"])
        a_ap = XA[:, j, HAL:TW]
        b_ap = XA[:, 8 + j, HAL:TW]
        S.op(ACT, lambda e: e.activation(out=a_ap, in_=thr[:], func=AF.Exp, bias=LP[:, j:j + 1], scale=LP[:, j:j + 1]), reads=L["rk"] + ["LP"], writes=[xa_keys[j]])
        S.op(ACT, lambda e: e.activation(out=a2[:], in_=thr[:], func=AF.Exp, bias=LP[:, 8 + j:9 + j], scale=LP[:, 8 + j:9 + j]), reads=L["rk"] + ["LP"], writes=L["a2k"])
        S.op(ACT, lambda e: e.activation(out=a2[:], in_=a2[:], func=AF.Sqrt, bias=QRTC, scale=-0.25), reads=L["a2k"] + ["CST"], writes=L["a2k"])
        S.op(DVE, lambda e: e.scalar_tensor_tensor(out=thi[:], in0=thi[:], scalar=1.0, in1=acc[:], op0=ALU.add, op1=ALU.mult),
             reads=L["ik"] + L["ak"], writes=L["ik"])
        S.op(DVE, lambda e: e.tensor_tensor(out=b_ap, in0=thi[:], in1=a2[:], op=ALU.mult), reads=L["ik"] + L["a2k"], writes=[xa_keys[8 + j]])
        S.op(DVE, lambda e: e.tensor_tensor_scan(out=thr[:], data0=a_ap, data1=b_ap, initial=0.0, op0=ALU.mult, op1=ALU.add),
             reads=[xa_keys[j], xa_keys[8 + j]] + L["rk"], writes=L["rk"])
        S.op(DVE, lambda e: e.tensor_copy(out=HEND[:, j:j + 1], in_=thr[:, T - 1:T]), reads=L["rk"], writes=["HEND"])

    VGB = [
        [(TA[0][:, 0:512], ["TA0"]), (TA[0][:, 512:1024], ["TA0"]), (TA[1][:, 0:512], ["TA1"]), (TA[1][:, 512:1024], ["TA1"])],
        [(MAL[2][:, 0:512], MALK[2]), (MAL[2][:, 512:1024], MALK[2]), (MAL[3][:, 0:512], MALK[3]), (MAL[3][:, 512:1024], MALK[3])],
    ]
    VBANKS = [(PSA, 0, "PA0"), (PSA, 1, "PA1"), (PSB, 0, "PB0"), (PSB, 1, "PB1")]

    def v_acc(s, batch):
        for ci in range(4):
            c = batch * 4 + ci
            pp, half, key = VBANKS[ci]
            for k in range(16):
                S.op(PE, lambda e, pp=pp, half=half, k=k, c=c: e.matmul(
                    pp[:, half * 512:(half + 1) * 512], lhsT=HT[:, k, HAL + c * 128:HAL + (c + 1) * 128], rhs=W[:, s, k, :],
                    start=(k == 0), stop=(k == 15)),
                    reads=["W%d" % s, ht_keys[k]], writes=[key], inc=(k == 15))

    def v_stats(batch):
        STATS, MV, RS4 = STATS2[batch], MV2[batch], RS42[batch]
        sk, mk, rk = "STATS%d" % batch, "MV%d" % batch, "RS4%d" % batch
        for ci in range(4):
            pp, half, key = VBANKS[ci]
            vg, tk = VGB[batch][ci]
            S.op(ACT, lambda e, pp=pp, half=half, vg=vg: e.activation(out=vg, in_=pp[:, half * 512:(half + 1) * 512], func=AF.Gelu_apprx_tanh,
                                                                        bias=ZEROC, scale=1.0), reads=[key, "CST"], writes=tk)
            for hh in range(4):
                S.op(DVE, lambda e, vg=vg, ci=ci, hh=hh: e.bn_stats(out=STATS[:, ci * 4 + hh, :], in_=vg[:, hh * 128:(hh + 1) * 128]),
                     reads=tk, writes=[sk])
                S.op(DVE, lambda e, ci=ci, hh=hh: e.bn_aggr(out=MV[:, ci * 4 + hh, :], in_=STATS[:, ci * 4 + hh, :]),
                     reads=[sk], writes=[mk])
        S.op(ACT, lambda e: e.activation(out=RS4[:], in_=MV[:, :, 1], func=AF.Sqrt, bias=EPSC, scale=1.0), reads=[mk, "CST"], writes=[rk])
        S.op(DVE, lambda e: e.reciprocal(out=RS4[:], in_=RS4[:]), reads=[rk], writes=[rk])

    def v_spatial(hb, batch):
        MV, RS4 = MV2[batch], RS42[batch]
        mk, rk = "MV%d" % batch, "RS4%d" % batch
        for ci in range(4):
            c = batch * 4 + ci
            vg, tk = VGB[batch][ci]
            vh = VH[ci % 2]
            vk = "VH%d" % (ci % 2)
            pc_o = (ci % 2) * 512
            pck = "PC%d" % (ci % 2)
            for hh in range(4):
                S.op(DVE, lambda e, vg=vg, vh=vh, ci=ci, hh=hh: e.tensor_scalar(
                    out=vh[:, hh * 128:(hh + 1) * 128], in0=vg[:, hh * 128:(hh + 1) * 128],
                    scalar1=MV[:, ci * 4 + hh, 0:1], scalar2=RS4[:, ci * 4 + hh:ci * 4 + hh + 1], op0=ALU.subtract, op1=ALU.mult),
                    reads=tk + [mk, rk], writes=[vk])
            for hh in range(4):
                S.op(PE, lambda e, vh=vh, hh=hh, pc_o=pc_o: e.matmul(PSC[:, pc_o + hh * 128:pc_o + (hh + 1) * 128], lhsT=vh[:, hh * 128:(hh + 1) * 128],
                                                                       rhs=WSB[:, hb + hh, :], start=True, stop=True),
                     reads=[vk, "WSB"], writes=[pck], inc=(hh == 3))
            for hh in range(4):
                S.op(DVE, lambda e, hh=hh, c=c, pc_o=pc_o: e.scalar_tensor_tensor(
                    out=TT[hh][:, c * 128:(c + 1) * 128], in0=PSC[:, pc_o + hh * 128:pc_o + (hh + 1) * 128], scalar=P(O_LNG + hb + hh),
                    in1=R[:, (hb + hh) * 128:(hb + hh + 1) * 128], op0=ALU.mult, op1=ALU.add),
                    reads=[pck, "PRM"] + RK, writes=TK[hh])

    def v_group(s, hb):
        v_acc(s, 0)
        v_stats(0)
        v_acc(s, 1)
        v_stats(1)
        v_spatial(hb, 0)
        v_spatial(hb, 1)

    sumsq_state = {"n": 0}

    pending_mm = []

    def flush_pending():
        while pending_mm:
            pending_mm.pop(0)()

    def branch_tail(y_ap, ykey, idx, scale_col, mixk):
        sq = SQ[idx % 2]
        sk = "SQ%d" % (idx % 2)
        S.op(ACT, lambda e: e.activation(out=sq[:, 0:T], in_=y_ap, func=AF.Square, bias=ZEROC, scale=1.0), reads=[ykey, "CST"], writes=[sk])

        def mm():
            n = sumsq_state["n"]
            for tb in range(2):
                S.op(PE, lambda e, tb=tb, n=n: e.matmul(PSS[:, tb * 512:(tb + 1) * 512], lhsT=ONES[:], rhs=sq[:, tb * 512:(tb + 1) * 512],
                                                        start=(n == 0), stop=(n == 7)),
                     reads=["ONES", sk], writes=["S%d" % tb], inc=(tb == 1))
            sumsq_state["n"] = (n + 1) % 8
        pending_mm.append(mm)
        S.op(ACT, lambda e: e.activation(out=MIX[:, mixk, :], in_=y_ap, func=AF.Identity, bias=ZEROC, scale=P(scale_col)),
             reads=[ykey, "PRM", "CST"], writes=[mx_keys[mixk]])

    def branch_finish(k0):
        flush_pending()
        S.op(ACT, lambda e: e.activation(out=RSTDM[:], in_=PSS[:, 0:T], func=AF.Sqrt, bias=EPSC, scale=1.0 / 1024.0),
             reads=["S0", "S1", "CST"], writes=["T1"])
        S.op(DVE, lambda e: e.reciprocal(out=RSTDM[:], in_=RSTDM[:]), reads=["T1"], writes=["T1"])
        for k in range(k0, k0 + 8):
            S.op(DVE, lambda e, k=k: e.tensor_tensor(out=MIX[:, k, :], in0=MIX[:, k, :], in1=RSTDM[:], op=ALU.mult),
                 reads=[mx_keys[k], "T1"], writes=[mx_keys[k]])

    def u_group(s, hb):
        for cc in range(4):
            h = hb + cc
            ps, pk = inproj_fm(s, cc)
            flush_pending()
            gu = TA[cc % 2]
            gk = "TA%d" % (cc % 2)
            S.op(ACT, lambda e, ps=ps, gu=gu: e.activation(out=gu[:, 0:T], in_=ps[:, 0:T], func=AF.Gelu_apprx_tanh, bias=ZEROC, scale=1.0),
                 reads=pk + ["CST"], writes=[gk])
            S.op(DVE, lambda e, gu=gu, cc=cc: e.tensor_tensor(out=gu[:, 0:T], in0=gu[:, 0:T], in1=TT[cc][:], op=ALU.mult),
                 reads=[gk] + TK[cc], writes=[gk])
            branch_tail(gu[:, 0:T], gk, cc, O_GA + h, h)

    def gb_group(s, jb):
        for cc in range(4):
            j = jb + cc
            ps, pk = inproj_fm(s, cc)
            flush_pending()
            hbuf = TT[cc % 2]
            hk = "T%d" % (cc % 2)
            S.op(DVE, lambda e, hbuf=hbuf, j=j: e.tensor_tensor_scan(out=hbuf[:], data0=XA[:, j, HAL:TW], data1=XA[:, 8 + j, HAL:TW],
                                                                      initial=INIT[:, j:j + 1], op0=ALU.mult, op1=ALU.add),
                 reads=[xa_keys[j], xa_keys[8 + j], "INIT"], writes=[hk])
            gg = TA[cc % 2]
            gk = "TA%d" % (cc % 2)
            S.op(ACT, lambda e, ps=ps, gg=gg: e.activation(out=gg[:, 0:T], in_=ps[:, 0:T], func=AF.Gelu_apprx_tanh, bias=ZEROC, scale=1.0),
                 reads=pk + ["CST"], writes=[gk])
            S.op(DVE, lambda e, gg=gg, hbuf=hbuf: e.tensor_tensor(out=gg[:, 0:T], in0=gg[:, 0:T], in1=hbuf[:], op=ALU.mult),
                 reads=[gk, hk], writes=[gk])
            branch_tail(gg[:, 0:T], gk, cc, O_GL + j, 8 + j)

    lru_slots = [w_next()]
    ip = inproj_fm(lru_slots[0], 0, halo_col=0)
    for j in range(8):
        L = LSET[j % 2]
        lru_conv(j, ip[0], ip[1], L)
        ip_next = None
        if j < 7:
            if j == 3:
                lru_slots.append(w_next())
            ip_next = inproj_fm(lru_slots[(j + 1) // 4], (j + 1) % 4, halo_col=(j + 1) * HAL)
        gts = lru_gates(j, L)
        lru_rest(j, L, *gts)
        ip = ip_next
    s = w_next(); v_group(s, 0)
    s = w_next(); u_group(s, 0)
    S.dma(SP, lambda e: e.dma_start(out=ag1_in, in_=HEND[:]), dbo, reads=["HEND"], writes=["ag1_in"])
    S.dma(POOL, lambda e: e.collective_compute("AllGather", ALU.bypass, replica_groups=[list(range(8))],
                                               ins=[ag1_in], outs=[ag1_out]), dcc[1], reads=["ag1_in"], writes=["ag1_out"])
    S.dma(SP, lambda e: e.dma_start(out=AGT1[:], in_=ag1_out.rearrange("(r p) c -> p r c", p=128)), dbi, reads=["ag1_out"], writes=["AGT1"])
    s = w_next(); v_group(s, 4)
    s = w_next(); u_group(s, 4)
    branch_finish(0)
    S.op(DVE, lambda e: e.tensor_scalar(out=INIT[:], in0=AGT1[:, 0, :], scalar1=P(O_SEL), scalar2=None, op0=ALU.mult),
         reads=["AGT1", "PRM"], writes=["INIT"])
    for r in range(1, 8):
        S.op(DVE, lambda e, r=r: e.scalar_tensor_tensor(out=INIT[:], in0=AGT1[:, r, :], scalar=P(O_SEL + r), in1=INIT[:],
                                                        op0=ALU.mult, op1=ALU.add), reads=["AGT1", "INIT", "PRM"], writes=["INIT"])
    s = w_next(); gb_group(s, 0)
    s = w_next(); gb_group(s, 4)
    branch_finish(8)
    for q in range(4):
        S.dma(SP, lambda e, q=q: e.dma_start(out=XA[:, 4 * q:4 * q + 4, :],
                                             in_=xT[512 * q:512 * (q + 1), :].rearrange("(k p) t -> p k t", p=128)),
              dxr, writes=xa_keys[4 * q:4 * q + 4])
    tok_xr = (dxr, dxr.cnt)
    for k in xa_keys:
        S.last_w[k] = tok_xr

    def proj_down(s, mm, m, gate_col, nk):
        ps, pk = next_pair()
        for k in range(nk):
            for tb in range(2):
                S.op(PE, lambda e, k=k, tb=tb, ps=ps: e.matmul(ps[:, tb * 512:(tb + 1) * 512], lhsT=W[:, s, k, mm * 128:(mm + 1) * 128],
                                                               rhs=MIX[:, k, tb * 512:(tb + 1) * 512], start=(k == 0), stop=(k == nk - 1)),
                     reads=["W%d" % s, mx_keys[k]], writes=[pk[tb]], inc=(k == nk - 1 and tb == 1))
        S.op(DVE, lambda e, ps=ps: e.scalar_tensor_tensor(out=XA[:, m, HAL:TW], in0=ps[:, 0:T], scalar=MOD[:, gate_col + m:gate_col + m + 1],
                                                          in1=XA[:, m, HAL:TW], op0=ALU.mult, op1=ALU.add),
             reads=pk + ["MOD", xa_keys[m]], writes=[xa_keys[m]])

    for g in range(4):
        s = w_next()
        for mm in range(4):
            m = g * 4 + mm
            proj_down(s, mm, m, 32, 16)
            if m >= 2:
                rms_mm(m - 2, *sqbuf2(m - 2))
            rms_square(m, *sqbuf2(m))
    rms_mm(14, *sqbuf2(14))
    rms_mm(15, *sqbuf2(15))

    S.op(DVE, lambda e: e.tensor_copy(out=X1H[:].rearrange("p (k t) -> p k t", t=2), in_=XA[:, :, TW - 2:TW]), reads=xa_keys, writes=["X1H"])
    S.dma(SP, lambda e: e.dma_start(out=ag2_in, in_=X1H[:]), dbo, reads=["X1H"], writes=["ag2_in"])
    S.dma(POOL, lambda e: e.collective_compute("AllGather", ALU.bypass, replica_groups=[list(range(8))],
                                               ins=[ag2_in], outs=[ag2_out]), dcc[2], reads=["ag2_in"], writes=["ag2_out"])
    S.dma(SP, lambda e: e.dma_start(out=AGT2[:], in_=ag2_out.rearrange("(r p) c -> p r c", p=128)), dbi, reads=["ag2_out"], writes=["T0"])

    rms_finish(D)
    T5 = MIX[:, 14:16, :].rearrange("p a b -> p (a b)").bitcast(F32)
    n2bufs = [(TA[0][:, HAL:TW], ["TA0"]), (TA[1][:, HAL:TW], ["TA1"]), (TT[2][:], TK[2]), (TT[3][:], TK[3]), (R[:], RK), (T5[:], ["MX14", "MX15"])]
    for k in range(16):
        tb_, tk_ = n2bufs[k % 6]
        S.op(DVE, lambda e, k=k, tb_=tb_: e.scalar_tensor_tensor(out=tb_, in0=XA[:, k, HAL:TW], scalar=A2[:, k:k + 1], in1=RSTDM[:],
                                                                 op0=ALU.mult, op1=ALU.mult),
             reads=[xa_keys[k], "A2", "T1"], writes=tk_)
        S.op(ACT, lambda e, k=k, tb_=tb_: e.activation(out=HT[:, k, HAL:TW], in_=tb_, func=AF.Identity, bias=MOD[:, 48 + k:49 + k], scale=1.0),
             reads=tk_ + ["MOD"], writes=[ht_keys[k]])
    S.op(DVE, lambda e: e.tensor_scalar(out=XH[:], in0=AGT2[:, 0, :], scalar1=P(O_SEL), scalar2=None, op0=ALU.mult), reads=["T0", "PRM"], writes=["XH"])
    for r in range(1, 8):
        S.op(DVE, lambda e, r=r: e.scalar_tensor_tensor(out=XH[:], in0=AGT2[:, r, :], scalar=P(O_SEL + r), in1=XH[:], op0=ALU.mult, op1=ALU.add),
             reads=["T0", "XH", "PRM"], writes=["XH"])
    S.op(ACT, lambda e: e.activation(out=SQH2[:], in_=XH[:], func=AF.Square, bias=ZEROC, scale=1.0), reads=["XH", "CST"], writes=["SQH2"])
    for k in range(16):
        S.op(PE, lambda e, k=k: e.matmul(PSC[:, 0:2], lhsT=ONES[:], rhs=SQH2[:, k * 2:(k + 1) * 2], start=(k == 0), stop=(k == 15)),
             reads=["ONES", "SQH2"], writes=["PC0"], inc=(k == 15))
    S.op(ACT, lambda e: e.activation(out=RSTDH[:, 0:2], in_=PSC[:, 0:2], func=AF.Sqrt, bias=EPSC, scale=1.0 / D), reads=["PC0", "CST"], writes=["RSTDH"])
    S.op(DVE, lambda e: e.reciprocal(out=RSTDH[:, 0:2], in_=RSTDH[:, 0:2]), reads=["RSTDH"], writes=["RSTDH"])
    for k in range(16):
        S.op(DVE, lambda e, k=k: e.scalar_tensor_tensor(out=TMPH[:, k * 2:(k + 1) * 2], in0=XH[:, k * 2:(k + 1) * 2], scalar=A2[:, k:k + 1],
                                                        in1=RSTDH[:, 0:2], op0=ALU.mult, op1=ALU.mult),
             reads=["XH", "A2", "RSTDH"], writes=["TMPH"])
        S.op(DVE, lambda e, k=k: e.tensor_scalar(out=HT[:, k, 2:HAL], in0=TMPH[:, k * 2:(k + 1) * 2], scalar1=MOD[:, 48 + k:49 + k],
                                                 scalar2=P(O_FLAG), op0=ALU.add, op1=ALU.mult),
             reads=["TMPH", "MOD", "PRM"], writes=["HTh"])

    ffn_i = [0]
    for H in range(3):
        for q in range(4):
            sg = w_next()
            sv = w_next(2)
            for cc in range(4):
                j = H * 16 + q * 4 + cc
                jj = q * 4 + cc
                i = ffn_i[0]
                ffn_i[0] += 1
                pg, pkg = next_pair()
                for k in range(16):
                    for tb in range(2):
                        S.op(PE, lambda e, k=k, tb=tb, pg=pg, cc=cc, sg=sg: e.matmul(
                            pg[:, tb * 512:(tb + 1) * 512], lhsT=W[:, sg, k, cc * 128:(cc + 1) * 128],
                            rhs=HT[:, k, HAL + tb * 512:HAL + (tb + 1) * 512], start=(k == 0), stop=(k == 15)),
                            reads=["W%d" % sg, ht_keys[k]], writes=[pkg[tb]], inc=False)
                    S.op(PE, lambda e, k=k, cc=cc, sg=sg, j=j: e.matmul(
                        PSS[:, j * 2:j * 2 + 2], lhsT=W[:, sg, k, cc * 128:(cc + 1) * 128], rhs=HT[:, k, 2:HAL],
                        start=(k == 0), stop=(k == 15)),
                        reads=["W%d" % sg, "HTh"], writes=["S0"], inc=(k == 15))
                pv, pkv = next_pair()
                for k in range(16):
                    for tb in range(2):
                        S.op(PE, lambda e, k=k, tb=tb, pv=pv, cc=cc, sv=sv: e.matmul(
                            pv[:, tb * 512:(tb + 1) * 512], lhsT=W[:, sv, k, cc * 128:(cc + 1) * 128],
                            rhs=HT[:, k, HAL + tb * 512:HAL + (tb + 1) * 512], start=(k == 0), stop=(k == 15)),
                            reads=["W%d" % sv, ht_keys[k]], writes=[pkv[tb]], inc=(k == 15 and tb == 1))
                gs = TA[i % 2]
                gk = "TA%d" % (i % 2)
                acc = TT[i % 2]
                ak = "T%d" % (i % 2)
                S.op(ACT, lambda e, gs=gs, pg=pg: e.activation(out=gs[:, 2:2 + T], in_=pg[:, 0:T], func=AF.Identity, bias=ZEROC, scale=1.0),
                     reads=pkg + ["CST"], writes=[gk])
                S.op(ACT, lambda e, gs=gs, j=j: e.activation(out=gs[:, 0:2], in_=PSS[:, j * 2:j * 2 + 2], func=AF.Identity, bias=ZEROC, scale=1.0),
                     reads=["S0", "CST"], writes=[gk])
                S.op(ACT, lambda e, gs=gs, acc=acc, j=j: e.activation(out=acc[:], in_=gs[:, 2:2 + T], func=AF.Identity, bias=P(O_FCB + j),
                                                                      scale=P(O_FCW + j * 3 + 2)), reads=[gk, "PRM"], writes=[ak])
                for sft in (1, 2):
                    S.op(DVE, lambda e, gs=gs, acc=acc, j=j, sft=sft: e.scalar_tensor_tensor(
                        out=acc[:], in0=gs[:, 2 - sft:2 - sft + T], scalar=P(O_FCW + j * 3 + 2 - sft), in1=acc[:], op0=ALU.mult, op1=ALU.add),
                        reads=[gk, ak, "PRM"], writes=[ak])
                S.op(ACT, lambda e, acc=acc: e.activation(out=acc[:], in_=acc[:], func=AF.Gelu_apprx_tanh, bias=ZEROC, scale=1.0),
                     reads=[ak, "CST"], writes=[ak])
                S.op(DVE, lambda e, acc=acc, pv=pv, jj=jj: e.tensor_tensor(out=MIX[:, jj, :], in0=pv[:, 0:T], in1=acc[:], op=ALU.mult),
                     reads=[ak] + pkv, writes=[mx_keys[jj]])
        for gq in range(4):
            s = w_next()
            for mm in range(4):
                m = gq * 4 + mm
                proj_down(s, mm, m, 80, 16)
                if H == 2:
                    if m >= 2:
                        rms_mm(m - 2, *sqbuf2(m - 2))
                    rms_square(m, *sqbuf2(m))

    rms_mm(14, *sqbuf2(14))
    rms_mm(15, *sqbuf2(15))
    rms_finish(D)
    for k in range(16):
        S.op(DVE, lambda e, k=k: e.scalar_tensor_tensor(out=XA[:, k, HAL:TW], in0=XA[:, k, HAL:TW], scalar=P(O_NF + k), in1=RSTDM[:],
                                                        op0=ALU.mult, op1=ALU.mult),
             reads=[xa_keys[k], "PRM", "T1"], writes=[xa_keys[k]])
        S.dma(SP, lambda e, k=k: e.dma_start(out=yT[k * 128:(k + 1) * 128, :], in_=XA[:, k, HAL:TW]), dout, reads=[xa_keys[k]])
    S.wait(SP, (dout, dout.cnt))
    SP.prog.append(("o", lambda e: e.nop(), False))

    with S.alloc(nc):
        with nc.Block() as block:
            S.emit(block)
    es.close()
    return nc


def _pk(v, n):
    return np.ascontiguousarray(np.asarray(v, np.float32).reshape(n, 128).T)


def prep_inputs(inp):
    f = lambda a: np.asarray(a, np.float32)
    x = f(inp["x"]); c = f(inp["c"])
    w_ada = f(inp["w_ada"])[0]
    shared = {
        "w_in": np.ascontiguousarray(f(inp["w_in"])[0]),
        "w_out": np.ascontiguousarray(f(inp["w_out"])[0]),
        "w_up": np.ascontiguousarray(f(inp["w_up"])[0]),
        "w_down": np.ascontiguousarray(f(inp["w_down"])[0]),
    }
    cT = np.ascontiguousarray(c.T.reshape(16, 128, 4).transpose(1, 0, 2).reshape(128, 64))
    w_s = f(inp["gmlp_w_s"])[0]
    wsT = np.ascontiguousarray(w_s.transpose(2, 0, 1).reshape(128, 1024))
    tri = np.triu(np.ones((128, 128), np.float32))
    bs_bc = np.ascontiguousarray(np.broadcast_to(f(inp["gmlp_b_s"])[0].reshape(1, 1024), (128, 1024)))

    def bd(w):
        w = f(w)[0]
        o = np.zeros((128, 8, 128), np.float32)
        for j in range(8):
            o[0:64, j, 0:64] = w[2 * j]
            o[64:128, j, 64:128] = w[2 * j + 1]
        return o.reshape(128, 1024)

    wr_bd = bd(inp["lru_w_r"]); wi_bd = bd(inp["lru_w_i"])
    base = np.zeros((128, NPRM), np.float32)
    base[:, O_BADA:O_BADA + 96] = _pk(f(inp["b_ada"])[0], 96)
    base[:, O_N1:O_N1 + 16] = _pk(f(inp["norm1"])[0], 16)
    base[:, O_N2:O_N2 + 16] = _pk(f(inp["norm2"])[0], 16)
    base[:, O_NF:O_NF + 16] = _pk(f(inp["norm_final"]), 16)
    base[:, O_LNG:O_LNG + 8] = _pk(f(inp["gmlp_ln_g"])[0], 8)
    base[:, O_LNB:O_LNB + 8] = _pk(f(inp["gmlp_ln_b"])[0], 8)
    base[:, O_CW:O_CW + 32] = f(inp["lru_conv_w"])[0].reshape(4, 8, 128).transpose(2, 1, 0).reshape(128, 32)
    base[:, O_CB:O_CB + 8] = _pk(f(inp["lru_conv_b"])[0], 8)
    base[:, O_BR:O_BR + 8] = _pk(f(inp["lru_b_r"])[0], 8)
    base[:, O_BI:O_BI + 8] = _pk(f(inp["lru_b_i"])[0], 8)
    base[:, O_LAM:O_LAM + 8] = _pk(f(inp["lru_lambda"])[0], 8)
    base[:, O_GL:O_GL + 8] = _pk(f(inp["out_norm_lru"])[0], 8)
    base[:, O_GA:O_GA + 8] = _pk(f(inp["out_norm_gmlp"])[0], 8)
    base[:, O_FCW:O_FCW + 144] = f(inp["ffn_conv_w"])[0].reshape(3, 48, 128).transpose(2, 1, 0).reshape(128, 144)
    base[:, O_FCB:O_FCB + 48] = _pk(f(inp["ffn_conv_b"])[0], 48)
    maps = []
    for core in range(8):
        b, half = core // 2, core % 2
        t0 = half * T
        xt = np.zeros((D, TW), np.float32)
        xt[:, HAL:] = x[b, t0:t0 + T, :].T
        if half == 1:
            xt[:, 0:HAL] = x[b, t0 - HAL:t0, :].T
        p = base.copy()
        if half == 1:
            p[:, O_SEL + core - 1] = 1.0
            p[:, O_FLAG] = 1.0
        p[:, O_SELB + b] = 1.0
        m = dict(shared)
        m.update({
            "xT": xt, "cT": cT, "w_ada_r": np.ascontiguousarray(w_ada[:, core * 1536:(core + 1) * 1536]),
            "prm": p, "wsT": wsT, "tri": tri, "bs_bc": bs_bc, "wr_bd": wr_bd, "wi_bd": wi_bd,
        })
        maps.append(m)
    return maps


_NC = None


def kernel(**inputs):
    global _NC
    maps = prep_inputs(inputs)
    if _NC is None:
        _NC = build_program()
    res = run_bass_kernel_spmd(_NC, maps, core_ids=list(range(8)))
    out = np.empty((4, 2048, D), np.float32)
    for core in range(8):
        b, half = core // 2, core % 2
        out[b, half * T:(half + 1) * T, :] = np.asarray(res.results[core]["yT"], np.float32).T
    return out
```

```python
import contextlib
import numpy as np
import concourse.bass as bass
import concourse.mybir as mybir
from concourse.bass_utils import run_bass_kernel_spmd

F32 = mybir.dt.float32
BF16 = mybir.dt.bfloat16
AF = mybir.ActivationFunctionType
ALU = mybir.AluOpType

D = 2048
T = 1024
HAL = 4
TW = T + HAL
EPS = 1e-6
NPRM = 448
O_BADA, O_N1, O_N2, O_NF, O_LNG, O_LNB, O_CW, O_CB, O_BR, O_BI, O_LAM, O_GL, O_GA, O_FCW, O_FCB, O_SEL, O_FLAG, O_SELB = (
    0, 96, 112, 128, 144, 152, 160, 192, 200, 208, 216, 224, 232, 240, 384, 432, 440, 441)


class DSem:
    def __init__(self, name, step=16):
        self.name = name
        self.sem = None
        self.cnt = 0
        self.step = step


class Eng:
    def __init__(self, name, self_sync=True):
        self.name = name
        self.sem = None
        self.cnt = 0
        self.prog = []
        self.waited = {}
        self.self_sync = self_sync

    def _wait(self, tok):
        src, val = tok
        if src is self and not self.self_sync:
            return
        if self.waited.get(id(src), 0) >= val:
            return
        self.waited[id(src)] = val
        self.prog.append(("w", src, val))


class Sched:
    def __init__(self):
        self.engs = {}
        self.dsems = []
        self.last_w = {}
        self.readers = {}

    def eng(self, name, self_sync=True):
        e = Eng(name, self_sync)
        self.engs[name] = e
        return e

    def dsem(self, name, step=16):
        d = DSem(name, step)
        self.dsems.append(d)
        return d

    def _deps(self, e, reads, writes, extra):
        for k in reads:
            t = self.last_w.get(k)
            if t is not None:
                e._wait(t)
        for k in writes:
            t = self.last_w.get(k)
            if t is not None:
                e._wait(t)
            for t in self.readers.get(k, ()):
                e._wait(t)
        for t in extra:
            if t is not None:
                e._wait(t)

    def _commit(self, tok, reads, writes):
        for k in reads:
            self.readers.setdefault(k, []).append(tok)
        for k in writes:
            self.last_w[k] = tok
            self.readers[k] = []

    def op(self, e, fn, reads=(), writes=(), extra=(), inc=True):
        self._deps(e, reads, writes, extra)
        if inc:
            e.cnt += 1
            tok = (e, e.cnt)
            e.prog.append(("o", fn, True))
        else:
            tok = (e, e.cnt + 1)
            e.prog.append(("o", fn, False))
        self._commit(tok, reads, writes)
        return tok

    def dma(self, e, fn, ds, reads=(), writes=(), extra=()):
        self._deps(e, reads, writes, extra)
        ds.cnt += ds.step
        tok = (ds, ds.cnt)
        e.prog.append(("d", fn, ds))
        self._commit(tok, reads, writes)
        return tok

    def wait(self, e, tok):
        e._wait(tok)

    def fix_pending(self):
        pass

    def alloc(self, nc):
        stack = contextlib.ExitStack()
        for e in self.engs.values():
            e.sem = stack.enter_context(nc.semaphore("s_" + e.name))
        for d in self.dsems:
            d.sem = stack.enter_context(nc.semaphore("d_" + d.name))
        return stack

    def emit(self, block):
        def runner(e):
            def run(engine):
                for item in e.prog:
                    if item[0] == "w":
                        engine.wait_ge(item[1].sem, item[2])
                    elif item[0] == "o":
                        ins = item[1](engine)
                        if item[2]:
                            ins.then_inc(e.sem, 1)
                    else:
                        ins = item[1](engine)
                        ins.then_inc(item[2].sem, item[2].step)
            return run

        for name, e in self.engs.items():
            getattr(block, name)(runner(e))


def build_program():
    nc = bass.Bass("TRN2", target_bir_lowering=False)

    def din(name, shape):
        return nc.dram_tensor(name, list(shape), F32, kind="ExternalInput").ap()

    xT = din("xT", [D, TW])
    cT = din("cT", [128, 64])
    w_ada = din("w_ada_r", [D, 1536])
    prm = din("prm", [128, NPRM])
    wsT_d = din("wsT", [128, 1024])
    tri_d = din("tri", [128, 128])
    bs_d = din("bs_bc", [128, 1024])
    wr_d = din("wr_bd", [128, 1024])
    wi_d = din("wi_bd", [128, 1024])
    w_in = din("w_in", [D, 4096])
    w_out = din("w_out", [D, D])
    w_up = din("w_up", [D, 12288])
    w_down = din("w_down", [6144, D])
    yT = nc.dram_tensor("yT", [D, T], F32, kind="ExternalOutput").ap()
    ag0_in = nc.dram_tensor("ag0_in", [128, 48], F32).ap()
    ag0_out = nc.dram_tensor("ag0_out", [1024, 48], F32).ap()
    ag1_in = nc.dram_tensor("ag1_in", [128, 8], F32).ap()
    ag1_out = nc.dram_tensor("ag1_out", [1024, 8], F32).ap()
    ag2_in = nc.dram_tensor("ag2_in", [128, 32], F32).ap()
    ag2_out = nc.dram_tensor("ag2_out", [1024, 32], F32).ap()

    S = Sched()
    PE = S.eng("tensor", self_sync=False)
    ACT = S.eng("scalar")
    DVE = S.eng("vector")
    POOL = S.eng("gpsimd")
    SP = S.eng("sync")
    dx = S.dsem("x")
    dp = S.dsem("prm")
    dwg = S.dsem("wg")
    dws = [S.dsem("w%d" % i) for i in range(3)]
    dbo = S.dsem("bo")
    dbi = S.dsem("bi")
    dxr = S.dsem("xr")
    dout = S.dsem("out")
    dcc = [S.dsem("cc%d" % i, step=1) for i in range(3)]

    es = contextlib.ExitStack()

    def sb(name, shape, dt=F32):
        return es.enter_context(nc.sbuf_tensor(name, list(shape), dt))

    def pst(name):
        return es.enter_context(nc.psum_tensor(name, [128, 1024], F32))

    XA = sb("XA", [128, 16, TW])
    HT = sb("HT", [128, 16, TW], BF16)
    MIX = sb("MIX", [128, 16, T], BF16)
    W = sb("W", [128, 3, 16, 512], BF16)
    SQ = [sb("SQ%d" % i, [128, T], BF16) for i in range(2)]
    TA = [sb("TA%d" % i, [128, TW]) for i in range(2)]
    TT01 = [sb("T%d" % i, [128, T]) for i in range(2)]
    TT = TT01 + [MIX[:, 8:10, :].rearrange("p a b -> p (a b)").bitcast(F32),
                 MIX[:, 10:12, :].rearrange("p a b -> p (a b)").bitcast(F32)]
    TK = [["T0"], ["T1"], ["MX8", "MX9"], ["MX10", "MX11"]]
    R = MIX[:, 12:14, :].rearrange("p a b -> p (a b)").bitcast(F32)
    RK = ["MX12", "MX13"]
    WRB = MIX[:, 14, :].rearrange("p (j q) -> p j q", q=128)
    WIB = MIX[:, 15, :].rearrange("p (j q) -> p j q", q=128)
    RSTDM = TT01[1]
    WSB = sb("WSB", [128, 8, 128], BF16)
    PRM = sb("PRM", [128, NPRM])
    CT = sb("CT", [128, 64])
    CB = sb("CB", [128, 64], BF16)
    ONES = sb("ONES", [128, 128], BF16)
    MODP = sb("MODP", [128, 48])
    MOD = sb("MOD", [128, 96])
    A1 = sb("A1", [128, 16])
    A2 = sb("A2", [128, 16])
    LP = sb("LP", [128, 64])
    CST = sb("CST", [128, 4])
    SQH = sb("SQH", [128, 16, 4], BF16)
    STATS2 = [sb("STATS%d" % i, [128, 16, 6]) for i in range(2)]
    MV2 = [sb("MV%d" % i, [128, 16, 2]) for i in range(2)]
    RS42 = [sb("RS4%d" % i, [128, 16]) for i in range(2)]
    VH = [sb("VH%d" % i, [128, 512], BF16) for i in range(2)]
    HEND = sb("HEND", [128, 8])
    AGT1 = sb("AGT1", [128, 8, 8])
    INIT = sb("INIT", [128, 8])
    X1H = sb("X1H", [128, 32])
    XH = sb("XH", [128, 32])
    SQH2 = sb("SQH2", [128, 32], BF16)
    RSTDH = sb("RSTDH", [128, 4])
    TMPH = sb("TMPH", [128, 32])
    PSA = pst("PSA")
    PSB = pst("PSB")
    PSC = pst("PSC")
    PSS = pst("PSS")
    TRI = TT01[0][:, 0:128]
    MODG = sb("MODG", [128, 96, 4])
    HTMP = sb("HTMP", [128, 16, 4])
    AGT2 = TT01[0][:, 0:256].rearrange("p (a b) -> p a b", b=32)
    pairs = [(PSA, ["PA0", "PA1"]), (PSB, ["PB0", "PB1"]), (PSC, ["PC0", "PC1"])]
    pair_i = [0]

    def next_pair():
        p = pairs[pair_i[0] % 3]
        pair_i[0] += 1
        return p

    def P(o, n=1):
        return PRM[:, o:o + n]

    xa_keys = ["XA%d" % k for k in range(16)]
    ht_keys = ["HT%d" % k for k in range(16)]
    mx_keys = ["MX%d" % k for k in range(16)]

    wq = []
    for g in range(3):
        wq.append(w_ada[:, g * 512:(g + 1) * 512])
    IN_ORDER = [4, 5, 2, 0, 3, 1, 6, 7]
    for g in IN_ORDER:
        wq.append(w_in[:, g * 512:(g + 1) * 512])
    for g in range(4):
        wq.append(w_out[:, g * 512:(g + 1) * 512])
    for H in range(3):
        for q in range(4):
            j0 = H * 16 + q * 4
            wq.append(w_up[:, j0 * 128:j0 * 128 + 512])
            wq.append(w_up[:, 6144 + j0 * 128:6144 + j0 * 128 + 512])
        for gq in range(4):
            wq.append(w_down[H * 2048:(H + 1) * 2048, gq * 512:(gq + 1) * 512])
    w_issued = [0]
    w_used = [0]

    def w_issue_upto(n):
        while w_issued[0] < min(n, len(wq)):
            i = w_issued[0]
            s = i % 3
            src = wq[i].rearrange("(k p) e -> p k e", p=128)
            S.dma(POOL, lambda e, s=s, src=src: e.dma_start(out=W[:, s], in_=src), dws[s], writes=["W%d" % s])
            w_issued[0] += 1

    def w_next(ahead=3):
        i = w_used[0]
        w_used[0] += 1
        w_issue_upto(i + ahead)
        return i % 3

    S.dma(SP, lambda e: e.dma_start(out=PRM[:], in_=prm), dp, writes=["PRM"])
    S.dma(SP, lambda e: e.dma_start(out=CT[:], in_=cT), dp, writes=["CT"])
    S.dma(SP, lambda e: e.dma_start(out=TRI[:], in_=tri_d), dp, writes=["T0"])
    S.dma(SP, lambda e: e.dma_start(out=TA[1][:, 0:1024], in_=wsT_d), dp, writes=["TA1"])
    S.dma(SP, lambda e: e.dma_start(out=TA[0][:, 0:1024], in_=bs_d), dp, writes=["TA0"])
    tok_p = (dp, dp.cnt)
    for k in ["PRM", "CT", "T0", "TA1", "TA0"]:
        S.last_w[k] = tok_p
    w_issue_upto(3)
    for i in range(3):
        S.wait(SP, (dws[i], 16))
    for q in range(4):
        S.dma(SP, lambda e, q=q: e.dma_start(out=XA[:, 4 * q:4 * q + 4, :],
                                             in_=xT[512 * q:512 * (q + 1), :].rearrange("(k p) t -> p k t", p=128)),
              dx, writes=xa_keys[4 * q:4 * q + 4])
    tok_x = (dx, dx.cnt)
    for k in xa_keys:
        S.last_w[k] = tok_x
    S.dma(POOL, lambda e: e.dma_start(out=MIX[:, 14, :], in_=wr_d), dwg, writes=["MX14"])
    S.dma(POOL, lambda e: e.dma_start(out=MIX[:, 15, :], in_=wi_d), dwg, writes=["MX15"])
    tok_wg = (dwg, dwg.cnt)
    S.last_w["MX14"] = tok_wg
    S.last_w["MX15"] = tok_wg

    S.op(DVE, lambda e: e.memset(ONES[:], 1.0), writes=["ONES"])
    S.op(DVE, lambda e: e.memset(CST[:, 0:1], EPS), writes=["CST"])
    S.op(DVE, lambda e: e.memset(CST[:, 1:2], 1.0), writes=["CST"])
    S.op(DVE, lambda e: e.memset(CST[:, 2:3], 0.25), writes=["CST"])
    S.op(DVE, lambda e: e.memset(CST[:, 3:4], 0.0), writes=["CST"])
    EPSC = CST[:, 0:1]
    ONEC = CST[:, 1:2]
    QRTC = CST[:, 2:3]
    ZEROC = CST[:, 3:4]

    S.op(ACT, lambda e: e.activation(out=LP[:, 32:40], in_=P(O_LAM, 8), func=AF.Abs, bias=ZEROC, scale=1.0), reads=["PRM", "CST"], writes=["LP"])
    S.op(ACT, lambda e: e.activation(out=LP[:, 32:40], in_=LP[:, 32:40], func=AF.Exp, bias=ZEROC, scale=-1.0), reads=["LP", "CST"], writes=["LP"])
    S.op(ACT, lambda e: e.activation(out=LP[:, 32:40], in_=LP[:, 32:40], func=AF.Ln, bias=ONEC, scale=1.0), reads=["LP", "CST"], writes=["LP"])
    S.op(DVE, lambda e: e.tensor_scalar(out=LP[:, 40:48], in0=P(O_LAM, 8), scalar1=-1.0, scalar2=0.0, op0=ALU.mult, op1=ALU.max),
         reads=["PRM", "LP"], writes=["LP"])
    S.op(DVE, lambda e: e.tensor_tensor(out=LP[:, 40:48], in0=LP[:, 40:48], in1=LP[:, 32:40], op=ALU.add), reads=["LP"], writes=["LP"])
    S.op(DVE, lambda e: e.tensor_scalar(out=LP[:, 0:8], in0=LP[:, 40:48], scalar1=-4.0, scalar2=None, op0=ALU.mult), reads=["LP"], writes=["LP"])
    S.op(DVE, lambda e: e.tensor_scalar(out=LP[:, 8:16], in0=LP[:, 40:48], scalar1=-8.0, scalar2=None, op0=ALU.mult), reads=["LP"], writes=["LP"])
    S.op(DVE, lambda e: e.tensor_scalar(out=LP[:, 16:24], in0=P(O_BR, 8), scalar1=0.5, scalar2=None, op0=ALU.mult), reads=["PRM", "LP"], writes=["LP"])
    S.op(DVE, lambda e: e.tensor_scalar(out=LP[:, 24:32], in0=P(O_BI, 8), scalar1=0.5, scalar2=None, op0=ALU.mult), reads=["PRM", "LP"], writes=["LP"])
    for h in range(8):
        S.op(DVE, lambda e, h=h: e.tensor_tensor(out=WSB[:, h, :], in0=TA[1][:, h * 128:(h + 1) * 128], in1=TRI, op=ALU.mult),
             reads=["TA1", "T0"], writes=["WSB"])
    for h in range(8):
        S.op(PE, lambda e, h=h: e.matmul(PSB[:, h * 128:(h + 1) * 128], lhsT=ONES[:], rhs=WSB[:, h, :], start=True, stop=True),
             reads=["ONES", "WSB"], writes=["PB0", "PB1"], inc=(h == 7))
    for h in range(8):
        S.op(DVE, lambda e, h=h: e.scalar_tensor_tensor(out=R[:, h * 128:(h + 1) * 128], in0=PSB[:, h * 128:(h + 1) * 128],
                                                        scalar=P(O_LNB + h), in1=TA[0][:, h * 128:(h + 1) * 128],
                                                        op0=ALU.mult, op1=ALU.add),
             reads=["PB0", "PB1", "PRM", "TA0"], writes=RK)

    S.op(ACT, lambda e: e.activation(out=CB[:], in_=CT[:], func=AF.Silu, bias=ZEROC, scale=1.0), reads=["CT", "CST"], writes=["CB"])
    for g in range(3):
        s = w_next()
        for cc in range(4):
            c0 = (g * 4 + cc) * 4
            for k in range(16):
                S.op(PE, lambda e, s=s, cc=cc, k=k, c0=c0: e.matmul(
                    PSA[:, c0:c0 + 4], lhsT=W[:, s, k, cc * 128:(cc + 1) * 128], rhs=CB[:, k * 4:(k + 1) * 4],
                    start=(k == 0), stop=(k == 15)),
                    reads=["W%d" % s, "CB"], writes=["PA0"], inc=(k == 15))
    S.op(DVE, lambda e: e.tensor_copy(out=MODP[:], in_=PSA[:, 0:48]), reads=["PA0"], writes=["MODP"])
    S.dma(SP, lambda e: e.dma_start(out=ag0_in, in_=MODP[:]), dbo, reads=["MODP"], writes=["ag0_in"])
    S.dma(POOL, lambda e: e.collective_compute("AllGather", ALU.bypass, replica_groups=[list(range(8))],
                                               ins=[ag0_in], outs=[ag0_out]), dcc[0], reads=["ag0_in"], writes=["ag0_out"])
    S.dma(SP, lambda e: e.dma_start(out=MODG[:].rearrange("p (r c) b -> p r (c b)", r=8),
                                    in_=ag0_out.rearrange("(r p) c -> p r c", p=128)),
          dbi, reads=["ag0_out"], writes=["MODG"])
    def rms_square(k, buf, bkey):
        S.op(ACT, lambda e, k=k, buf=buf: e.activation(out=buf, in_=XA[:, k, HAL:TW], func=AF.Square, bias=ZEROC, scale=1.0),
             reads=[xa_keys[k], "CST"], writes=[bkey])

    def rms_mm(k, buf, bkey):
        for tb in range(2):
            S.op(PE, lambda e, k=k, buf=buf, tb=tb: e.matmul(PSS[:, tb * 512:(tb + 1) * 512], lhsT=ONES[:], rhs=buf[:, tb * 512:(tb + 1) * 512],
                                                           start=(k == 0), stop=(k == 15)),
                 reads=["ONES", bkey], writes=["S%d" % tb], inc=(tb == 1))

    def rms_finish(sum_div):
        S.op(ACT, lambda e: e.activation(out=RSTDM[:], in_=PSS[:, 0:T], func=AF.Sqrt, bias=EPSC, scale=1.0 / sum_div),
             reads=["S0", "S1", "CST"], writes=["T1"])
        S.op(DVE, lambda e: e.reciprocal(out=RSTDM[:], in_=RSTDM[:]), reads=["T1"], writes=["T1"])

    def sqbuf2(k):
        return SQ[k % 2][:, 0:T], "SQ%d" % (k % 2)

    S.op(ACT, lambda e: e.activation(out=SQH[:], in_=XA[:, :, 0:HAL], func=AF.Square, bias=ZEROC, scale=1.0), reads=xa_keys + ["CST"], writes=["SQH"])
    for k in range(16):
        S.op(PE, lambda e, k=k: e.matmul(PSC[:, 0:HAL], lhsT=ONES[:], rhs=SQH[:, k, :], start=(k == 0), stop=(k == 15)),
             reads=["ONES", "SQH"], writes=["PC0"], inc=(k == 15))
    S.op(ACT, lambda e: e.activation(out=RSTDH[:, 0:HAL], in_=PSC[:, 0:HAL], func=AF.Sqrt, bias=EPSC, scale=1.0 / D), reads=["PC0", "CST"], writes=["RSTDH"])
    S.op(DVE, lambda e: e.reciprocal(out=RSTDH[:, 0:HAL], in_=RSTDH[:, 0:HAL]), reads=["RSTDH"], writes=["RSTDH"])
    for k in range(16):
        rms_square(k, MIX[:, k % 8, :], mx_keys[k % 8])
        rms_mm(k, MIX[:, k % 8, :], mx_keys[k % 8])
    rms_finish(D)
    S.op(DVE, lambda e: e.tensor_scalar(out=MOD[:], in0=MODG[:, :, 0], scalar1=P(O_SELB), scalar2=None, op0=ALU.mult),
         reads=["MODG", "PRM"], writes=["MOD"])
    for b in range(1, 4):
        S.op(DVE, lambda e, b=b: e.scalar_tensor_tensor(out=MOD[:], in0=MODG[:, :, b], scalar=P(O_SELB + b), in1=MOD[:],
                                                        op0=ALU.mult, op1=ALU.add), reads=["MODG", "MOD", "PRM"], writes=["MOD"])
    S.op(DVE, lambda e: e.tensor_tensor(out=MOD[:], in0=MOD[:], in1=P(O_BADA, 96), op=ALU.add), reads=["MOD", "PRM"], writes=["MOD"])
    S.op(DVE, lambda e: e.scalar_tensor_tensor(out=A1[:], in0=MOD[:, 16:32], scalar=1.0, in1=P(O_N1, 16), op0=ALU.add, op1=ALU.mult),
         reads=["MOD", "PRM"], writes=["A1"])
    S.op(DVE, lambda e: e.scalar_tensor_tensor(out=A2[:], in0=MOD[:, 64:80], scalar=1.0, in1=P(O_N2, 16), op0=ALU.add, op1=ALU.mult),
         reads=["MOD", "PRM"], writes=["A2"])
    n1bufs = [(TA[0][:, HAL:TW], ["TA0"]), (TA[1][:, HAL:TW], ["TA1"]), (TT[0][:], TK[0]), (TT[2][:], TK[2]), (TT[3][:], TK[3])]
    for k in range(16):
        tb_, tk_ = n1bufs[k % 5]
        S.op(DVE, lambda e, k=k, tb_=tb_: e.scalar_tensor_tensor(out=tb_, in0=XA[:, k, HAL:TW], scalar=A1[:, k:k + 1], in1=RSTDM[:],
                                                                 op0=ALU.mult, op1=ALU.mult),
             reads=[xa_keys[k], "A1", "T1"], writes=tk_)
        S.op(ACT, lambda e, k=k, tb_=tb_: e.activation(out=HT[:, k, HAL:TW], in_=tb_, func=AF.Identity, bias=MOD[:, k:k + 1], scale=1.0),
             reads=tk_ + ["MOD"], writes=[ht_keys[k]])
    for k in range(16):
        S.op(DVE, lambda e, k=k: e.scalar_tensor_tensor(out=HTMP[:, k, :], in0=XA[:, k, 0:HAL], scalar=A1[:, k:k + 1], in1=RSTDH[:, 0:HAL],
                                                        op0=ALU.mult, op1=ALU.mult),
             reads=[xa_keys[k], "A1", "RSTDH"], writes=["HTMP"])
    for k in range(16):
        S.op(DVE, lambda e, k=k: e.tensor_scalar(out=HT[:, k, 0:HAL], in0=HTMP[:, k, :], scalar1=MOD[:, k:k + 1], scalar2=P(O_FLAG),
                                                 op0=ALU.add, op1=ALU.mult),
             reads=["HTMP", "MOD", "PRM"], writes=["HTh"])
    def inproj_fm(s, cc, halo_col=None, pair=None):
        ps, pk = pair if pair is not None else next_pair()
        for k in range(16):
            for tb in range(2):
                S.op(PE, lambda e, s=s, cc=cc, k=k, tb=tb, ps=ps: e.matmul(
                    ps[:, tb * 512:(tb + 1) * 512], lhsT=W[:, s, k, cc * 128:(cc + 1) * 128],
                    rhs=HT[:, k, HAL + tb * 512:HAL + (tb + 1) * 512], start=(k == 0), stop=(k == 15)),
                    reads=["W%d" % s, ht_keys[k]], writes=[pk[tb]], inc=(k == 15 and tb == 1 and halo_col is None))
            if halo_col is not None:
                S.op(PE, lambda e, s=s, cc=cc, k=k: e.matmul(
                    PSS[:, halo_col:halo_col + HAL], lhsT=W[:, s, k, cc * 128:(cc + 1) * 128], rhs=HT[:, k, 0:HAL],
                    start=(k == 0), stop=(k == 15)),
                    reads=["W%d" % s, "HTh"], writes=["S0"], inc=(k == 15))
        return ps, pk

    MAL = [MIX[:, 2 * i:2 * i + 2, :].rearrange("p a b -> p (a b)").bitcast(F32) for i in range(4)]
    MALK = [["MX%d" % (2 * i), "MX%d" % (2 * i + 1)] for i in range(4)]
    LSET = [
        dict(xbs=TA[0], xk=["TA0"], acc=TT[0], ak=TK[0], thr=TT[1], rk=TK[1], thi=TT[2], ik=TK[2], a2=TT[3], a2k=TK[3], xcb=SQ[0], ck=["SQ0"]),
        dict(xbs=TA[1], xk=["TA1"], acc=MAL[0], ak=MALK[0], thr=MAL[1], rk=MALK[1], thi=MAL[2], ik=MALK[2], a2=MAL[3], a2k=MALK[3], xcb=SQ[1], ck=["SQ1"]),
    ]

    def lru_conv_ab(j, ps, pk, L):
        xbs, acc = L["xbs"], L["acc"]
        S.op(DVE, lambda e: e.tensor_copy(out=xbs[:, HAL:TW], in_=ps[:, 0:T]), reads=pk, writes=L["xk"])
        S.op(DVE, lambda e: e.tensor_copy(out=xbs[:, 0:HAL], in_=PSS[:, j * HAL:(j + 1) * HAL]), reads=["S0"], writes=L["xk"])
        S.op(ACT, lambda e: e.activation(out=acc[:], in_=xbs[:, HAL:TW], func=AF.Identity, bias=P(O_CB + j), scale=P(O_CW + j * 4 + 3)),
             reads=L["xk"] + ["PRM"], writes=L["ak"])
        for sft in (1, 2, 3):
            S.op(DVE, lambda e, sft=sft: e.scalar_tensor_tensor(out=acc[:], in0=xbs[:, HAL - sft:TW - sft], scalar=P(O_CW + j * 4 + 3 - sft),
                                                                in1=acc[:], op0=ALU.mult, op1=ALU.add),
                 reads=L["xk"] + L["ak"] + ["PRM"], writes=L["ak"])

    def lru_conv_c(j, L):
        acc, xcb = L["acc"], L["xcb"]
        S.op(ACT, lambda e: e.activation(out=xcb[:, 0:T], in_=acc[:], func=AF.Identity, bias=ZEROC, scale=1.0), reads=L["ak"] + ["CST"], writes=L["ck"])

    def lru_gates(j, L):
        xcb = L["xcb"]
        pr, pkr = next_pair()
        pi, pki = next_pair()
        for (pp, kk, wb, wk) in ((pr, pkr, WRB, "MX14"), (pi, pki, WIB, "MX15")):
            for tb in range(2):
                S.op(PE, lambda e, pp=pp, wb=wb, tb=tb: e.matmul(pp[:, tb * 512:(tb + 1) * 512], lhsT=wb[:, j, :],
                                                                  rhs=xcb[:, tb * 512:(tb + 1) * 512], start=True, stop=True),
                     reads=[wk] + L["ck"], writes=[kk[tb]], inc=(tb == 1))
        return pr, pkr, pi, pki

    def lru_rest(j, L, pr, pkr, pi, pki):
        acc, thr, thi, a2 = L["acc"], L["thr"], L["thi"], L["a2"]
        S.op(ACT, lambda e: e.activation(out=thr[:], in_=pr[:, 0:T], func=AF.Tanh, bias=LP[:, 16 + j:17 + j], scale=0.5), reads=pkr + ["LP"], writes=L["rk"])
        S.op(ACT, lambda e: e.activation(out=thi[:], in_=pi[:, 0:T], func=AF.Tanh, bias=LP[:, 24 + j:25 + j], scale=0.5), reads=pki + ["LP"], writes=L["ik"])
        a_ap = XA[:, j, HAL:TW]
        b_ap = XA[:, 8 + j, HAL:TW]
        S.op(ACT, lambda e: e.activation(out=a_ap, in_=thr[:], func=AF.Exp, bias=LP[:, j:j + 1], scale=LP[:, j:j + 1]), reads=L["rk"] + ["LP"], writes=[xa_keys[j]])
        S.op(ACT, lambda e: e.activation(out=a2[:], in_=thr[:], func=AF.Exp, bias=LP[:, 8 + j:9 + j], scale=LP[:, 8 + j:9 + j]), reads=L["rk"] + ["LP"], writes=L["a2k"])
        S.op(ACT, lambda e: e.activation(out=a2[:], in_=a2[:], func=AF.Sqrt, bias=QRTC, scale=-0.25), reads=L["a2k"] + ["CST"], writes=L["a2k"])
        S.op(DVE, lambda e: e.scalar_tensor_tensor(out=thi[:], in0=thi[:], scalar=1.0, in1=acc[:], op0=ALU.add, op1=ALU.mult),
             reads=L["ik"] + L["ak"], writes=L["ik"])
        S.op(DVE, lambda e: e.tensor_tensor(out=b_ap, in0=thi[:], in1=a2[:], op=ALU.mult), reads=L["ik"] + L["a2k"], writes=[xa_keys[8 + j]])
        S.op(DVE, lambda e: e.tensor_tensor_scan(out=thr[:], data0=a_ap, data1=b_ap, initial=0.0, op0=ALU.mult, op1=ALU.add),
             reads=[xa_keys[j], xa_keys[8 + j]] + L["rk"], writes=L["rk"])
        S.op(DVE, lambda e: e.tensor_copy(out=HEND[:, j:j + 1], in_=thr[:, T - 1:T]), reads=L["rk"], writes=["HEND"])

    VGB = [
        [(TA[0][:, 0:512], ["TA0"]), (TA[0][:, 512:1024], ["TA0"]), (TA[1][:, 0:512], ["TA1"]), (TA[1][:, 512:1024], ["TA1"])],
        [(MAL[2][:, 0:512], MALK[2]), (MAL[2][:, 512:1024], MALK[2]), (MAL[3][:, 0:512], MALK[3]), (MAL[3][:, 512:1024], MALK[3])],
    ]
    VBANKS = [(PSA, 0, "PA0"), (PSA, 1, "PA1"), (PSB, 0, "PB0"), (PSB, 1, "PB1")]

    def v_acc(s, batch):
        for ci in range(4):
            c = batch * 4 + ci
            pp, half, key = VBANKS[ci]
            for k in range(16):
                S.op(PE, lambda e, pp=pp, half=half, k=k, c=c: e.matmul(
                    pp[:, half * 512:(half + 1) * 512], lhsT=HT[:, k, HAL + c * 128:HAL + (c + 1) * 128], rhs=W[:, s, k, :],
                    start=(k == 0), stop=(k == 15)),
                    reads=["W%d" % s, ht_keys[k]], writes=[key], inc=(k == 15))

    def v_stats(batch):
        STATS, MV, RS4 = STATS2[batch], MV2[batch], RS42[batch]
        sk, mk, rk = "STATS%d" % batch, "MV%d" % batch, "RS4%d" % batch
        for ci in range(4):
            pp, half, key = VBANKS[ci]
            vg, tk = VGB[batch][ci]
            S.op(ACT, lambda e, pp=pp, half=half, vg=vg: e.activation(out=vg, in_=pp[:, half * 512:(half + 1) * 512], func=AF.Gelu_apprx_tanh,
                                                                        bias=ZEROC, scale=1.0), reads=[key, "CST"], writes=tk)
            for hh in range(4):
                S.op(DVE, lambda e, vg=vg, ci=ci, hh=hh: e.bn_stats(out=STATS[:, ci * 4 + hh, :], in_=vg[:, hh * 128:(hh + 1) * 128]),
                     reads=tk, writes=[sk])
                S.op(DVE, lambda e, ci=ci, hh=hh: e.bn_aggr(out=MV[:, ci * 4 + hh, :], in_=STATS[:, ci * 4 + hh, :]),
                     reads=[sk], writes=[mk])
        S.op(ACT, lambda e: e.activation(out=RS4[:], in_=MV[:, :, 1], func=AF.Sqrt, bias=EPSC, scale=1.0), reads=[mk, "CST"], writes=[rk])
        S.op(DVE, lambda e: e.reciprocal(out=RS4[:], in_=RS4[:]), reads=[rk], writes=[rk])

    def v_spatial(hb, batch):
        MV, RS4 = MV2[batch], RS42[batch]
        mk, rk = "MV%d" % batch, "RS4%d" % batch
        for ci in range(4):
            c = batch * 4 + ci
            vg, tk = VGB[batch][ci]
            vh = VH[ci % 2]
            vk = "VH%d" % (ci % 2)
            pc_o = (ci % 2) * 512
            pck = "PC%d" % (ci % 2)
            for hh in range(4):
                S.op(DVE, lambda e, vg=vg, vh=vh, ci=ci, hh=hh: e.tensor_scalar(
                    out=vh[:, hh * 128:(hh + 1) * 128], in0=vg[:, hh * 128:(hh + 1) * 128],
                    scalar1=MV[:, ci * 4 + hh, 0:1], scalar2=RS4[:, ci * 4 + hh:ci * 4 + hh + 1], op0=ALU.subtract, op1=ALU.mult),
                    reads=tk + [mk, rk], writes=[vk])
            for hh in range(4):
                S.op(PE, lambda e, vh=vh, hh=hh, pc_o=pc_o: e.matmul(PSC[:, pc_o + hh * 128:pc_o + (hh + 1) * 128], lhsT=vh[:, hh * 128:(hh + 1) * 128],
                                                                       rhs=WSB[:, hb + hh, :], start=True, stop=True),
                     reads=[vk, "WSB"], writes=[pck], inc=(hh == 3))
            for hh in range(4):
                S.op(DVE, lambda e, hh=hh, c=c, pc_o=pc_o: e.scalar_tensor_tensor(
                    out=TT[hh][:, c * 128:(c + 1) * 128], in0=PSC[:, pc_o + hh * 128:pc_o + (hh + 1) * 128], scalar=P(O_LNG + hb + hh),
                    in1=R[:, (hb + hh) * 128:(hb + hh + 1) * 128], op0=ALU.mult, op1=ALU.add),
                    reads=[pck, "PRM"] + RK, writes=TK[hh])

    def v_group(s, hb):
        v_acc(s, 0)
        v_stats(0)
        v_acc(s, 1)
        v_stats(1)
        v_spatial(hb, 0)
        v_spatial(hb, 1)

    sumsq_state = {"n": 0}

    pending_mm = []

    def flush_pending():
        while pending_mm:
            pending_mm.pop(0)()

    def branch_tail(y_ap, ykey, idx, scale_col, mixk):
        sq = SQ[idx % 2]
        sk = "SQ%d" % (idx % 2)
        S.op(ACT, lambda e: e.activation(out=sq[:, 0:T], in_=y_ap, func=AF.Square, bias=ZEROC, scale=1.0), reads=[ykey, "CST"], writes=[sk])

        def mm():
            n = sumsq_state["n"]
            for tb in range(2):
                S.op(PE, lambda e, tb=tb, n=n: e.matmul(PSS[:, tb * 512:(tb + 1) * 512], lhsT=ONES[:], rhs=sq[:, tb * 512:(tb + 1) * 512],
                                                        start=(n == 0), stop=(n == 7)),
                     reads=["ONES", sk], writes=["S%d" % tb], inc=(tb == 1))
            sumsq_state["n"] = (n + 1) % 8
        pending_mm.append(mm)
        S.op(ACT, lambda e: e.activation(out=MIX[:, mixk, :], in_=y_ap, func=AF.Identity, bias=ZEROC, scale=P(scale_col)),
             reads=[ykey, "PRM", "CST"], writes=[mx_keys[mixk]])

    def branch_finish(k0):
        flush_pending()
        S.op(ACT, lambda e: e.activation(out=RSTDM[:], in_=PSS[:, 0:T], func=AF.Sqrt, bias=EPSC, scale=1.0 / 1024.0),
             reads=["S0", "S1", "CST"], writes=["T1"])
        S.op(DVE, lambda e: e.reciprocal(out=RSTDM[:], in_=RSTDM[:]), reads=["T1"], writes=["T1"])
        for k in range(k0, k0 + 8):
            S.op(DVE, lambda e, k=k: e.tensor_tensor(out=MIX[:, k, :], in0=MIX[:, k, :], in1=RSTDM[:], op=ALU.mult),
                 reads=[mx_keys[k], "T1"], writes=[mx_keys[k]])

    def u_evac(hb, cc, ps, pk):
        h = hb + cc
        gu = TA[cc % 2]
        gk = "TA%d" % (cc % 2)
        S.op(ACT, lambda e, ps=ps, gu=gu: e.activation(out=gu[:, 0:T], in_=ps[:, 0:T], func=AF.Gelu_apprx_tanh, bias=ZEROC, scale=1.0),
             reads=pk + ["CST"], writes=[gk])
        S.op(DVE, lambda e, gu=gu, cc=cc: e.tensor_tensor(out=gu[:, 0:T], in0=gu[:, 0:T], in1=TT[cc][:], op=ALU.mult),
             reads=[gk] + TK[cc], writes=[gk])
        branch_tail(gu[:, 0:T], gk, cc, O_GA + h, h)

    def vu_group(hb):
        sv = w_next()
        v_acc(sv, 0)
        v_stats(0)
        v_acc(sv, 1)
        v_stats(1)
        su = w_next()
        pre = [inproj_fm(su, 0, pair=pairs[0]), inproj_fm(su, 1, pair=pairs[1])]
        flush_pending()
        v_spatial(hb, 0)
        v_spatial(hb, 1)
        u_evac(hb, 0, *pre[0])
        u_evac(hb, 1, *pre[1])
        for cc in (2, 3):
            ps, pk = inproj_fm(su, cc, pair=pairs[cc % 2])
            flush_pending()
            u_evac(hb, cc, ps, pk)

    def gb_group(s, jb):
        for cc in range(4):
            j = jb + cc
            ps, pk = inproj_fm(s, cc)
            flush_pending()
            hbuf = TT[cc % 2]
            hk = "T%d" % (cc % 2)
            S.op(DVE, lambda e, hbuf=hbuf, j=j: e.tensor_tensor_scan(out=hbuf[:], data0=XA[:, j, HAL:TW], data1=XA[:, 8 + j, HAL:TW],
                                                                      initial=INIT[:, j:j + 1], op0=ALU.mult, op1=ALU.add),
                 reads=[xa_keys[j], xa_keys[8 + j], "INIT"], writes=[hk])
            gg = TA[cc % 2]
            gk = "TA%d" % (cc % 2)
            S.op(ACT, lambda e, ps=ps, gg=gg: e.activation(out=gg[:, 0:T], in_=ps[:, 0:T], func=AF.Gelu_apprx_tanh, bias=ZEROC, scale=1.0),
                 reads=pk + ["CST"], writes=[gk])
            S.op(DVE, lambda e, gg=gg, hbuf=hbuf: e.tensor_tensor(out=gg[:, 0:T], in0=gg[:, 0:T], in1=hbuf[:], op=ALU.mult),
                 reads=[gk, hk], writes=[gk])
            branch_tail(gg[:, 0:T], gk, cc, O_GL + j, 8 + j)

    lru_slots = [w_next()]
    ip = inproj_fm(lru_slots[0], 0, halo_col=0)
    lru_conv_ab(0, ip[0], ip[1], LSET[0])
    lru_conv_c(0, LSET[0])
    for j in range(8):
        L = LSET[j % 2]
        ip_next = None
        if j < 7:
            if j == 3:
                lru_slots.append(w_next())
            ip_next = inproj_fm(lru_slots[(j + 1) // 4], (j + 1) % 4, halo_col=(j + 1) * HAL)
        gts = lru_gates(j, L)
        if j < 7:
            lru_conv_ab(j + 1, ip_next[0], ip_next[1], LSET[(j + 1) % 2])
        lru_rest(j, L, *gts)
        if j < 7:
            lru_conv_c(j + 1, LSET[(j + 1) % 2])
        ip = ip_next
    vu_group(0)
    S.dma(SP, lambda e: e.dma_start(out=ag1_in, in_=HEND[:]), dbo, reads=["HEND"], writes=["ag1_in"])
    S.dma(POOL, lambda e: e.collective_compute("AllGather", ALU.bypass, replica_groups=[list(range(8))],
                                               ins=[ag1_in], outs=[ag1_out]), dcc[1], reads=["ag1_in"], writes=["ag1_out"])
    S.dma(SP, lambda e: e.dma_start(out=AGT1[:], in_=ag1_out.rearrange("(r p) c -> p r c", p=128)), dbi, reads=["ag1_out"], writes=["AGT1"])
    vu_group(4)
    branch_finish(0)
    S.op(DVE, lambda e: e.tensor_scalar(out=INIT[:], in0=AGT1[:, 0, :], scalar1=P(O_SEL), scalar2=None, op0=ALU.mult),
         reads=["AGT1", "PRM"], writes=["INIT"])
    for r in range(1, 8):
        S.op(DVE, lambda e, r=r: e.scalar_tensor_tensor(out=INIT[:], in0=AGT1[:, r, :], scalar=P(O_SEL + r), in1=INIT[:],
                                                        op0=ALU.mult, op1=ALU.add), reads=["AGT1", "INIT", "PRM"], writes=["INIT"])
    s = w_next(); gb_group(s, 0)
    s = w_next(); gb_group(s, 4)
    branch_finish(8)
    for q in range(4):
        S.dma(SP, lambda e, q=q: e.dma_start(out=XA[:, 4 * q:4 * q + 4, :],
                                             in_=xT[512 * q:512 * (q + 1), :].rearrange("(k p) t -> p k t", p=128)),
              dxr, writes=xa_keys[4 * q:4 * q + 4])
    tok_xr = (dxr, dxr.cnt)
    for k in xa_keys:
        S.last_w[k] = tok_xr

    def proj_down(s, mm, m, gate_col, nk):
        ps, pk = next_pair()
        for k in range(nk):
            for tb in range(2):
                S.op(PE, lambda e, k=k, tb=tb, ps=ps: e.matmul(ps[:, tb * 512:(tb + 1) * 512], lhsT=W[:, s, k, mm * 128:(mm + 1) * 128],
                                                               rhs=MIX[:, k, tb * 512:(tb + 1) * 512], start=(k == 0), stop=(k == nk - 1)),
                     reads=["W%d" % s, mx_keys[k]], writes=[pk[tb]], inc=(k == nk - 1 and tb == 1))
        S.op(DVE, lambda e, ps=ps: e.scalar_tensor_tensor(out=XA[:, m, HAL:TW], in0=ps[:, 0:T], scalar=MOD[:, gate_col + m:gate_col + m + 1],
                                                          in1=XA[:, m, HAL:TW], op0=ALU.mult, op1=ALU.add),
             reads=pk + ["MOD", xa_keys[m]], writes=[xa_keys[m]])

    for g in range(4):
        s = w_next()
        for mm in range(4):
            m = g * 4 + mm
            proj_down(s, mm, m, 32, 16)
            if m >= 2:
                rms_mm(m - 2, *sqbuf2(m - 2))
            rms_square(m, *sqbuf2(m))
    rms_mm(14, *sqbuf2(14))
    rms_mm(15, *sqbuf2(15))

    S.op(DVE, lambda e: e.tensor_copy(out=X1H[:].rearrange("p (k t) -> p k t", t=2), in_=XA[:, :, TW - 2:TW]), reads=xa_keys, writes=["X1H"])
    S.dma(SP, lambda e: e.dma_start(out=ag2_in, in_=X1H[:]), dbo, reads=["X1H"], writes=["ag2_in"])
    S.dma(POOL, lambda e: e.collective_compute("AllGather", ALU.bypass, replica_groups=[list(range(8))],
                                               ins=[ag2_in], outs=[ag2_out]), dcc[2], reads=["ag2_in"], writes=["ag2_out"])
    S.dma(SP, lambda e: e.dma_start(out=AGT2[:], in_=ag2_out.rearrange("(r p) c -> p r c", p=128)), dbi, reads=["ag2_out"], writes=["T0"])

    rms_finish(D)
    T5 = MIX[:, 14:16, :].rearrange("p a b -> p (a b)").bitcast(F32)
    n2bufs = [(TA[0][:, HAL:TW], ["TA0"]), (TA[1][:, HAL:TW], ["TA1"]), (TT[2][:], TK[2]), (TT[3][:], TK[3]), (R[:], RK), (T5[:], ["MX14", "MX15"])]
    for k in range(16):
        tb_, tk_ = n2bufs[k % 6]
        S.op(DVE, lambda e, k=k, tb_=tb_: e.scalar_tensor_tensor(out=tb_, in0=XA[:, k, HAL:TW], scalar=A2[:, k:k + 1], in1=RSTDM[:],
                                                                 op0=ALU.mult, op1=ALU.mult),
             reads=[xa_keys[k], "A2", "T1"], writes=tk_)
        S.op(ACT, lambda e, k=k, tb_=tb_: e.activation(out=HT[:, k, HAL:TW], in_=tb_, func=AF.Identity, bias=MOD[:, 48 + k:49 + k], scale=1.0),
             reads=tk_ + ["MOD"], writes=[ht_keys[k]])
    S.op(DVE, lambda e: e.tensor_scalar(out=XH[:], in0=AGT2[:, 0, :], scalar1=P(O_SEL), scalar2=None, op0=ALU.mult), reads=["T0", "PRM"], writes=["XH"])
    for r in range(1, 8):
        S.op(DVE, lambda e, r=r: e.scalar_tensor_tensor(out=XH[:], in0=AGT2[:, r, :], scalar=P(O_SEL + r), in1=XH[:], op0=ALU.mult, op1=ALU.add),
             reads=["T0", "XH", "PRM"], writes=["XH"])
    S.op(ACT, lambda e: e.activation(out=SQH2[:], in_=XH[:], func=AF.Square, bias=ZEROC, scale=1.0), reads=["XH", "CST"], writes=["SQH2"])
    for k in range(16):
        S.op(PE, lambda e, k=k: e.matmul(PSC[:, 0:2], lhsT=ONES[:], rhs=SQH2[:, k * 2:(k + 1) * 2], start=(k == 0), stop=(k == 15)),
             reads=["ONES", "SQH2"], writes=["PC0"], inc=(k == 15))
    S.op(ACT, lambda e: e.activation(out=RSTDH[:, 0:2], in_=PSC[:, 0:2], func=AF.Sqrt, bias=EPSC, scale=1.0 / D), reads=["PC0", "CST"], writes=["RSTDH"])
    S.op(DVE, lambda e: e.reciprocal(out=RSTDH[:, 0:2], in_=RSTDH[:, 0:2]), reads=["RSTDH"], writes=["RSTDH"])
    for k in range(16):
        S.op(DVE, lambda e, k=k: e.scalar_tensor_tensor(out=TMPH[:, k * 2:(k + 1) * 2], in0=XH[:, k * 2:(k + 1) * 2], scalar=A2[:, k:k + 1],
                                                        in1=RSTDH[:, 0:2], op0=ALU.mult, op1=ALU.mult),
             reads=["XH", "A2", "RSTDH"], writes=["TMPH"])
        S.op(DVE, lambda e, k=k: e.tensor_scalar(out=HT[:, k, 2:HAL], in0=TMPH[:, k * 2:(k + 1) * 2], scalar1=MOD[:, 48 + k:49 + k],
                                                 scalar2=P(O_FLAG), op0=ALU.add, op1=ALU.mult),
             reads=["TMPH", "MOD", "PRM"], writes=["HTh"])

    ffn_i = [0]
    for H in range(3):
        for q in range(4):
            sg = w_next()
            sv = w_next(2)
            for cc in range(4):
                j = H * 16 + q * 4 + cc
                jj = q * 4 + cc
                i = ffn_i[0]
                ffn_i[0] += 1
                pg, pkg = next_pair()
                defer_halo = (H == 0 and q == 0)

                def halo_mm(k, cc=cc, sg=sg, j=j):
                    S.op(PE, lambda e: e.matmul(
                        PSS[:, j * 2:j * 2 + 2], lhsT=W[:, sg, k, cc * 128:(cc + 1) * 128], rhs=HT[:, k, 2:HAL],
                        start=(k == 0), stop=(k == 15)),
                        reads=["W%d" % sg, "HTh"], writes=["S0"], inc=(k == 15))

                for k in range(16):
                    for tb in range(2):
                        S.op(PE, lambda e, k=k, tb=tb, pg=pg, cc=cc, sg=sg: e.matmul(
                            pg[:, tb * 512:(tb + 1) * 512], lhsT=W[:, sg, k, cc * 128:(cc + 1) * 128],
                            rhs=HT[:, k, HAL + tb * 512:HAL + (tb + 1) * 512], start=(k == 0), stop=(k == 15)),
                            reads=["W%d" % sg, ht_keys[k]], writes=[pkg[tb]], inc=(defer_halo and k == 15 and tb == 1))
                    if not defer_halo:
                        halo_mm(k)
                pv, pkv = next_pair()
                for k in range(16):
                    for tb in range(2):
                        S.op(PE, lambda e, k=k, tb=tb, pv=pv, cc=cc, sv=sv: e.matmul(
                            pv[:, tb * 512:(tb + 1) * 512], lhsT=W[:, sv, k, cc * 128:(cc + 1) * 128],
                            rhs=HT[:, k, HAL + tb * 512:HAL + (tb + 1) * 512], start=(k == 0), stop=(k == 15)),
                            reads=["W%d" % sv, ht_keys[k]], writes=[pkv[tb]], inc=(k == 15 and tb == 1))
                if defer_halo:
                    for k in range(16):
                        halo_mm(k)
                gs = TA[i % 2]
                gk = "TA%d" % (i % 2)
                acc = TT[i % 2]
                ak = "T%d" % (i % 2)
                S.op(ACT, lambda e, gs=gs, pg=pg: e.activation(out=gs[:, 2:2 + T], in_=pg[:, 0:T], func=AF.Identity, bias=ZEROC, scale=1.0),
                     reads=pkg + ["CST"], writes=[gk])
                S.op(ACT, lambda e, gs=gs, j=j: e.activation(out=gs[:, 0:2], in_=PSS[:, j * 2:j * 2 + 2], func=AF.Identity, bias=ZEROC, scale=1.0),
                     reads=["S0", "CST"], writes=[gk])
                S.op(ACT, lambda e, gs=gs, acc=acc, j=j: e.activation(out=acc[:], in_=gs[:, 2:2 + T], func=AF.Identity, bias=P(O_FCB + j),
                                                                      scale=P(O_FCW + j * 3 + 2)), reads=[gk, "PRM"], writes=[ak])
                for sft in (1, 2):
                    S.op(DVE, lambda e, gs=gs, acc=acc, j=j, sft=sft: e.scalar_tensor_tensor(
                        out=acc[:], in0=gs[:, 2 - sft:2 - sft + T], scalar=P(O_FCW + j * 3 + 2 - sft), in1=acc[:], op0=ALU.mult, op1=ALU.add),
                        reads=[gk, ak, "PRM"], writes=[ak])
                S.op(ACT, lambda e, acc=acc: e.activation(out=acc[:], in_=acc[:], func=AF.Gelu_apprx_tanh, bias=ZEROC, scale=1.0),
                     reads=[ak, "CST"], writes=[ak])
                S.op(DVE, lambda e, acc=acc, pv=pv, jj=jj: e.tensor_tensor(out=MIX[:, jj, :], in0=pv[:, 0:T], in1=acc[:], op=ALU.mult),
                     reads=[ak] + pkv, writes=[mx_keys[jj]])
        for gq in range(4):
            s = w_next()
            for mm in range(4):
                m = gq * 4 + mm
                proj_down(s, mm, m, 80, 16)
                if H == 2:
                    if m >= 2:
                        rms_mm(m - 2, *sqbuf2(m - 2))
                    rms_square(m, *sqbuf2(m))

    rms_mm(14, *sqbuf2(14))
    rms_mm(15, *sqbuf2(15))
    rms_finish(D)
    for k in range(16):
        S.op(DVE, lambda e, k=k: e.scalar_tensor_tensor(out=XA[:, k, HAL:TW], in0=XA[:, k, HAL:TW], scalar=P(O_NF + k), in1=RSTDM[:],
                                                        op0=ALU.mult, op1=ALU.mult),
             reads=[xa_keys[k], "PRM", "T1"], writes=[xa_keys[k]])
        S.dma(SP, lambda e, k=k: e.dma_start(out=yT[k * 128:(k + 1) * 128, :], in_=XA[:, k, HAL:TW]), dout, reads=[xa_keys[k]])
    S.wait(SP, (dout, dout.cnt))
    SP.prog.append(("o", lambda e: e.nop(), False))

    with S.alloc(nc):
        with nc.Block() as block:
            S.emit(block)
    es.close()
    return nc


def _pk(v, n):
    return np.ascontiguousarray(np.asarray(v, np.float32).reshape(n, 128).T)


def prep_inputs(inp):
    f = lambda a: np.asarray(a, np.float32)
    x = f(inp["x"]); c = f(inp["c"])
    w_ada = f(inp["w_ada"])[0]
    shared = {
        "w_in": np.ascontiguousarray(f(inp["w_in"])[0]),
        "w_out": np.ascontiguousarray(f(inp["w_out"])[0]),
        "w_up": np.ascontiguousarray(f(inp["w_up"])[0]),
        "w_down": np.ascontiguousarray(f(inp["w_down"])[0]),
    }
    cT = np.ascontiguousarray(c.T.reshape(16, 128, 4).transpose(1, 0, 2).reshape(128, 64))
    w_s = f(inp["gmlp_w_s"])[0]
    wsT = np.ascontiguousarray(w_s.transpose(2, 0, 1).reshape(128, 1024))
    tri = np.triu(np.ones((128, 128), np.float32))
    bs_bc = np.ascontiguousarray(np.broadcast_to(f(inp["gmlp_b_s"])[0].reshape(1, 1024), (128, 1024)))

    def bd(w):
        w = f(w)[0]
        o = np.zeros((128, 8, 128), np.float32)
        for j in range(8):
            o[0:64, j, 0:64] = w[2 * j]
            o[64:128, j, 64:128] = w[2 * j + 1]
        return o.reshape(128, 1024)

    wr_bd = bd(inp["lru_w_r"]); wi_bd = bd(inp["lru_w_i"])
    base = np.zeros((128, NPRM), np.float32)
    base[:, O_BADA:O_BADA + 96] = _pk(f(inp["b_ada"])[0], 96)
    base[:, O_N1:O_N1 + 16] = _pk(f(inp["norm1"])[0], 16)
    base[:, O_N2:O_N2 + 16] = _pk(f(inp["norm2"])[0], 16)
    base[:, O_NF:O_NF + 16] = _pk(f(inp["norm_final"]), 16)
    base[:, O_LNG:O_LNG + 8] = _pk(f(inp["gmlp_ln_g"])[0], 8)
    base[:, O_LNB:O_LNB + 8] = _pk(f(inp["gmlp_ln_b"])[0], 8)
    base[:, O_CW:O_CW + 32] = f(inp["lru_conv_w"])[0].reshape(4, 8, 128).transpose(2, 1, 0).reshape(128, 32)
    base[:, O_CB:O_CB + 8] = _pk(f(inp["lru_conv_b"])[0], 8)
    base[:, O_BR:O_BR + 8] = _pk(f(inp["lru_b_r"])[0], 8)
    base[:, O_BI:O_BI + 8] = _pk(f(inp["lru_b_i"])[0], 8)
    base[:, O_LAM:O_LAM + 8] = _pk(f(inp["lru_lambda"])[0], 8)
    base[:, O_GL:O_GL + 8] = _pk(f(inp["out_norm_lru"])[0], 8)
    base[:, O_GA:O_GA + 8] = _pk(f(inp["out_norm_gmlp"])[0], 8)
    base[:, O_FCW:O_FCW + 144] = f(inp["ffn_conv_w"])[0].reshape(3, 48, 128).transpose(2, 1, 0).reshape(128, 144)
    base[:, O_FCB:O_FCB + 48] = _pk(f(inp["ffn_conv_b"])[0], 48)
    maps = []
    for core in range(8):
        b, half = core // 2, core % 2
        t0 = half * T
        xt = np.zeros((D, TW), np.float32)
        xt[:, HAL:] = x[b, t0:t0 + T, :].T
        if half == 1:
            xt[:, 0:HAL] = x[b, t0 - HAL:t0, :].T
        p = base.copy()
        if half == 1:
            p[:, O_SEL + core - 1] = 1.0
            p[:, O_FLAG] = 1.0
        p[:, O_SELB + b] = 1.0
        m = dict(shared)
        m.update({
            "xT": xt, "cT": cT, "w_ada_r": np.ascontiguousarray(w_ada[:, core * 1536:(core + 1) * 1536]),
            "prm": p, "wsT": wsT, "tri": tri, "bs_bc": bs_bc, "wr_bd": wr_bd, "wi_bd": wi_bd,
        })
        maps.append(m)
    return maps


_NC = None


def kernel(**inputs):
    global _NC
    maps = prep_inputs(inputs)
    if _NC is None:
        _NC = build_program()
    res = run_bass_kernel_spmd(_NC, maps, core_ids=list(range(8)))
    out = np.empty((4, 2048, D), np.float32)
    for core in range(8):
        b, half = core // 2, core % 2
        out[b, half * T:(half + 1) * T, :] = np.asarray(res.results[core]["yT"], np.float32).T
    return out
```
